# Optimizing a Trainium2 kernel written in Bass

```python
import math
import jax, jax.numpy as jnp
from jax import lax
import numpy as np

D_MODEL = 1024
BATCH = 8
SEQ = 4096
DEPTH = 1

HEAD_DIM = 64
N_ATTN_HEADS = 8
N_GMLP_GROUPS = 8
D_ATTN = N_ATTN_HEADS * HEAD_DIM
D_GMLP = N_GMLP_GROUPS * HEAD_DIM
D_MIX = D_ATTN + D_GMLP
D_IN = 3 * D_ATTN + 2 * D_GMLP
DILATIONS = ((128, 1), (512, 4), (2048, 16))
ROPE_THETA = 500000.0
ROPE_DIM = HEAD_DIM // 4
CHUNK = 128
D_FF = 2816
CONV_WIDTH = 3
D_PLE = 256
LN_EPS = 1e-5
ALPHA = (2.0 * DEPTH) ** 0.25
BETA = (8.0 * DEPTH) ** -0.25
NEG_INF = -1e30

kernel_name = "hybrid_dilated_attn_gmlp_deepnorm_layer"


def _layernorm(x, g, b):
    xf = x.astype(jnp.float32)
    mu = jnp.mean(xf, axis=-1, keepdims=True)
    var = jnp.mean(jnp.square(xf - mu), axis=-1, keepdims=True)
    y = (xf - mu) * lax.rsqrt(var + LN_EPS)
    return (y * g.astype(jnp.float32) + b.astype(jnp.float32)).astype(x.dtype)


def _partial_rope(t, positions):
    inv = ROPE_THETA ** (-jnp.arange(0, ROPE_DIM, 2, dtype=jnp.float32) / ROPE_DIM)
    ang = positions.astype(jnp.float32)[..., None] * inv
    cos = jnp.cos(ang)[:, :, None, :]
    sin = jnp.sin(ang)[:, :, None, :]
    half = ROPE_DIM // 2
    x1 = t[..., :half]
    x2 = t[..., half:ROPE_DIM]
    rot = jnp.concatenate([x1 * cos - x2 * sin, x2 * cos + x1 * sin], axis=-1)
    return jnp.concatenate([rot, t[..., ROPE_DIM:]], axis=-1)


def _dilated_branch(q, k, v, window, dilation):
    b, s, h, dh = q.shape
    d = dilation
    w = window // d
    L = s // d
    nb = -(-L // w)
    Lp = nb * w

    def sub(t):
        t = t.reshape(b, L, d, h, dh)
        return jnp.pad(t, ((0, 0), (0, Lp - L), (0, 0), (0, 0), (0, 0)))

    def band(t):
        tp = jnp.pad(t, ((0, 0), (w, 0), (0, 0), (0, 0), (0, 0)))
        prev = tp[:, :Lp].reshape(b, nb, w, d, h, dh)
        cur = t.reshape(b, nb, w, d, h, dh)
        return jnp.concatenate([prev, cur], axis=2)

    qb = sub(q).reshape(b, nb, w, d, h, dh)
    kb = band(sub(k))
    vb = band(sub(v))
    scores = jnp.einsum('bnqrhd,bnkrhd->bnrhqk', qb, kb) * (1.0 / math.sqrt(dh))
    qi = jnp.arange(nb)[:, None, None] * w + jnp.arange(w)[None, :, None]
    kj = jnp.arange(nb)[:, None, None] * w - w + jnp.arange(2 * w)[None, None, :]
    dist = qi - kj
    mask = (dist >= 0) & (dist <= w) & (kj >= 0)
    scores = jnp.where(mask[None, :, None, None], scores, NEG_INF)
    m = jnp.max(scores, axis=-1, keepdims=True)
    e = jnp.exp(scores - m)
    den = jnp.sum(e, axis=-1, keepdims=True)
    out = jnp.einsum('bnrhqk,bnkrhd->bnqrhd', e / den, vb)
    lse = jnp.transpose((m + jnp.log(den))[..., 0], (0, 1, 4, 2, 3))
    out = out.reshape(b, Lp, d, h, dh)[:, :L].reshape(b, s, h, dh)
    lse = lse.reshape(b, Lp, d, h)[:, :L].reshape(b, s, h)
    return out, lse


def _dilated_attention(q, k, v):
    outs, lses = [], []
    for window, dilation in DILATIONS:
        o, l = _dilated_branch(q, k, v, window, dilation)
        outs.append(o)
        lses.append(l)
    wts = jax.nn.softmax(jnp.stack(lses, axis=0), axis=0)
    return jnp.einsum('cbsh,cbshd->bshd', wts, jnp.stack(outs, axis=0))


def _chunked_gmlp(u, z, ln_z_g, ln_z_b, w_s, b_s):
    b, s, _ = z.shape
    zn = _layernorm(z, ln_z_g, ln_z_b)
    zc = zn.reshape(b, s // CHUNK, CHUNK, N_GMLP_GROUPS, HEAD_DIM)
    causal = jnp.tril(jnp.ones((CHUNK, CHUNK), dtype=w_s.dtype))
    mixed = jnp.einsum('gij,bcjgd->bcigd', w_s * causal, zc) + jnp.transpose(b_s)[None, None, :, :, None]
    return u * mixed.reshape(b, s, D_GMLP)


def _causal_dwconv(a, w, bias):
    s = a.shape[1]
    ap = jnp.pad(a, ((0, 0), (CONV_WIDTH - 1, 0), (0, 0)))
    out = bias
    for kk in range(CONV_WIDTH):
        out = out + w[kk] * ap[:, kk:kk + s]
    return out


def setup_inputs(seed: int = 0) -> dict:
    key = jax.random.key(seed)
    ks = jax.random.split(key, 24)
    nrm = lambda k, shape, scale: jax.random.normal(k, shape, dtype=jnp.float32) * scale
    gain = lambda k, n: 1.0 + nrm(k, (DEPTH, n), 0.01)
    bias = lambda k, n: nrm(k, (DEPTH, n), 0.01)
    x = nrm(ks[0], (BATCH, SEQ, D_MODEL), 1.0)
    p = nrm(ks[1], (DEPTH, BATCH, SEQ, D_PLE), 1.0)
    offs = jax.random.randint(ks[2], (BATCH, 1), 0, 1024, dtype=jnp.int32)
    positions = offs + jnp.arange(SEQ, dtype=jnp.int32)[None, :]
    w_in = nrm(ks[3], (DEPTH, D_MODEL, D_IN), D_MODEL ** -0.5)
    col_scale = jnp.concatenate([jnp.ones((2 * D_ATTN,), jnp.float32),
                                 jnp.full((D_ATTN,), BETA, jnp.float32),
                                 jnp.ones((2 * D_GMLP,), jnp.float32)])
    w_in = w_in * col_scale
    return {
        "x": x,
        "p": p,
        "positions": positions,
        "w_in": w_in,
        "ln_z_g": gain(ks[4], D_GMLP),
        "ln_z_b": bias(ks[5], D_GMLP),
        "w_s": nrm(ks[6], (DEPTH, N_GMLP_GROUPS, CHUNK, CHUNK), CHUNK ** -0.5),
        "b_s": 1.0 + nrm(ks[7], (DEPTH, N_GMLP_GROUPS, CHUNK), 0.01),
        "w_o": nrm(ks[8], (DEPTH, D_MIX, D_MODEL), BETA * D_MIX ** -0.5),
        "ln1_g": gain(ks[9], D_MODEL),
        "ln1_b": bias(ks[10], D_MODEL),
        "w_ff_a": nrm(ks[11], (DEPTH, D_MODEL, D_FF), BETA * D_MODEL ** -0.5),
        "w_ff_b": nrm(ks[12], (DEPTH, D_MODEL, D_FF), BETA * D_MODEL ** -0.5),
        "conv_w": nrm(ks[13], (DEPTH, CONV_WIDTH, D_FF), CONV_WIDTH ** -0.5),
        "conv_b": bias(ks[14], D_FF),
        "w_ff_down": nrm(ks[15], (DEPTH, D_FF, D_MODEL), BETA * D_FF ** -0.5),
        "ln2_g": gain(ks[16], D_MODEL),
        "ln2_b": bias(ks[17], D_MODEL),
        "w_ple_gate": nrm(ks[18], (DEPTH, D_MODEL, D_MODEL), D_MODEL ** -0.5),
        "b_ple_gate": bias(ks[19], D_MODEL),
        "w_ple_in": nrm(ks[20], (DEPTH, D_PLE, D_MODEL), BETA * D_PLE ** -0.5),
        "ln3_g": gain(ks[21], D_MODEL),
        "ln3_b": bias(ks[22], D_MODEL),
    }


def reference(x, p, positions, w_in, ln_z_g, ln_z_b, w_s, b_s, w_o, ln1_g, ln1_b,
              w_ff_a, w_ff_b, conv_w, conv_b, w_ff_down, ln2_g, ln2_b,
              w_ple_gate, b_ple_gate, w_ple_in, ln3_g, ln3_b):
    b, s, _ = x.shape
    for i in range(DEPTH):
        h = x @ w_in[i]
        q = h[..., :D_ATTN].reshape(b, s, N_ATTN_HEADS, HEAD_DIM)
        k = h[..., D_ATTN:2 * D_ATTN].reshape(b, s, N_ATTN_HEADS, HEAD_DIM)
        v = h[..., 2 * D_ATTN:3 * D_ATTN].reshape(b, s, N_ATTN_HEADS, HEAD_DIM)
        u = jax.nn.gelu(h[..., 3 * D_ATTN:3 * D_ATTN + D_GMLP], approximate=False)
        z = jax.nn.gelu(h[..., 3 * D_ATTN + D_GMLP:], approximate=False)
        q = _partial_rope(q, positions).astype(jnp.float32)
        k = _partial_rope(k, positions).astype(jnp.float32)
        attn = _dilated_attention(q, k, v.astype(jnp.float32)).astype(x.dtype).reshape(b, s, D_ATTN)
        gm = _chunked_gmlp(u, z, ln_z_g[i], ln_z_b[i], w_s[i], b_s[i])
        mix = jnp.concatenate([attn, gm], axis=-1) @ w_o[i]
        x = _layernorm(ALPHA * x + mix, ln1_g[i], ln1_b[i])
        a = _causal_dwconv(x @ w_ff_a[i], conv_w[i], conv_b[i])
        ff = (jax.nn.gelu(a, approximate=False) * (x @ w_ff_b[i])) @ w_ff_down[i]
        x = _layernorm(ALPHA * x + ff, ln2_g[i], ln2_b[i])
        gate = jax.nn.sigmoid(x @ w_ple_gate[i] + b_ple_gate[i])
        ple = gate * (p[i] @ w_ple_in[i])
        x = _layernorm(ALPHA * x + ple, ln3_g[i], ln3_b[i])
    return x
```

```python
import numpy as np
import concourse.bass as bass
import concourse.mybir as mybir
from concourse.bass_utils import run_bass_kernel_spmd

F32 = mybir.dt.float32
BF16 = mybir.dt.bfloat16
I32 = mybir.dt.int32
AF = mybir.ActivationFunctionType
ALU = mybir.AluOpType

S = 4096
D = 1024
DFF = 2816
NF = 22
DPLE = 256
ALPHA = float(2.0 ** 0.25)
EPS = 1e-5
PI = float(np.pi)
TWO_PI = float(2 * np.pi)
KB = 1024


class Op:
    __slots__ = ("idx", "eng", "fn", "deps", "is_dma", "lane", "sig", "signals")

    def __init__(self, idx, eng, fn, is_dma, lane):
        self.idx = idx
        self.eng = eng
        self.fn = fn
        self.deps = set()
        self.is_dma = is_dma
        self.lane = lane
        self.sig = None
        self.signals = False


class Prog:
    ENGS = ("pe", "act", "dve", "pool", "sp")

    def __init__(self, nc):
        self.nc = nc
        self.ops = []
        self.state = {}
        self.final_waits = []
        self.last_eng = {}
        self.last_lane = {}
        self.pending_fence = {}

    def op(self, eng, fn, reads=(), writes=(), dma=False, lane=None):
        o = Op(len(self.ops), eng, fn, dma, lane)
        self.ops.append(o)
        pf = self.pending_fence.pop(eng, None)
        if pf is not None:
            o.deps.update(pf)
        for k in reads:
            st = self.state.get(k)
            if st is None:
                st = [None, []]
                self.state[k] = st
            if st[0] is not None:
                self._dep(o, st[0], "raw")
            if isinstance(k, tuple) and k[0] == "ps":
                for r in st[1]:
                    if self.ops[r].eng != eng:
                        o.deps.add(r)
            if not dma:
                st[1] = [r for r in st[1] if self.ops[r].is_dma or self.ops[r].eng != eng]
            st[1].append(o.idx)
        for k in writes:
            st = self.state.get(k)
            if st is None:
                st = [None, []]
                self.state[k] = st
            if st[0] is not None:
                self._dep(o, st[0], "waw")
            for r in st[1]:
                if r != o.idx:
                    self._dep(o, r, "war")
            st[0] = o.idx
            st[1] = []
        if dma:
            self.last_lane[tuple(lane)] = o.idx
        else:
            self.last_eng[eng] = o.idx
        return o

    def _dep(self, o, j, kind):
        t = self.ops[j]
        if t.eng == o.eng and not t.is_dma and not o.is_dma:
            if o.eng == "pe":
                return
        o.deps.add(j)

    def fence(self):
        deps = set(self.last_eng.values()) | set(self.last_lane.values())
        for e in self.ENGS:
            self.pending_fence[e] = set(deps)
        self.state = {}

    def emit(self):
        nc = self.nc
        ops = self.ops
        for o in ops:
            o.deps.discard(o.idx)
            if o.is_dma:
                o.signals = True
            for j in o.deps:
                ops[j].signals = True
        for j in self.final_waits:
            ops[j].signals = True
        sems = {}
        counters = {}
        for o in ops:
            if not o.signals:
                continue
            key = ("dma",) + tuple(o.lane) if o.is_dma else ("eng", o.eng)
            if key not in sems:
                sems[key] = nc.alloc_semaphore("s_" + "_".join(str(x) for x in key))
                counters[key] = 0
            counters[key] += 16 if o.is_dma else 1
            o.sig = (key, counters[key])
        streams = {e: [o for o in ops if o.eng == e] for e in self.ENGS}
        final = [ops[j].sig for j in self.final_waits]

        def run_stream(e, engh):
            waited = {}
            for o in streams[e]:
                need = {}
                for j in o.deps:
                    t = ops[j]
                    if t.eng == e and not t.is_dma and not o.is_dma and e == "pe":
                        continue
                    k, v = t.sig
                    if need.get(k, 0) < v:
                        need[k] = v
                for k, v in need.items():
                    if waited.get(k, 0) >= v:
                        continue
                    engh.wait_ge(sems[k], v)
                    waited[k] = v
                ins = o.fn(engh)
                if o.signals:
                    ins.then_inc(sems[o.sig[0]], 16 if o.is_dma else 1)
            if e == "sp":
                need = {}
                for k, v in final:
                    if need.get(k, 0) < v:
                        need[k] = v
                for k, v in need.items():
                    engh.wait_ge(sems[k], v)

        with nc.Block() as block:
            @block.tensor
            def _(pe):
                run_stream("pe", pe)

            @block.scalar
            def _(act):
                run_stream("act", act)

            @block.vector
            def _(dve):
                run_stream("dve", dve)

            @block.gpsimd
            def _(pool):
                run_stream("pool", pool)

            @block.sync
            def _(sp):
                run_stream("sp", sp)


def build(stop_after=None):
    nc = bass.Bass("TRN2", target_bir_lowering=False)
    P = Prog(nc)

    def din(name, shape, dt=F32):
        return nc.dram_tensor(name, list(shape), dt, kind="ExternalInput").ap()

    x_d = din("x", [S, D])
    p_d = din("p", [S, DPLE])
    pos_d = din("pos", [1, S], I32)
    w_in_d = din("w_in", [D, 2560])
    w_sw_d = din("w_sw", [D, 1024])
    lzg_d = din("ln_z_g", [1, 512])
    lzb_d = din("ln_z_b", [1, 512])
    wsT_d = din("w_sT", [8, 128, 128])
    bs_d = din("b_s", [8, 128])
    wo_d = din("w_o", [D, D])
    ln_d = {}
    for nm in ("ln1_g", "ln1_b", "ln2_g", "ln2_b", "ln3_g", "ln3_b", "b_ple_gate"):
        ln_d[nm] = din(nm, [1, D])
    wa_d = din("w_ff_a", [D, DFF])
    wb_d = din("w_ff_b", [D, DFF])
    cw_d = din("cw", [128, 3 * NF])
    cb_d = din("cb", [128, NF])
    wd_d = din("w_ff_down", [DFF, D])
    wg_d = din("w_ple_gate", [D, D])
    wp_d = din("w_ple_in", [DPLE, D])
    ident_d = din("ident", [128, 128])
    mL_d = din("maskL", [128, 128])
    mU_d = din("maskU", [128, 128])
    ropec_d = din("ropec", [128, 2])
    lnT_d = din("lnT", [128, 32])
    if stop_after == "A":
        out_d = nc.dram_tensor("out", [128, 8 * S], BF16, kind="ExternalOutput").ap()
    elif stop_after in ("B", "D", "G"):
        out_d = nc.dram_tensor("out", [128, 4 * D], F32, kind="ExternalOutput").ap()
    elif stop_after == "F":
        out_d = nc.dram_tensor("out", [128, NF * 512], BF16, kind="ExternalOutput").ap()
    else:
        out_d = nc.dram_tensor("out", [S, D], F32, kind="ExternalOutput").ap()

    tape = nc.dram_tensor("tape", [84, 128, D], BF16).ap()
    chunks84 = []
    for k in range(8):
        chunks84.append((wo_d[128 * k:128 * (k + 1), :], False))
    for f in range(NF):
        chunks84.append((wa_d[:, 128 * f:128 * (f + 1)], True))
        chunks84.append((wb_d[:, 128 * f:128 * (f + 1)], True))
    for f in range(NF):
        chunks84.append((wd_d[128 * f:128 * (f + 1), :], False))
    for k in range(8):
        chunks84.append((wg_d[128 * k:128 * (k + 1), :], False))
    for k in range(2):
        chunks84.append((wp_d[128 * k:128 * (k + 1), :], False))

    def sb(name, shape, dt, off):
        assert off % 32 == 0, (name, off)
        return nc.alloc_sbuf_tensor_at(name, list(shape), dt, offset=int(off))

    ps = nc.alloc_psum_tensor("ps", [128, 8, 512], F32)
    psb = ps.bitcast(BF16)

    CC = sb("CC", [128, 8, S], BF16, 16 * KB)
    TC = sb("TC", [128, S], F32, 48 * KB)
    TS = sb("TS", [128, S], F32, 64 * KB)
    c0 = 80 * KB
    identf = sb("identf", [128, 128], F32, c0)
    identb = sb("identb", [128, 128], BF16, c0 + 512)
    mL = sb("mL", [128, 128], BF16, c0 + 768)
    mU = sb("mU", [128, 128], BF16, c0 + 1024)
    ropec = sb("ropec", [128, 2], F32, c0 + 1280)
    lnst = sb("lnst", [128, 4, 2, 6], F32, c0 + 1312)
    lnmv = sb("lnmv", [128, 4, 4], F32, c0 + 1312 + 192)

    P.op("sp", lambda e: e.dma_start(out=identf[:], in_=ident_d), writes=["identf"], dma=True, lane=("c", 0))
    P.op("pool", lambda e: e.dma_start(out=identb[:], in_=ident_d), writes=["identb"], dma=True, lane=("c", 1))
    P.op("pool", lambda e: e.dma_start(out=mL[:], in_=mL_d), writes=["mL"], dma=True, lane=("c", 2))
    P.op("pool", lambda e: e.dma_start(out=mU[:], in_=mU_d), writes=["mU"], dma=True, lane=("c", 3))
    P.op("sp", lambda e: e.dma_start(out=ropec[:], in_=ropec_d), writes=["ropec"], dma=True, lane=("c", 4))

    e0 = 146 * KB
    HT = 1024
    pos_i = sb("pos_i", [128, HT], I32, e0)
    pos_f = sb("pos_f", [128, HT], F32, e0 + 4 * KB)
    ang = sb("ang", [128, HT], F32, e0 + 8 * KB)
    a2 = sb("a2", [128, HT], F32, e0 + 12 * KB)
    ki = sb("ki", [128, HT], I32, e0 + 16 * KB)
    rr = sb("rr", [128, HT], F32, e0 + 20 * KB)
    tt_ = sb("tt_", [128, HT], F32, e0 + 24 * KB)
    C1 = 6.28125
    C2 = TWO_PI - C1
    for h in range(4):
        cs = slice(h * HT, (h + 1) * HT)
        P.op("sp", lambda e, cs=cs: e.dma_start(out=pos_i[:], in_=pos_d[0, cs].partition_broadcast(128)),
             writes=["pos_i"], dma=True, lane=("c", 5))
        P.op("dve", lambda e: e.tensor_copy(out=pos_f[:], in_=pos_i[:]), reads=["pos_i"], writes=["pos_f"])
        P.op("dve", lambda e: e.tensor_scalar(out=ang[:], in0=pos_f[:], scalar1=ropec[:, 0:1], scalar2=None, op0=ALU.mult),
             reads=["pos_f", "ropec"], writes=["ang"])
        for tab, shift in ((TS, 0.0), (TC, PI / 2)):
            if shift != 0.0:
                P.op("dve", lambda e, shift=shift: e.tensor_scalar(out=a2[:], in0=ang[:], scalar1=shift, scalar2=None, op0=ALU.add),
                     reads=["ang"], writes=["a2"])
                src = a2
                srck = "a2"
            else:
                src = ang
                srck = "ang"
            P.op("dve", lambda e, src=src: e.tensor_scalar(out=ki[:], in0=src[:], scalar1=1.0 / TWO_PI, scalar2=None, op0=ALU.mult),
                 reads=[srck], writes=["ki"])
            P.op("dve", lambda e, src=src: e.scalar_tensor_tensor(out=rr[:], in0=ki[:], scalar=-C1, in1=src[:], op0=ALU.mult, op1=ALU.add),
                 reads=["ki", srck], writes=["rr"])
            P.op("dve", lambda e: e.scalar_tensor_tensor(out=rr[:], in0=ki[:], scalar=-C2, in1=rr[:], op0=ALU.mult, op1=ALU.add),
                 reads=["ki", "rr"], writes=["rr"])
            P.op("dve", lambda e: e.tensor_scalar(out=tt_[:], in0=rr[:], scalar1=PI, scalar2=TWO_PI, op0=ALU.is_gt, op1=ALU.mult),
                 reads=["rr"], writes=["tt_"])
            P.op("dve", lambda e: e.tensor_tensor(out=rr[:], in0=rr[:], in1=tt_[:], op=ALU.subtract), reads=["rr", "tt_"], writes=["rr"])
            P.op("dve", lambda e: e.tensor_scalar(out=tt_[:], in0=rr[:], scalar1=-PI, scalar2=TWO_PI, op0=ALU.is_lt, op1=ALU.mult),
                 reads=["rr"], writes=["tt_"])
            P.op("dve", lambda e: e.tensor_tensor(out=rr[:], in0=rr[:], in1=tt_[:], op=ALU.add), reads=["rr", "tt_"], writes=["rr"])
            P.op("dve", lambda e: e.tensor_scalar(out=rr[:], in0=rr[:], scalar1=PI, scalar2=-PI, op0=ALU.min, op1=ALU.max),
                 reads=["rr"], writes=["rr"])
            if shift == 0.0:
                P.op("act", lambda e, tab=tab, cs=cs: e.activation(out=tab[:, cs], in_=rr[:], func=AF.Sin, scale=ropec[:, 1:2]),
                     reads=["rr", "ropec"], writes=[("tab", "S", h)])
            else:
                P.op("act", lambda e, tab=tab, cs=cs: e.activation(out=tab[:, cs], in_=rr[:], func=AF.Sin),
                     reads=["rr"], writes=[("tab", "C", h)])

    P.fence()

    XT = sb("XT", [128, 8, S], BF16, 82 * KB)
    Qt = sb("Qt", [128, S], BF16, 146 * KB)
    Kt = sb("Kt", [128, S], BF16, 154 * KB)
    Vt = sb("Vt", [128, S], BF16, 162 * KB)
    wbuf = [sb("wbuf%d" % i, [128, 8, 128], BF16, 170 * KB + 2 * KB * i) for i in range(4)]
    t1 = [sb("t1_%d" % i, [128, 512], F32, 178 * KB + 4 * KB * i) for i in range(2)]
    t2 = [sb("t2_%d" % i, [128, 512], F32, 180 * KB + 4 * KB * i) for i in range(2)]
    xs = [sb("xs%d" % i, [128, D], BF16, 186 * KB + 2 * KB * i) for i in range(2)]
    NPT = 6
    PT = [sb("PT%d" % i, [128, 512], BF16, 190 * KB + KB * i) for i in range(NPT)]
    VB1 = sb("VB1", [128, 8, 3, 64], BF16, 196 * KB)
    VB4 = sb("VB4", [128, 8, 3, 64], BF16, 199 * KB)
    VB16 = sb("VB16", [128, 32, 3, 64], BF16, 202 * KB)
    rc = [sb("rc%d" % i, [128, 512], F32, 214 * KB + 2 * KB * i) for i in range(2)]

    for nm_, t in (("VB1", VB1), ("VB4", VB4), ("VB16", VB16)):
        P.op("pool", lambda e, t=t: e.memset(t[:, :, 1, :], 1.0), writes=[("vbones", nm_)])

    for i in range(32):
        j = i % 2
        P.op("pool", lambda e, i=i, j=j: e.dma_start(out=xs[j][:], in_=x_d[128 * i:128 * (i + 1), :]),
             writes=[("xs", j)], dma=True, lane=("xs", j))
        bk = 4 + j
        for k in range(8):
            P.op("pe", lambda e, j=j, k=k, bk=bk: e.transpose(out=psb[:, bk, k * 128:(k + 1) * 128],
                                                              in_=xs[j][:, k * 128:(k + 1) * 128], identity=identb[:]),
                 reads=[("xs", j), "identb"], writes=[("ps", bk)])
        eng = "act" if i % 2 == 0 else "dve"
        src = psb[:, bk, :].rearrange("p (k c) -> p k c", c=128)
        dst = XT[:, :, 128 * i:128 * (i + 1)]
        if eng == "act":
            P.op("act", lambda e, src=src, dst=dst: e.copy(out=dst, in_=src), reads=[("ps", bk)], writes=[("XT", i)])
        else:
            P.op("dve", lambda e, src=src, dst=dst: e.tensor_copy(out=dst, in_=src), reads=[("ps", bk)], writes=[("XT", i)])

    wcount = [0]

    def load_w(src_ap):
        s_ = wcount[0] % 4
        wcount[0] += 1
        P.op("pool", lambda e, s_=s_, src_ap=src_ap: e.dma_start(out=wbuf[s_][:], in_=src_ap.rearrange("(kc p) c -> p kc c", p=128)),
             writes=[("wb", s_)], dma=True, lane=("wb", s_))
        return s_

    def proj_mm(bank, slot, T8):
        for kc in range(8):
            P.op("pe", lambda e, bank=bank, slot=slot, kc=kc, T8=T8: e.matmul(
                ps[:, bank, :], lhsT=wbuf[slot][:, kc, :], rhs=XT[:, kc, 512 * T8:512 * (T8 + 1)],
                start=(kc == 0), stop=(kc == 7)),
                reads=[("wb", slot)] + [("XT", 4 * T8 + q) for q in range(4)], writes=[("ps", bank)])

    pt_ctr = [0]
    grp_ctr = [0]
    acc_ctr = [0]
    tr_ctr = [0]
    msk_ctr = [0]

    def build_vb(vbt, nm_, idx0, tok_ap_fns):
        cnt = len(tok_ap_fns)
        bk = tr_ctr[0] % 2
        tr_ctr[0] += 1
        for q, fn in enumerate(tok_ap_fns):
            P.op("pe", lambda e, q=q, fn=fn: e.transpose(out=psb[:, bk, q * 128:(q + 1) * 128], in_=fn(), identity=identb[:]),
                 reads=["Vall", "identb"], writes=[("ps", bk)])
        src = psb[:, bk, 0:cnt * 128].rearrange("p (q a b) -> p q a b", a=2, b=64)
        P.op("dve", lambda e: e.tensor_copy(out=vbt[:, idx0:idx0 + cnt, 0:3:2, :], in_=src), reads=[("ps", bk)],
             writes=[("vb", nm_, idx0 + q) for q in range(cnt)])

    for hp in range(4):
        for (dest, dkey, c_main, c_sw) in ((Qt, "Q", hp * 128, hp * 128), (Kt, "K", 512 + hp * 128, 512 + hp * 128)):
            sa = load_w(w_in_d[:, c_main:c_main + 128])
            sbw = load_w(w_sw_d[:, c_sw:c_sw + 128])
            for T8 in range(8):
                st_ = T8 % 2
                ba, bb = 2 * st_, 2 * st_ + 1
                proj_mm(ba, sa, T8)
                proj_mm(bb, sbw, T8)
                cs = slice(512 * T8, 512 * (T8 + 1))
                P.op("dve", lambda e, st_=st_, ba=ba, cs=cs: e.tensor_tensor(out=t1[st_][:], in0=ps[:, ba, :], in1=TC[:, cs], op=ALU.mult),
                     reads=[("ps", ba)] + [("tab", "C", h_) for h_ in range(4)], writes=[("t1", st_)])
                P.op("dve", lambda e, st_=st_, bb=bb, cs=cs: e.tensor_tensor(out=t2[st_][:], in0=ps[:, bb, :], in1=TS[:, cs], op=ALU.mult),
                     reads=[("ps", bb)] + [("tab", "S", h_) for h_ in range(4)], writes=[("t2", st_)])
                P.op("pool", lambda e, st_=st_, dest=dest, cs=cs: e.tensor_tensor(out=dest[:, cs], in0=t1[st_][:], in1=t2[st_][:], op=ALU.add),
                     reads=[("t1", st_), ("t2", st_)], writes=[(dkey, T8), dkey + "all"])
        sv = load_w(w_in_d[:, 1024 + hp * 128:1024 + (hp + 1) * 128])
        for T8 in range(8):
            bk = 2 * (T8 % 2)
            proj_mm(bk, sv, T8)
            cs = slice(512 * T8, 512 * (T8 + 1))
            P.op("act", lambda e, bk=bk, cs=cs: e.copy(out=Vt[:, cs], in_=ps[:, bk, :]), reads=[("ps", bk)], writes=["Vall"])

        for m in range(21 * hp, 21 * (hp + 1)):
            src_ap, view3 = chunks84[m]
            if view3:
                P.op("pool", lambda e, m=m, src_ap=src_ap: e.dma_start(out=tape[m].rearrange("p (kc c) -> p kc c", c=128),
                                                                   in_=src_ap.rearrange("(kc p) c -> p kc c", p=128)),
                     writes=[("tape", m)], dma=True, lane=("tape", m % 4))
            else:
                P.op("pool", lambda e, m=m, src_ap=src_ap: e.dma_start(out=tape[m], in_=src_ap),
                     writes=[("tape", m)], dma=True, lane=("tape", m % 4))

        for n16 in range(2):
            for r8 in range(2):
                build_vb(VB16, "VB16", n16 * 16 + r8 * 8,
                         [(lambda n16=n16, r16=r8 * 8 + q: Vt[:, 2048 * n16 + r16:2048 * (n16 + 1):16]) for q in range(8)])

        G = []
        for W in range(8):
            n16 = W // 4
            Wl = W % 4
            for hh in range(2):
                po = 64 * hh
                accb = 6 + acc_ctr[0] % 2
                acc_ctr[0] += 1
                ri = acc_ctr[0] % 2
                groups = []
                blk = []
                for b in range(4):
                    n = 4 * W + b
                    sl = slice(128 * n, 128 * (n + 1))
                    blk.append((sl, sl, slice(128 * b, 128 * (b + 1)), (VB1, 'VB1', n % 8), b))
                groups.append((blk, 128, "L"))
                blk = []
                for b in range(4):
                    n = 4 * W + b
                    if n == 0:
                        continue
                    blk.append((slice(128 * (n - 1), 128 * n), slice(128 * n, 128 * (n + 1)), slice(128 * b, 128 * (b + 1)),
                                (VB1, 'VB1', (n - 1) % 8), b))
                groups.append((blk, 128, "U"))
                blk = []
                for r4 in range(4):
                    sl = slice(512 * W + r4, 512 * (W + 1), 4)
                    blk.append((sl, sl, slice(r4, 512, 4), (VB4, 'VB4', (W % 2) * 4 + r4), r4))
                groups.append((blk, 128, "L"))
                if W > 0:
                    blk = []
                    for r4 in range(4):
                        blk.append((slice(512 * (W - 1) + r4, 512 * W, 4), slice(512 * W + r4, 512 * (W + 1), 4),
                                    slice(r4, 512, 4), (VB4, 'VB4', ((W - 1) % 2) * 4 + r4), r4))
                    groups.append((blk, 128, "U"))
                blk = []
                for r16 in range(16):
                    ksl = slice(2048 * n16 + r16, 2048 * (n16 + 1), 16)
                    qsl = slice(512 * W + r16, 512 * (W + 1), 16)
                    blk.append((ksl, qsl, slice(r16, 512, 16), (VB16, 'VB16', n16 * 16 + r16), r16))
                groups.append((blk, 32, "L16"))
                if n16 == 1:
                    blk = []
                    for r16 in range(16):
                        ksl = slice(r16, 2048, 16)
                        qsl = slice(512 * W + r16, 512 * (W + 1), 16)
                        blk.append((ksl, qsl, slice(r16, 512, 16), (VB16, 'VB16', r16), r16))
                    groups.append((blk, 32, "U16"))
                ngr = len(groups)
                for gi, (blk, N, mtype) in enumerate(groups):
                    G.append(dict(blk=blk, N=N, mtype=mtype, W=W, hh=hh, po=po, Wl=Wl, accb=accb, ri=ri,
                                  first=(gi == 0), last=(gi == ngr - 1), newwin=(gi == 0 and hh == 0)))

        def emit_qk(g):
            W = g["W"]
            if g["newwin"]:
                build_vb(VB1, "VB1", (4 * W) % 8, [(lambda n=4 * W + b: Vt[:, 128 * n:128 * (n + 1)]) for b in range(4)])
                build_vb(VB4, "VB4", (W % 2) * 4, [(lambda W=W, r4=r4: Vt[:, 512 * W + r4:512 * (W + 1):4]) for r4 in range(4)])
            blk, N, mtype, po, Wl = g["blk"], g["N"], g["mtype"], g["po"], g["Wl"]
            sbk = 4 + grp_ctr[0] % 2
            grp_ctr[0] += 1
            pt = pt_ctr[0] % NPT
            pt_ctr[0] += 1
            g["pt"] = pt
            cols_lo = blk[0][4] * N
            cols_hi = (blk[-1][4] + 1) * N
            for (ksl, qsl, acols, vbt, pos) in blk:
                P.op("pe", lambda e, sbk=sbk, pos=pos, N=N, ksl=ksl, qsl=qsl, po=po: e.matmul(
                    ps[:, sbk, pos * N:(pos + 1) * N], lhsT=Kt[po:po + 64, ksl], rhs=Qt[po:po + 64, qsl],
                    start=True, stop=True),
                    reads=["Kall", "Qall"], writes=[("ps", sbk)])
            P.op("act", lambda e, sbk=sbk, pt=pt, lo=cols_lo, hi=cols_hi: e.activation(
                out=PT[pt][:, lo:hi], in_=ps[:, sbk, lo:hi], func=AF.Exp, scale=0.125),
                reads=[("ps", sbk)], writes=[("PT", pt)])
            nb = (cols_hi - cols_lo) // N
            if mtype == "L":
                mfn = lambda nb=nb: mL[:, :].unsqueeze(1).broadcast_to([128, nb, 128])
            elif mtype == "U":
                mfn = lambda nb=nb: mU[:, :].unsqueeze(1).broadcast_to([128, nb, 128])
            elif mtype == "L16":
                mfn = lambda Wl=Wl: mL[:, 32 * Wl:32 * (Wl + 1)].unsqueeze(1).broadcast_to([128, 16, 32])
            else:
                mfn = lambda Wl=Wl: mU[:, 32 * Wl:32 * (Wl + 1)].unsqueeze(1).broadcast_to([128, 16, 32])
            meng = "pool" if msk_ctr[0] % 2 == 0 else "dve"
            msk_ctr[0] += 1
            P.op(meng, lambda e, pt=pt, lo=cols_lo, hi=cols_hi, N=N, mfn=mfn: e.tensor_tensor(
                out=PT[pt][:, lo:hi].rearrange("p (a b) -> p a b", b=N),
                in0=PT[pt][:, lo:hi].rearrange("p (a b) -> p a b", b=N), in1=mfn(), op=ALU.mult),
                reads=[("PT", pt), "mL", "mU"], writes=[("PT", pt)])

        def emit_pv(g, hp=hp):
            blk, N, po, accb, ri, hh, W, pt = g["blk"], g["N"], g["po"], g["accb"], g["ri"], g["hh"], g["W"], g["pt"]
            for bi, (ksl, qsl, acols, vbt, pos) in enumerate(blk):
                first = g["first"] and bi == 0
                last = g["last"] and (bi == len(blk) - 1)
                P.op("pe", lambda e, accb=accb, acols=acols, vbt=vbt, hh=hh, pt=pt, pos=pos, N=N, first=first, last=last: e.matmul(
                    ps[:, accb, acols], lhsT=vbt[0][:, vbt[2], hh:hh + 2, :].rearrange("p a b -> p (a b)"),
                    rhs=PT[pt][:, pos * N:(pos + 1) * N], start=first, stop=last, skip_group_check=True),
                    reads=[("PT", pt), ("vb", vbt[1], vbt[2]), ("vbones", vbt[1])], writes=[("ps", accb)])
            if g["last"]:
                dlo = 64 - po
                P.op("act", lambda e, ri=ri, accb=accb, dlo=dlo: e.activation(out=rc[ri][dlo:dlo + 64, :], in_=ps[dlo:dlo + 64, accb, :], func=AF.Ln),
                     reads=[("ps", accb)], writes=[("rc", ri)])
                P.op("act", lambda e, ri=ri, dlo=dlo: e.activation(out=rc[ri][dlo:dlo + 64, :], in_=rc[ri][dlo:dlo + 64, :], func=AF.Exp, scale=-1.0),
                     reads=[("rc", ri)], writes=[("rc", ri)])
                P.op("dve", lambda e, ri=ri, accb=accb, dlo=dlo, po=po, hp=hp, W=W: e.tensor_tensor(
                    out=CC[po:po + 64, hp, 512 * W:512 * (W + 1)], in0=ps[po:po + 64, accb, :], in1=rc[ri][dlo:dlo + 64, :], op=ALU.mult),
                    reads=[("ps", accb), ("rc", ri)], writes=[("CCa", hp, W, hh)])

        LOOK = 2
        for q in range(min(LOOK, len(G))):
            emit_qk(G[q])
        for q in range(len(G)):
            emit_pv(G[q])
            if q + LOOK < len(G):
                emit_qk(G[q + LOOK])
    P.fence()

    g0 = 146 * KB
    wz = sb("wz", [128, 8, 512], BF16, g0)
    wu = [sb("wu%d" % c, [128, 8, 128], BF16, g0 + 8 * KB + 2 * KB * c) for c in range(4)]
    wsf = sb("wsf", [128, 8, 128], F32, g0 + 16 * KB)
    wsm = sb("wsm", [128, 8, 128], BF16, g0 + 20 * KB)
    bsb = sb("bsb", [128, 4, 128], F32, g0 + 22 * KB)
    lzg = sb("lzg", [128, 512], F32, g0 + 24 * KB)
    lzb = sb("lzb", [128, 512], F32, g0 + 26 * KB)
    ug = [sb("ug%d" % i, [128, 4, 512], F32, g0 + 28 * KB + 8 * KB * i) for i in range(2)]
    zg = [sb("zg%d" % i, [128, 512], F32, g0 + 54 * KB + 2 * KB * i) for i in range(4)]
    zn = [sb("zn%d" % i, [128, 512], BF16, g0 + 62 * KB + KB * i) for i in range(4)]
    mx = [sb("mx%d" % i, [128, 4, 128], F32, g0 + 50 * KB + 2 * KB * i) for i in range(2)]

    P.op("pool", lambda e: e.dma_start(out=wz[:], in_=w_in_d[:, 2048:2560].rearrange("(kc p) c -> p kc c", p=128)),
         writes=["wz"], dma=True, lane=("g", 0))
    for c in range(4):
        P.op("pool", lambda e, c=c: e.dma_start(out=wu[c][:], in_=w_in_d[:, 1536 + 128 * c:1536 + 128 * (c + 1)].rearrange("(kc p) c -> p kc c", p=128)),
             writes=[("wu", c)], dma=True, lane=("g", 1 + c))
    P.op("sp", lambda e: e.dma_start(out=wsf[:], in_=wsT_d.rearrange("g j i -> j g i")), writes=["wsf"], dma=True, lane=("g", 5))
    for gp in range(4):
        for hf in range(2):
            P.op("sp", lambda e, gp=gp, hf=hf: e.dma_start(out=bsb[64 * hf:64 * (hf + 1), gp, :], in_=bs_d[2 * gp + hf, :].partition_broadcast(64)),
                 writes=[("bsb", gp, hf)], dma=True, lane=("g", 6 + gp * 2 + hf))
    P.op("sp", lambda e: e.dma_start(out=lzg[:], in_=lzg_d[0, :].partition_broadcast(128)), writes=["lzg"], dma=True, lane=("g", 14))
    P.op("sp", lambda e: e.dma_start(out=lzb[:], in_=lzb_d[0, :].partition_broadcast(128)), writes=["lzb"], dma=True, lane=("g", 15))
    P.op("dve", lambda e: e.tensor_tensor(out=wsm[:], in0=wsf[:], in1=mL[:, :].unsqueeze(1).broadcast_to([128, 8, 128]), op=ALU.mult),
         reads=["wsf", "mL"], writes=["wsm"])

    def layernorm_stats(src_fn, nchunks, slot, key_in):
        for c in range(nchunks):
            P.op("dve", lambda e, c=c: e.bn_stats(out=lnst[:, slot, c, :], in_=src_fn(c)), reads=[key_in], writes=[("lnst", slot, c)])
        P.op("dve", lambda e: e.bn_aggr(out=lnmv[:, slot, 0:2], in_=lnst[:, slot, 0:nchunks, :]),
             reads=[("lnst", slot, c) for c in range(nchunks)], writes=[("mv", slot)])
        P.op("act", lambda e: e.activation(out=lnmv[:, slot, 2:3], in_=lnmv[:, slot, 1:2], func=AF.Sqrt, bias=epsb[:, 0:1]),
             reads=[("mv", slot), "epsb"], writes=[("sd", slot)])
        P.op("dve", lambda e: e.reciprocal(out=lnmv[:, slot, 2:3], in_=lnmv[:, slot, 2:3]), reads=[("sd", slot)], writes=[("sd", slot)])
        P.op("dve", lambda e: e.tensor_scalar(out=lnmv[:, slot, 3:4], in0=lnmv[:, slot, 0:1], scalar1=lnmv[:, slot, 2:3], scalar2=-1.0,
                                              op0=ALU.mult, op1=ALU.mult),
             reads=[("mv", slot), ("sd", slot)], writes=[("nmr", slot)])

    epsb = sb("epsb", [128, 1], F32, c0 + 1600)
    P.op("dve", lambda e: e.memset(epsb[:], EPS), writes=["epsb"])

    ln_ctr = [0]

    def g_uproj(T8):
        ub = T8 % 2
        for c in range(4):
            bk = c % 2
            for kc in range(8):
                P.op("pe", lambda e, bk=bk, c=c, kc=kc, T8=T8: e.matmul(ps[:, bk, :], lhsT=wu[c][:, kc, :], rhs=XT[:, kc, 512 * T8:512 * (T8 + 1)],
                                                                    start=(kc == 0), stop=(kc == 7)),
                     reads=[("wu", c)], writes=[("ps", bk)])
            P.op("act", lambda e, bk=bk, c=c, ub=ub: e.activation(out=ug[ub][:, c, :], in_=ps[:, bk, :], func=AF.Gelu),
                 reads=[("ps", bk)], writes=[("ug", ub, c)])

    def g_zproj(i):
        zb = i % 4
        bk = 2 + i % 2
        for kc in range(8):
            P.op("pe", lambda e, bk=bk, kc=kc, i=i: e.matmul(ps[:, bk, :], lhsT=XT[:, kc, 128 * i:128 * (i + 1)], rhs=wz[:, kc, :],
                                                             start=(kc == 0), stop=(kc == 7)),
                 reads=["wz"], writes=[("ps", bk)])
        P.op("act", lambda e, bk=bk, zb=zb: e.activation(out=zg[zb][:], in_=ps[:, bk, :], func=AF.Gelu),
             reads=[("ps", bk)], writes=[("zg", zb)])
        slot = ln_ctr[0] % 4
        ln_ctr[0] += 1
        layernorm_stats(lambda c, zb=zb: zg[zb][:], 1, slot, ("zg", zb))
        P.op("act", lambda e, zb=zb, slot=slot: e.activation(out=zg[zb][:], in_=zg[zb][:], func=AF.Identity,
                                                             scale=lnmv[:, slot, 2:3], bias=lnmv[:, slot, 3:4]),
             reads=[("zg", zb), ("sd", slot), ("nmr", slot)], writes=[("zg", zb)])
        P.op("pool", lambda e, zb=zb: e.tensor_tensor(out=zg[zb][:], in0=zg[zb][:], in1=lzg[:], op=ALU.mult),
             reads=[("zg", zb), "lzg"], writes=[("zg", zb)])
        P.op("pool", lambda e, zb=zb: e.tensor_tensor(out=zn[zb][:], in0=zg[zb][:], in1=lzb[:], op=ALU.add),
             reads=[("zg", zb), "lzb"], writes=[("zn", zb)])

    def g_spatial(i):
        zb = i % 4
        mb = i % 2
        T8, tt = i // 4, i % 4
        ub = T8 % 2
        sbk = 4 + i % 2
        for gp in range(4):
            for hf in range(2):
                g = 2 * gp + hf
                P.op("pe", lambda e, sbk=sbk, gp=gp, hf=hf, g=g, zb=zb: e.matmul(
                    ps[64 * hf:64 * (hf + 1), sbk, 128 * gp:128 * (gp + 1)], lhsT=zn[zb][:, 64 * g:64 * (g + 1)], rhs=wsm[:, g, :],
                    start=True, stop=True),
                    reads=[("zn", zb), "wsm"], writes=[("ps", sbk)])
        P.op("dve", lambda e, sbk=sbk, mb=mb: e.tensor_tensor(out=mx[mb][:], in0=ps[:, sbk, :].rearrange("p (a b) -> p a b", b=128),
                                                           in1=bsb[:], op=ALU.add),
             reads=[("ps", sbk)] + [("bsb", gp, hf) for gp in range(4) for hf in range(2)], writes=[("mx", mb)])
        P.op("pool", lambda e, mb=mb, ub=ub, tt=tt, i=i: e.tensor_tensor(out=CC[:, 4:8, 128 * i:128 * (i + 1)], in0=mx[mb][:],
                                                                      in1=ug[ub][:, :, 128 * tt:128 * (tt + 1)], op=ALU.mult),
             reads=[("mx", mb)] + [("ug", ub, c) for c in range(4)], writes=[("CCg", i)])

    GL = 2
    for i in range(32 + GL):
        if i < 32:
            if i % 4 == 0:
                g_uproj(i // 4)
            g_zproj(i)
        if i >= GL:
            g_spatial(i - GL)

    if stop_after == "A":
        P.fence()
        o = P.op("sp", lambda e: e.dma_start(out=out_d, in_=CC[:].rearrange("p a b -> p (a b)")), writes=["out"], dma=True, lane=("o", 0))
        P.final_waits = [o.idx]
        P.emit()
        return nc
    P.fence()

    t0 = 82 * KB
    lnp = {}
    for qi, nm in enumerate(("ln1_g", "ln1_b", "ln2_g", "ln2_b", "ln3_g", "ln3_b", "b_ple_gate")):
        lnp[nm] = sb("lnp_" + nm, [128, D], F32, t0 + 4 * KB * qi)
        P.op("sp", lambda e, nm=nm: e.dma_start(out=lnp[nm][:], in_=ln_d[nm][0, :].partition_broadcast(128)),
             writes=[("lnp", nm)], dma=True, lane=("lnp", qi))
    x1s = [sb("x1_%d" % i, [128, 4, D], F32, 110 * KB + 16 * KB * i) for i in range(2)]
    gT = sb("gT", [128, NF, 512], BF16, 142 * KB)
    pT = sb("pT", [128, 2, 512], BF16, 164 * KB)
    NS = 6
    ringS = [sb("ringS%d" % i, [128, D], BF16, 166 * KB + 2 * KB * i) for i in range(NS)]
    grp = [sb("grp%d" % i, [128, D], BF16, 178 * KB + 2 * KB * i) for i in range(10)]
    xr = [sb("xr0", [128, D], F32, 198 * KB), sb("xr1", [128, D], F32, 217 * KB)]
    pbt = [sb("pbt%d" % i, [128, DPLE], BF16, 202 * KB + 512 * i) for i in range(4)]
    abuf = [sb("abuf%d" % i, [128, 520], F32, 204 * KB + 2080 * i) for i in range(2)]
    cacc = [sb("cacc%d" % i, [128, 512], F32, 209 * KB + 2 * KB * i) for i in range(2)]
    gtmp = [sb("gtmp0", [128, D], F32, 213 * KB)]
    cw = sb("cw", [128, 3 * NF], F32, 221 * KB)
    cb = sb("cb", [128, NF], F32, 221 * KB + 288)
    halo = sb("halo", [128, NF, 2], F32, 221 * KB + 384)
    lnT = sb("lnT", [128, 32], F32, 221 * KB + 576)

    P.op("sp", lambda e: e.dma_start(out=cw[:], in_=cw_d), writes=["cw"], dma=True, lane=("t", 0))
    P.op("sp", lambda e: e.dma_start(out=cb[:], in_=cb_d), writes=["cb"], dma=True, lane=("t", 1))
    P.op("sp", lambda e: e.dma_start(out=lnT[:], in_=lnT_d), writes=["lnT"], dma=True, lane=("t", 2))
    P.op("dve", lambda e: e.memset(halo[:], 0.0), writes=["halo"])

    seqS = []
    for Tt_ in range(8):
        for f in range(NF):
            seqS += [8 + 2 * f, 8 + 2 * f + 1]
        for f in range(NF):
            seqS.append(52 + f)
    PREF = 4
    emitted = [0]
    usectr = [0]

    def ring_next():
        n = usectr[0]
        usectr[0] += 1
        while emitted[0] <= min(n + PREF, len(seqS) - 1):
            m = emitted[0]
            emitted[0] += 1
            s_ = m % NS
            P.op("sp", lambda e, s_=s_, m=m: e.dma_start(out=ringS[s_][:], in_=tape[seqS[m]]),
                 writes=[("ringS", s_)], dma=True, lane=("ringS", s_))
        return n % NS

    def grp_load(first, count, tape0):
        for q in range(count):
            P.op("sp", lambda e, q=q: e.dma_start(out=grp[first + q][:], in_=tape[tape0 + q]),
                 writes=[("grp", first + q)], dma=True, lane=("grp", first + q))

    def ln_norm(x1, par, tt):
        slot = ln_ctr[0] % 4
        ln_ctr[0] += 1
        layernorm_stats(lambda c: x1[:, tt, 512 * c:512 * (c + 1)], 2, slot, ("x1", par, tt))
        P.op("act", lambda e: e.activation(out=x1[:, tt, :], in_=x1[:, tt, :], func=AF.Identity,
                                           scale=lnmv[:, slot, 2:3], bias=lnmv[:, slot, 3:4]),
             reads=[("x1", par, tt), ("sd", slot), ("nmr", slot)], writes=[("x1", par, tt)])

    def ln_affine(x1, par, tt, gname, bname):
        P.op("pool", lambda e: e.tensor_tensor(out=x1[:, tt, :], in0=x1[:, tt, :], in1=lnp[gname][:], op=ALU.mult),
             reads=[("x1", par, tt), ("lnp", gname)], writes=[("x1", par, tt)])
        P.op("pool", lambda e: e.tensor_tensor(out=x1[:, tt, :], in0=x1[:, tt, :], in1=lnp[bname][:], op=ALU.add),
             reads=[("x1", par, tt), ("lnp", bname)], writes=[("x1", par, tt)])

    evc = [0]

    def transpose_to_cc(x1, par, tt, i, b0, lnbase):
        for k in range(8):
            P.op("pe", lambda e, k=k: e.transpose(out=ps[:, b0 + k // 4, (k % 4) * 128:(k % 4 + 1) * 128],
                                                  in_=x1[:, tt, 128 * k:128 * (k + 1)], identity=identf[:]),
                 reads=[("x1", par, tt), "identf"], writes=[("ps", b0 + k // 4)])
        for k in range(8):
            src = ps[:, b0 + k // 4, (k % 4) * 128:(k % 4 + 1) * 128]
            dst = CC[:, k, 128 * i:128 * (i + 1)]
            gcol = lnT[:, lnbase + k:lnbase + k + 1]
            bcol = lnT[:, lnbase + 8 + k:lnbase + 8 + k + 1]
            if (evc[0] + k // 4) % 2 == 0:
                P.op("act", lambda e, src=src, dst=dst, gcol=gcol, bcol=bcol: e.activation(out=dst, in_=src, func=AF.Identity, scale=gcol, bias=bcol),
                     reads=[("ps", b0 + k // 4), "lnT"], writes=[("CT", i, k)])
            else:
                P.op("dve", lambda e, src=src, dst=dst, gcol=gcol, bcol=bcol: e.tensor_scalar(out=dst, in0=src, scalar1=gcol, scalar2=bcol,
                                                                                          op0=ALU.mult, op1=ALU.add),
                     reads=[("ps", b0 + k // 4), "lnT"], writes=[("CT", i, k)])
        evc[0] += 1

    def p_loads(Tt):
        for tt in range(4):
            i = 4 * Tt + tt
            P.op("pool", lambda e, i=i, tt=tt: e.dma_start(out=pbt[tt][:], in_=p_d[128 * i:128 * (i + 1), :]),
                 writes=[("pbt", tt)], dma=True, lane=("pbt", tt))

    def stB_main(Tt, tt, b0):
        par = Tt % 2
        x1 = x1s[par]
        i = 4 * Tt + tt
        xj = i % 2
        P.op("sp", lambda e, i=i, xj=xj: e.dma_start(out=xr[xj][:], in_=x_d[128 * i:128 * (i + 1), :]),
             writes=[("xr", xj)], dma=True, lane=("xr", xj))
        for hf in range(2):
            for k in range(8):
                P.op("pe", lambda e, hf=hf, k=k, i=i, b0=b0: e.matmul(ps[:, b0 + hf, :], lhsT=CC[:, k, 128 * i:128 * (i + 1)],
                                                                   rhs=grp[k][:, 512 * hf:512 * (hf + 1)],
                                                                   start=(k == 0), stop=(k == 7)),
                     reads=[("CT", i, k), ("grp", k)], writes=[("ps", b0 + hf)])
        P.op("dve", lambda e, tt=tt, b0=b0, xj=xj, x1=x1: e.scalar_tensor_tensor(
            out=x1[:, tt, :].rearrange("p (a b) -> p a b", a=2), in0=xr[xj][:].rearrange("p (a b) -> p a b", a=2), scalar=ALPHA,
            in1=ps[:, b0:b0 + 2, :], op0=ALU.mult, op1=ALU.add),
            reads=[("xr", xj), ("ps", b0), ("ps", b0 + 1)], writes=[("x1", par, tt)])
        ln_norm(x1, par, tt)

    def stB_post(Tt, tt, b0):
        par = Tt % 2
        x1 = x1s[par]
        transpose_to_cc(x1, par, tt, 4 * Tt + tt, b0, 0)
        ln_affine(x1, par, tt, "ln1_g", "ln1_b")

    def ffn_up_chunk(Tt, f):
        tok0 = 512 * Tt
        sa = ring_next()
        sbb = ring_next()
        ba = f % 2
        bb = 2 + f % 2
        aj = f % 2
        for (bank, slot) in ((ba, sa), (bb, sbb)):
            for kc in range(8):
                P.op("pe", lambda e, bank=bank, slot=slot, kc=kc, tok0=tok0: e.matmul(
                    ps[:, bank, :], lhsT=ringS[slot][:, 128 * kc:128 * (kc + 1)], rhs=CC[:, kc, tok0:tok0 + 512],
                    start=(kc == 0), stop=(kc == 7)),
                    reads=[("ringS", slot)] + [("CT", 4 * Tt + q, kc) for q in range(4)], writes=[("ps", bank)])
        P.op("dve", lambda e, aj=aj, f=f: e.tensor_copy(out=abuf[aj][:, 0:2], in_=halo[:, f, :]),
             reads=[("halo", f), "halo"], writes=[("abh", aj)])
        P.op("act", lambda e, aj=aj, ba=ba: e.copy(out=abuf[aj][:, 2:514], in_=ps[:, ba, :]),
             reads=[("ps", ba)], writes=[("ab", aj)])
        P.op("dve", lambda e, aj=aj, f=f: e.tensor_copy(out=halo[:, f, :], in_=abuf[aj][:, 512:514]),
             reads=[("ab", aj)], writes=[("halo", f)])
        P.op("act", lambda e, aj=aj, ba=ba, f=f: e.activation(out=cacc[aj][:], in_=ps[:, ba, :], func=AF.Identity,
                                                              scale=cw[:, 2 * NF + f:2 * NF + f + 1], bias=cb[:, f:f + 1]),
             reads=[("ps", ba), "cw", "cb"], writes=[("cacc", aj)])
        P.op("dve", lambda e, aj=aj, f=f: e.scalar_tensor_tensor(out=cacc[aj][:], in0=abuf[aj][:, 1:513], scalar=cw[:, NF + f:NF + f + 1],
                                                                in1=cacc[aj][:], op0=ALU.mult, op1=ALU.add),
             reads=[("ab", aj), ("abh", aj), ("cacc", aj), "cw"], writes=[("cacc", aj)])
        P.op("dve", lambda e, aj=aj, f=f: e.scalar_tensor_tensor(out=cacc[aj][:], in0=abuf[aj][:, 0:512], scalar=cw[:, f:f + 1],
                                                                in1=cacc[aj][:], op0=ALU.mult, op1=ALU.add),
             reads=[("ab", aj), ("abh", aj), ("cacc", aj), "cw"], writes=[("cacc", aj)])
        P.op("act", lambda e, aj=aj: e.activation(out=cacc[aj][:], in_=cacc[aj][:], func=AF.Gelu),
             reads=[("cacc", aj)], writes=[("cacc", aj)])
        P.op("dve", lambda e, aj=aj, bb=bb, f=f: e.tensor_tensor(out=gT[:, f, :], in0=cacc[aj][:], in1=ps[:, bb, :], op=ALU.mult),
             reads=[("cacc", aj), ("ps", bb)], writes=[("gT", f)])

    def p_transposes():
        for tt in range(4):
            for kc in range(2):
                P.op("pe", lambda e, tt=tt, kc=kc: e.transpose(out=psb[:, 4, (tt * 2 + kc) * 128:(tt * 2 + kc + 1) * 128],
                                                               in_=pbt[tt][:, 128 * kc:128 * (kc + 1)], identity=identb[:]),
                     reads=[("pbt", tt), "identb"], writes=[("ps", 4)])
        P.op("dve", lambda e: e.tensor_copy(out=pT[:].rearrange("p k (t c) -> p k t c", c=128),
                                            in_=psb[:, 4, :].rearrange("p (t k c) -> p k t c", t=4, k=2)),
             reads=[("ps", 4)], writes=["pT"])

    def ffn_down(Tt):
        for f in range(NF):
            sd = ring_next()
            for tt in range(4):
                for hf in range(2):
                    P.op("pe", lambda e, f=f, tt=tt, hf=hf, sd=sd: e.matmul(ps[:, 2 * tt + hf, :], lhsT=gT[:, f, 128 * tt:128 * (tt + 1)],
                                                                        rhs=ringS[sd][:, 512 * hf:512 * (hf + 1)],
                                                                        start=(f == 0), stop=(f == NF - 1)),
                         reads=[("gT", f), ("ringS", sd)], writes=[("ps", 2 * tt + hf)])

    def ln2_main(Tt, tt):
        par = Tt % 2
        x1 = x1s[par]
        b0 = 2 * tt
        P.op("dve", lambda e, tt=tt, b0=b0, x1=x1: e.scalar_tensor_tensor(
            out=x1[:, tt, :].rearrange("p (a b) -> p a b", a=2), in0=x1[:, tt, :].rearrange("p (a b) -> p a b", a=2), scalar=ALPHA,
            in1=ps[:, b0:b0 + 2, :], op0=ALU.mult, op1=ALU.add),
            reads=[("x1", par, tt), ("ps", b0), ("ps", b0 + 1)], writes=[("x1", par, tt)])
        ln_norm(x1, par, tt)

    def ln2_post(Tt, tt):
        par = Tt % 2
        x1 = x1s[par]
        transpose_to_cc(x1, par, tt, 4 * Tt + tt, 2 * tt, 16)
        ln_affine(x1, par, tt, "ln2_g", "ln2_b")

    def gate_a(Tt, tt):
        par = Tt % 2
        x1 = x1s[par]
        i = 4 * Tt + tt
        bg_, bp_ = 4 * (tt % 2), 4 * (tt % 2) + 2
        for hf in range(2):
            for k in range(8):
                P.op("pe", lambda e, hf=hf, k=k, i=i, bg_=bg_: e.matmul(ps[:, bg_ + hf, :], lhsT=CC[:, k, 128 * i:128 * (i + 1)],
                                                                     rhs=grp[k][:, 512 * hf:512 * (hf + 1)],
                                                                     start=(k == 0), stop=(k == 7)),
                     reads=[("CT", i, k), ("grp", k)], writes=[("ps", bg_ + hf)])
        for hf in range(2):
            for k in range(2):
                P.op("pe", lambda e, hf=hf, k=k, tt=tt, bp_=bp_: e.matmul(ps[:, bp_ + hf, :], lhsT=pT[:, k, 128 * tt:128 * (tt + 1)],
                                                                      rhs=grp[8 + k][:, 512 * hf:512 * (hf + 1)],
                                                                      start=(k == 0), stop=(k == 1)),
                     reads=["pT", ("grp", 8 + k)], writes=[("ps", bp_ + hf)])
        gt = gtmp[0]
        P.op("dve", lambda e, gt=gt, bg_=bg_: e.tensor_tensor(out=gt[:].rearrange("p (a b) -> p a b", a=2), in0=ps[:, bg_:bg_ + 2, :],
                                                          in1=lnp["b_ple_gate"][:].rearrange("p (a b) -> p a b", a=2), op=ALU.add),
             reads=[("ps", bg_), ("ps", bg_ + 1), ("lnp", "b_ple_gate")], writes=[("gtmp", 0)])
        P.op("act", lambda e, gt=gt: e.activation(out=gt[:], in_=gt[:], func=AF.Sigmoid), reads=[("gtmp", 0)], writes=[("gtmp", 0)])
        P.op("dve", lambda e, gt=gt, bp_=bp_: e.tensor_tensor(out=gt[:].rearrange("p (a b) -> p a b", a=2),
                                                          in0=gt[:].rearrange("p (a b) -> p a b", a=2), in1=ps[:, bp_:bp_ + 2, :], op=ALU.mult),
             reads=[("gtmp", 0), ("ps", bp_), ("ps", bp_ + 1)], writes=[("gtmp", 0)])
        P.op("dve", lambda e, gt=gt, tt=tt, x1=x1: e.scalar_tensor_tensor(out=x1[:, tt, :], in0=x1[:, tt, :], scalar=ALPHA, in1=gt[:],
                                                                       op0=ALU.mult, op1=ALU.add),
             reads=[("x1", par, tt), ("gtmp", 0)], writes=[("x1", par, tt)])

    def gate_b(Tt, tt):
        par = Tt % 2
        x1 = x1s[par]
        i = 4 * Tt + tt
        ln_norm(x1, par, tt)
        ln_affine(x1, par, tt, "ln3_g", "ln3_b")
        o = P.op("pool", lambda e, i=i, tt=tt, x1=x1: e.dma_start(out=out_d[128 * i:128 * (i + 1), :], in_=x1[:, tt, :]),
                 reads=[("x1", par, tt)], writes=[("out", i)], dma=True, lane=("o", par, tt))
        P.final_waits.append(o.idx)

    grp_load(0, 8, 0)
    p_loads(0)
    stB_main(0, 0, 4)
    stB_main(0, 1, 6)
    stB_post(0, 0, 0)
    stB_main(0, 2, 4)
    stB_post(0, 1, 2)
    stB_main(0, 3, 6)
    stB_post(0, 2, 0)
    stB_post(0, 3, 2)

    for Tt in range(8):
        nxt = Tt + 1 < 8
        extra = {}
        if Tt > 0:
            for tt in range(4):
                extra.setdefault(tt, []).append(lambda tt=tt, Tt=Tt: gate_b(Tt - 1, tt))
        if nxt:
            extra.setdefault(0, []).insert(0, lambda Tt=Tt: grp_load(0, 8, 0))
            for tt in range(4):
                extra.setdefault(5 + 4 * tt, []).append(lambda tt=tt, Tt=Tt: stB_main(Tt + 1, tt, 4))
                extra.setdefault(8 + 4 * tt, []).append(lambda tt=tt, Tt=Tt: stB_post(Tt + 1, tt, 6))
        if not nxt:
            pass
        if Tt > 0:
            p_loads(Tt)
        for f in range(NF):
            ffn_up_chunk(Tt, f)
            for fn in extra.get(f, []):
                fn()
        grp_load(0, 10, 74)
        p_transposes()
        ffn_down(Tt)
        ln2_main(Tt, 0)
        ln2_main(Tt, 1)
        ln2_post(Tt, 0)
        ln2_main(Tt, 2)
        ln2_post(Tt, 1)
        ln2_main(Tt, 3)
        ln2_post(Tt, 2)
        ln2_post(Tt, 3)
        for tt in range(4):
            gate_a(Tt, tt)
    for tt in range(4):
        gate_b(7, tt)
    P.emit()
    return nc


def _consts():
    j = np.arange(128)[:, None]
    i = np.arange(128)[None, :]
    maskL = (j <= i).astype(np.float32)
    maskU = (j >= i).astype(np.float32)
    ident = np.eye(128, dtype=np.float32)
    inv8 = np.float32(500000.0) ** (-(np.arange(0, 16, 2, dtype=np.float32)) / np.float32(16.0))
    ropec = np.zeros((128, 2), np.float32)
    for p_ in range(128):
        d = p_ % 64
        if d < 16:
            ropec[p_, 0] = inv8[d % 8]
            ropec[p_, 1] = -1.0 if d < 8 else 1.0
    return ident, maskL, maskU, ropec


def _prep_shared(inp):
    f = lambda a: np.ascontiguousarray(np.asarray(a, dtype=np.float32))
    w_in = f(inp["w_in"][0])
    perm = np.arange(1024)
    for h in range(16):
        base = h * 64
        perm[base:base + 8] = np.arange(base + 8, base + 16)
        perm[base + 8:base + 16] = np.arange(base, base + 8)
    w_sw = np.ascontiguousarray(w_in[:, :1024][:, perm])
    conv_w = f(inp["conv_w"][0])
    conv_b = f(inp["conv_b"][0])
    cw = np.ascontiguousarray(conv_w.reshape(3, NF, 128).transpose(2, 0, 1).reshape(128, 3 * NF))
    cb = np.ascontiguousarray(conv_b.reshape(NF, 128).T)
    ident, maskL, maskU, ropec = _consts()
    sh = {
        "w_in": w_in, "w_sw": w_sw,
        "ln_z_g": f(inp["ln_z_g"]), "ln_z_b": f(inp["ln_z_b"]),
        "w_sT": np.ascontiguousarray(f(inp["w_s"][0]).transpose(0, 2, 1)),
        "b_s": f(inp["b_s"][0]),
        "w_o": f(inp["w_o"][0]),
        "w_ff_a": f(inp["w_ff_a"][0]), "w_ff_b": f(inp["w_ff_b"][0]),
        "cw": cw, "cb": cb,
        "w_ff_down": f(inp["w_ff_down"][0]),
        "w_ple_gate": f(inp["w_ple_gate"][0]),
        "w_ple_in": f(inp["w_ple_in"][0]),
        "ident": ident, "maskL": maskL, "maskU": maskU, "ropec": ropec,
    }
    for nm in ("ln1_g", "ln1_b", "ln2_g", "ln2_b", "ln3_g", "ln3_b", "b_ple_gate"):
        sh[nm] = f(inp[nm])
    lnT = np.zeros((128, 32), np.float32)
    for j, (gn, bn) in enumerate((("ln1_g", "ln1_b"), ("ln2_g", "ln2_b"))):
        lnT[:, 16 * j:16 * j + 8] = sh[gn].reshape(8, 128).T
        lnT[:, 16 * j + 8:16 * j + 16] = sh[bn].reshape(8, 128).T
    sh["lnT"] = np.ascontiguousarray(lnT)
    return sh


_NC_CACHE = {}


def kernel(**inputs):
    sh = _prep_shared(inputs)
    x = np.asarray(inputs["x"], dtype=np.float32)
    p = np.asarray(inputs["p"], dtype=np.float32)
    pos = np.asarray(inputs["positions"], dtype=np.int32)
    in_maps = []
    for b in range(8):
        m = dict(sh)
        m["x"] = np.ascontiguousarray(x[b])
        m["p"] = np.ascontiguousarray(p[0, b])
        m["pos"] = np.ascontiguousarray(pos[b:b + 1])
        in_maps.append(m)
    nc = build()
    res = run_bass_kernel_spmd(nc, in_maps, core_ids=list(range(8)))
    out = np.stack([np.asarray(r["out"], dtype=np.float32) for r in res.results], axis=0)
    return out
```

```python
import numpy as np
import concourse.bass as bass
import concourse.mybir as mybir
from concourse.bass_utils import run_bass_kernel_spmd

F32 = mybir.dt.float32
BF16 = mybir.dt.bfloat16
I32 = mybir.dt.int32
AF = mybir.ActivationFunctionType
ALU = mybir.AluOpType

S = 4096
D = 1024
DFF = 2816
NF = 22
DPLE = 256
ALPHA = float(2.0 ** 0.25)
EPS = 1e-5
PI = float(np.pi)
TWO_PI = float(2 * np.pi)
KB = 1024


class Op:
    __slots__ = ("idx", "eng", "fn", "deps", "is_dma", "lane", "sig", "signals")

    def __init__(self, idx, eng, fn, is_dma, lane):
        self.idx = idx
        self.eng = eng
        self.fn = fn
        self.deps = set()
        self.is_dma = is_dma
        self.lane = lane
        self.sig = None
        self.signals = False


class Prog:
    ENGS = ("pe", "act", "dve", "pool", "sp")

    def __init__(self, nc):
        self.nc = nc
        self.ops = []
        self.state = {}
        self.final_waits = []
        self.last_eng = {}
        self.last_lane = {}
        self.pending_fence = {}

    def op(self, eng, fn, reads=(), writes=(), dma=False, lane=None):
        o = Op(len(self.ops), eng, fn, dma, lane)
        self.ops.append(o)
        pf = self.pending_fence.pop(eng, None)
        if pf is not None:
            o.deps.update(pf)
        for k in reads:
            st = self.state.get(k)
            if st is None:
                st = [None, []]
                self.state[k] = st
            if st[0] is not None:
                self._dep(o, st[0], "raw")
            if isinstance(k, tuple) and k[0] == "ps":
                for r in st[1]:
                    if self.ops[r].eng != eng:
                        o.deps.add(r)
            if not dma:
                st[1] = [r for r in st[1] if self.ops[r].is_dma or self.ops[r].eng != eng]
            st[1].append(o.idx)
        for k in writes:
            st = self.state.get(k)
            if st is None:
                st = [None, []]
                self.state[k] = st
            if st[0] is not None:
                self._dep(o, st[0], "waw")
            for r in st[1]:
                if r != o.idx:
                    self._dep(o, r, "war")
            st[0] = o.idx
            st[1] = []
        if dma:
            self.last_lane[tuple(lane)] = o.idx
        else:
            self.last_eng[eng] = o.idx
        return o

    def _dep(self, o, j, kind):
        t = self.ops[j]
        if t.eng == o.eng and not t.is_dma and not o.is_dma:
            if o.eng == "pe":
                return
        o.deps.add(j)

    def fence(self):
        deps = set(self.last_eng.values()) | set(self.last_lane.values())
        for e in self.ENGS:
            self.pending_fence[e] = set(deps)
        self.state = {}

    def emit(self):
        nc = self.nc
        ops = self.ops
        for o in ops:
            o.deps.discard(o.idx)
            if o.is_dma:
                o.signals = True
            for j in o.deps:
                ops[j].signals = True
        for j in self.final_waits:
            ops[j].signals = True
        sems = {}
        counters = {}
        for o in ops:
            if not o.signals:
                continue
            key = ("dma",) + tuple(o.lane) if o.is_dma else ("eng", o.eng)
            if key not in sems:
                sems[key] = nc.alloc_semaphore("s_" + "_".join(str(x) for x in key))
                counters[key] = 0
            counters[key] += 16 if o.is_dma else 1
            o.sig = (key, counters[key])
        streams = {e: [o for o in ops if o.eng == e] for e in self.ENGS}
        final = [ops[j].sig for j in self.final_waits]

        def run_stream(e, engh):
            waited = {}
            for o in streams[e]:
                need = {}
                for j in o.deps:
                    t = ops[j]
                    if t.eng == e and not t.is_dma and not o.is_dma and e == "pe":
                        continue
                    k, v = t.sig
                    if need.get(k, 0) < v:
                        need[k] = v
                for k, v in need.items():
                    if waited.get(k, 0) >= v:
                        continue
                    engh.wait_ge(sems[k], v)
                    waited[k] = v
                ins = o.fn(engh)
                if o.signals:
                    ins.then_inc(sems[o.sig[0]], 16 if o.is_dma else 1)
            if e == "sp":
                need = {}
                for k, v in final:
                    if need.get(k, 0) < v:
                        need[k] = v
                for k, v in need.items():
                    engh.wait_ge(sems[k], v)

        with nc.Block() as block:
            @block.tensor
            def _(pe):
                run_stream("pe", pe)

            @block.scalar
            def _(act):
                run_stream("act", act)

            @block.vector
            def _(dve):
                run_stream("dve", dve)

            @block.gpsimd
            def _(pool):
                run_stream("pool", pool)

            @block.sync
            def _(sp):
                run_stream("sp", sp)


def build(stop_after=None):
    nc = bass.Bass("TRN2", target_bir_lowering=False)
    P = Prog(nc)

    def din(name, shape, dt=F32):
        return nc.dram_tensor(name, list(shape), dt, kind="ExternalInput").ap()

    x_d = din("x", [S, D])
    p_d = din("p", [S, DPLE])
    pos_d = din("pos", [1, S], I32)
    w_in_d = din("w_in", [D, 2560])
    w_sw_d = din("w_sw", [D, 1024])
    lzg_d = din("ln_z_g", [1, 512])
    lzb_d = din("ln_z_b", [1, 512])
    wsT_d = din("w_sT", [8, 128, 128])
    bs_d = din("b_s", [8, 128])
    wo_d = din("w_o", [D, D])
    ln_d = {}
    for nm in ("ln1_g", "ln1_b", "ln2_g", "ln2_b", "ln3_g", "ln3_b", "b_ple_gate"):
        ln_d[nm] = din(nm, [1, D])
    wa_d = din("w_ff_a", [D, DFF])
    wb_d = din("w_ff_b", [D, DFF])
    cw_d = din("cw", [128, 3 * NF])
    cb_d = din("cb", [128, NF])
    wd_d = din("w_ff_down", [DFF, D])
    wg_d = din("w_ple_gate", [D, D])
    wp_d = din("w_ple_in", [DPLE, D])
    ident_d = din("ident", [128, 128])
    mL_d = din("maskL", [128, 128])
    mU_d = din("maskU", [128, 128])
    ropec_d = din("ropec", [128, 2])
    lnT_d = din("lnT", [128, 32])
    if stop_after == "A":
        out_d = nc.dram_tensor("out", [128, 8 * S], BF16, kind="ExternalOutput").ap()
    elif stop_after in ("B", "D", "G"):
        out_d = nc.dram_tensor("out", [128, 4 * D], F32, kind="ExternalOutput").ap()
    elif stop_after == "F":
        out_d = nc.dram_tensor("out", [128, NF * 512], BF16, kind="ExternalOutput").ap()
    else:
        out_d = nc.dram_tensor("out", [S, D], F32, kind="ExternalOutput").ap()

    tape = nc.dram_tensor("tape", [84, 128, D], BF16).ap()
    chunks84 = []
    for k in range(8):
        chunks84.append((wo_d[128 * k:128 * (k + 1), :], False))
    for f in range(NF):
        chunks84.append((wa_d[:, 128 * f:128 * (f + 1)], True))
        chunks84.append((wb_d[:, 128 * f:128 * (f + 1)], True))
    for f in range(NF):
        chunks84.append((wd_d[128 * f:128 * (f + 1), :], False))
    for k in range(8):
        chunks84.append((wg_d[128 * k:128 * (k + 1), :], False))
    for k in range(2):
        chunks84.append((wp_d[128 * k:128 * (k + 1), :], False))

    def sb(name, shape, dt, off):
        assert off % 32 == 0, (name, off)
        return nc.alloc_sbuf_tensor_at(name, list(shape), dt, offset=int(off))

    ps = nc.alloc_psum_tensor("ps", [128, 8, 512], F32)
    psb = ps.bitcast(BF16)

    CC = sb("CC", [128, 8, S], BF16, 16 * KB)
    TC = sb("TC", [128, S], F32, 48 * KB)
    TS = sb("TS", [128, S], F32, 64 * KB)
    c0 = 80 * KB
    identf = sb("identf", [128, 128], F32, c0)
    identb = sb("identb", [128, 128], BF16, c0 + 512)
    mL = sb("mL", [128, 128], BF16, c0 + 768)
    mU = sb("mU", [128, 128], BF16, c0 + 1024)
    ropec = sb("ropec", [128, 2], F32, c0 + 1280)
    NLS = 8
    lnst = sb("lnst", [128, NLS, 2, 6], F32, c0 + 1312)
    lnmv = sb("lnmv", [128, NLS, 4], F32, c0 + 1696)

    P.op("sp", lambda e: e.dma_start(out=identf[:], in_=ident_d), writes=["identf"], dma=True, lane=("c", 0))
    P.op("pool", lambda e: e.dma_start(out=identb[:], in_=ident_d), writes=["identb"], dma=True, lane=("c", 1))
    P.op("pool", lambda e: e.dma_start(out=mL[:], in_=mL_d), writes=["mL"], dma=True, lane=("c", 2))
    P.op("pool", lambda e: e.dma_start(out=mU[:], in_=mU_d), writes=["mU"], dma=True, lane=("c", 3))
    P.op("sp", lambda e: e.dma_start(out=ropec[:], in_=ropec_d), writes=["ropec"], dma=True, lane=("c", 4))

    e0 = 146 * KB
    HT = 1024
    pos_i = sb("pos_i", [128, HT], I32, e0)
    pos_f = sb("pos_f", [128, HT], F32, e0 + 4 * KB)
    ang = sb("ang", [128, HT], F32, e0 + 8 * KB)
    a2 = sb("a2", [128, HT], F32, e0 + 12 * KB)
    ki = sb("ki", [128, HT], I32, e0 + 16 * KB)
    rr = sb("rr", [128, HT], F32, e0 + 20 * KB)
    tt_ = sb("tt_", [128, HT], F32, e0 + 24 * KB)
    C1 = 6.28125
    C2 = TWO_PI - C1
    for h in range(4):
        cs = slice(h * HT, (h + 1) * HT)
        P.op("sp", lambda e, cs=cs: e.dma_start(out=pos_i[:], in_=pos_d[0, cs].partition_broadcast(128)),
             writes=["pos_i"], dma=True, lane=("c", 5))
        P.op("dve", lambda e: e.tensor_copy(out=pos_f[:], in_=pos_i[:]), reads=["pos_i"], writes=["pos_f"])
        P.op("dve", lambda e: e.tensor_scalar(out=ang[:], in0=pos_f[:], scalar1=ropec[:, 0:1], scalar2=None, op0=ALU.mult),
             reads=["pos_f", "ropec"], writes=["ang"])
        for tab, shift in ((TS, 0.0), (TC, PI / 2)):
            if shift != 0.0:
                P.op("dve", lambda e, shift=shift: e.tensor_scalar(out=a2[:], in0=ang[:], scalar1=shift, scalar2=None, op0=ALU.add),
                     reads=["ang"], writes=["a2"])
                src = a2
                srck = "a2"
            else:
                src = ang
                srck = "ang"
            P.op("dve", lambda e, src=src: e.tensor_scalar(out=ki[:], in0=src[:], scalar1=1.0 / TWO_PI, scalar2=None, op0=ALU.mult),
                 reads=[srck], writes=["ki"])
            P.op("dve", lambda e, src=src: e.scalar_tensor_tensor(out=rr[:], in0=ki[:], scalar=-C1, in1=src[:], op0=ALU.mult, op1=ALU.add),
                 reads=["ki", srck], writes=["rr"])
            P.op("dve", lambda e: e.scalar_tensor_tensor(out=rr[:], in0=ki[:], scalar=-C2, in1=rr[:], op0=ALU.mult, op1=ALU.add),
                 reads=["ki", "rr"], writes=["rr"])
            P.op("dve", lambda e: e.tensor_scalar(out=tt_[:], in0=rr[:], scalar1=PI, scalar2=TWO_PI, op0=ALU.is_gt, op1=ALU.mult),
                 reads=["rr"], writes=["tt_"])
            P.op("dve", lambda e: e.tensor_tensor(out=rr[:], in0=rr[:], in1=tt_[:], op=ALU.subtract), reads=["rr", "tt_"], writes=["rr"])
            P.op("dve", lambda e: e.tensor_scalar(out=tt_[:], in0=rr[:], scalar1=-PI, scalar2=TWO_PI, op0=ALU.is_lt, op1=ALU.mult),
                 reads=["rr"], writes=["tt_"])
            P.op("dve", lambda e: e.tensor_tensor(out=rr[:], in0=rr[:], in1=tt_[:], op=ALU.add), reads=["rr", "tt_"], writes=["rr"])
            P.op("dve", lambda e: e.tensor_scalar(out=rr[:], in0=rr[:], scalar1=PI, scalar2=-PI, op0=ALU.min, op1=ALU.max),
                 reads=["rr"], writes=["rr"])
            if shift == 0.0:
                P.op("act", lambda e, tab=tab, cs=cs: e.activation(out=tab[:, cs], in_=rr[:], func=AF.Sin, scale=ropec[:, 1:2]),
                     reads=["rr", "ropec"], writes=[("tab", "S", h)])
            else:
                P.op("act", lambda e, tab=tab, cs=cs: e.activation(out=tab[:, cs], in_=rr[:], func=AF.Sin),
                     reads=["rr"], writes=[("tab", "C", h)])

    P.fence()

    XT = sb("XT", [128, 8, S], BF16, 82 * KB)
    Qt = sb("Qt", [128, S], BF16, 146 * KB)
    Kt = sb("Kt", [128, S], BF16, 154 * KB)
    Vt = sb("Vt", [128, S], BF16, 162 * KB)
    wbuf = [sb("wbuf%d" % i, [128, 8, 128], BF16, 170 * KB + 2 * KB * i) for i in range(4)]
    t1 = [sb("t1_%d" % i, [128, 512], F32, 178 * KB + 4 * KB * i) for i in range(2)]
    t2 = [sb("t2_%d" % i, [128, 512], F32, 180 * KB + 4 * KB * i) for i in range(2)]
    xs = [sb("xs%d" % i, [128, D], BF16, 186 * KB + 2 * KB * i) for i in range(2)]
    NPT = 6
    PT = [sb("PT%d" % i, [128, 512], BF16, 190 * KB + KB * i) for i in range(NPT)]
    VB1 = sb("VB1", [128, 8, 3, 64], BF16, 196 * KB)
    VB4 = sb("VB4", [128, 8, 3, 64], BF16, 199 * KB)
    VB16 = sb("VB16", [128, 32, 3, 64], BF16, 202 * KB)
    rc = [sb("rc%d" % i, [128, 512], F32, 214 * KB + 2 * KB * i) for i in range(2)]

    for nm_, t in (("VB1", VB1), ("VB4", VB4), ("VB16", VB16)):
        P.op("pool", lambda e, t=t: e.memset(t[:, :, 1, :], 1.0), writes=[("vbones", nm_)])

    for i in range(32):
        j = i % 2
        P.op("pool", lambda e, i=i, j=j: e.dma_start(out=xs[j][:], in_=x_d[128 * i:128 * (i + 1), :]),
             writes=[("xs", j)], dma=True, lane=("xs", j))
        bk = 4 + j
        for k in range(8):
            P.op("pe", lambda e, j=j, k=k, bk=bk: e.transpose(out=psb[:, bk, k * 128:(k + 1) * 128],
                                                              in_=xs[j][:, k * 128:(k + 1) * 128], identity=identb[:]),
                 reads=[("xs", j), "identb"], writes=[("ps", bk)])
        eng = "act" if i % 2 == 0 else "dve"
        src = psb[:, bk, :].rearrange("p (k c) -> p k c", c=128)
        dst = XT[:, :, 128 * i:128 * (i + 1)]
        if eng == "act":
            P.op("act", lambda e, src=src, dst=dst: e.copy(out=dst, in_=src), reads=[("ps", bk)], writes=[("XT", i)])
        else:
            P.op("dve", lambda e, src=src, dst=dst: e.tensor_copy(out=dst, in_=src), reads=[("ps", bk)], writes=[("XT", i)])

    wcount = [0]

    def load_w(src_ap):
        s_ = wcount[0] % 4
        wcount[0] += 1
        P.op("pool", lambda e, s_=s_, src_ap=src_ap: e.dma_start(out=wbuf[s_][:], in_=src_ap.rearrange("(kc p) c -> p kc c", p=128)),
             writes=[("wb", s_)], dma=True, lane=("wb", s_))
        return s_

    def proj_mm(bank, slot, T8):
        for kc in range(8):
            P.op("pe", lambda e, bank=bank, slot=slot, kc=kc, T8=T8: e.matmul(
                ps[:, bank, :], lhsT=wbuf[slot][:, kc, :], rhs=XT[:, kc, 512 * T8:512 * (T8 + 1)],
                start=(kc == 0), stop=(kc == 7)),
                reads=[("wb", slot)] + [("XT", 4 * T8 + q) for q in range(4)], writes=[("ps", bank)])

    pt_ctr = [0]
    grp_ctr = [0]
    acc_ctr = [0]
    tr_ctr = [0]
    msk_ctr = [0]

    def build_vb(vbt, nm_, idx0, tok_ap_fns):
        cnt = len(tok_ap_fns)
        bk = tr_ctr[0] % 2
        tr_ctr[0] += 1
        for q, fn in enumerate(tok_ap_fns):
            P.op("pe", lambda e, q=q, fn=fn: e.transpose(out=psb[:, bk, q * 128:(q + 1) * 128], in_=fn(), identity=identb[:]),
                 reads=["Vall", "identb"], writes=[("ps", bk)])
        src = psb[:, bk, 0:cnt * 128].rearrange("p (q a b) -> p q a b", a=2, b=64)
        P.op("dve", lambda e: e.tensor_copy(out=vbt[:, idx0:idx0 + cnt, 0:3:2, :], in_=src), reads=[("ps", bk)],
             writes=[("vb", nm_, idx0 + q) for q in range(cnt)])

    for hp in range(4):
        for (dest, dkey, c_main, c_sw) in ((Qt, "Q", hp * 128, hp * 128), (Kt, "K", 512 + hp * 128, 512 + hp * 128)):
            sa = load_w(w_in_d[:, c_main:c_main + 128])
            sbw = load_w(w_sw_d[:, c_sw:c_sw + 128])
            for T8 in range(8):
                st_ = T8 % 2
                ba, bb = 2 * st_, 2 * st_ + 1
                proj_mm(ba, sa, T8)
                proj_mm(bb, sbw, T8)
                cs = slice(512 * T8, 512 * (T8 + 1))
                P.op("dve", lambda e, st_=st_, ba=ba, cs=cs: e.tensor_tensor(out=t1[st_][:], in0=ps[:, ba, :], in1=TC[:, cs], op=ALU.mult),
                     reads=[("ps", ba)] + [("tab", "C", h_) for h_ in range(4)], writes=[("t1", st_)])
                P.op("dve", lambda e, st_=st_, bb=bb, cs=cs: e.tensor_tensor(out=t2[st_][:], in0=ps[:, bb, :], in1=TS[:, cs], op=ALU.mult),
                     reads=[("ps", bb)] + [("tab", "S", h_) for h_ in range(4)], writes=[("t2", st_)])
                P.op("pool", lambda e, st_=st_, dest=dest, cs=cs: e.tensor_tensor(out=dest[:, cs], in0=t1[st_][:], in1=t2[st_][:], op=ALU.add),
                     reads=[("t1", st_), ("t2", st_)], writes=[(dkey, T8), dkey + "all"])
        sv = load_w(w_in_d[:, 1024 + hp * 128:1024 + (hp + 1) * 128])
        for T8 in range(8):
            bk = 2 * (T8 % 2)
            proj_mm(bk, sv, T8)
            cs = slice(512 * T8, 512 * (T8 + 1))
            P.op("act", lambda e, bk=bk, cs=cs: e.copy(out=Vt[:, cs], in_=ps[:, bk, :]), reads=[("ps", bk)], writes=["Vall"])

        for m in range(21 * hp, 21 * (hp + 1)):
            src_ap, view3 = chunks84[m]
            if view3:
                P.op("pool", lambda e, m=m, src_ap=src_ap: e.dma_start(out=tape[m].rearrange("p (kc c) -> p kc c", c=128),
                                                                   in_=src_ap.rearrange("(kc p) c -> p kc c", p=128)),
                     writes=[("tape", m)], dma=True, lane=("tape", m % 4))
            else:
                P.op("pool", lambda e, m=m, src_ap=src_ap: e.dma_start(out=tape[m], in_=src_ap),
                     writes=[("tape", m)], dma=True, lane=("tape", m % 4))

        for n16 in range(2):
            for r8 in range(2):
                build_vb(VB16, "VB16", n16 * 16 + r8 * 8,
                         [(lambda n16=n16, r16=r8 * 8 + q: Vt[:, 2048 * n16 + r16:2048 * (n16 + 1):16]) for q in range(8)])

        G = []
        for W in range(8):
            n16 = W // 4
            Wl = W % 4
            for hh in range(2):
                po = 64 * hh
                accb = 6 + acc_ctr[0] % 2
                acc_ctr[0] += 1
                ri = acc_ctr[0] % 2
                groups = []
                blk = []
                for b in range(4):
                    n = 4 * W + b
                    sl = slice(128 * n, 128 * (n + 1))
                    blk.append((sl, sl, slice(128 * b, 128 * (b + 1)), (VB1, 'VB1', n % 8), b))
                groups.append((blk, 128, "L"))
                blk = []
                for b in range(4):
                    n = 4 * W + b
                    if n == 0:
                        continue
                    blk.append((slice(128 * (n - 1), 128 * n), slice(128 * n, 128 * (n + 1)), slice(128 * b, 128 * (b + 1)),
                                (VB1, 'VB1', (n - 1) % 8), b))
                groups.append((blk, 128, "U"))
                blk = []
                for r4 in range(4):
                    sl = slice(512 * W + r4, 512 * (W + 1), 4)
                    blk.append((sl, sl, slice(r4, 512, 4), (VB4, 'VB4', (W % 2) * 4 + r4), r4))
                groups.append((blk, 128, "L"))
                if W > 0:
                    blk = []
                    for r4 in range(4):
                        blk.append((slice(512 * (W - 1) + r4, 512 * W, 4), slice(512 * W + r4, 512 * (W + 1), 4),
                                    slice(r4, 512, 4), (VB4, 'VB4', ((W - 1) % 2) * 4 + r4), r4))
                    groups.append((blk, 128, "U"))
                blk = []
                for r16 in range(16):
                    ksl = slice(2048 * n16 + r16, 2048 * (n16 + 1), 16)
                    qsl = slice(512 * W + r16, 512 * (W + 1), 16)
                    blk.append((ksl, qsl, slice(r16, 512, 16), (VB16, 'VB16', n16 * 16 + r16), r16))
                groups.append((blk, 32, "L16"))
                if n16 == 1:
                    blk = []
                    for r16 in range(16):
                        ksl = slice(r16, 2048, 16)
                        qsl = slice(512 * W + r16, 512 * (W + 1), 16)
                        blk.append((ksl, qsl, slice(r16, 512, 16), (VB16, 'VB16', r16), r16))
                    groups.append((blk, 32, "U16"))
                ngr = len(groups)
                for gi, (blk, N, mtype) in enumerate(groups):
                    G.append(dict(blk=blk, N=N, mtype=mtype, W=W, hh=hh, po=po, Wl=Wl, accb=accb, ri=ri,
                                  first=(gi == 0), last=(gi == ngr - 1), newwin=(gi == 0 and hh == 0)))

        def emit_qk(g):
            W = g["W"]
            if g["newwin"]:
                build_vb(VB1, "VB1", (4 * W) % 8, [(lambda n=4 * W + b: Vt[:, 128 * n:128 * (n + 1)]) for b in range(4)])
                build_vb(VB4, "VB4", (W % 2) * 4, [(lambda W=W, r4=r4: Vt[:, 512 * W + r4:512 * (W + 1):4]) for r4 in range(4)])
            blk, N, mtype, po, Wl = g["blk"], g["N"], g["mtype"], g["po"], g["Wl"]
            sbk = 4 + grp_ctr[0] % 2
            grp_ctr[0] += 1
            pt = pt_ctr[0] % NPT
            pt_ctr[0] += 1
            g["pt"] = pt
            cols_lo = blk[0][4] * N
            cols_hi = (blk[-1][4] + 1) * N
            for (ksl, qsl, acols, vbt, pos) in blk:
                P.op("pe", lambda e, sbk=sbk, pos=pos, N=N, ksl=ksl, qsl=qsl, po=po: e.matmul(
                    ps[:, sbk, pos * N:(pos + 1) * N], lhsT=Kt[po:po + 64, ksl], rhs=Qt[po:po + 64, qsl],
                    start=True, stop=True),
                    reads=["Kall", "Qall"], writes=[("ps", sbk)])
            P.op("act", lambda e, sbk=sbk, pt=pt, lo=cols_lo, hi=cols_hi: e.activation(
                out=PT[pt][:, lo:hi], in_=ps[:, sbk, lo:hi], func=AF.Exp, scale=0.125),
                reads=[("ps", sbk)], writes=[("PT", pt)])
            nb = (cols_hi - cols_lo) // N
            if mtype == "L":
                mfn = lambda nb=nb: mL[:, :].unsqueeze(1).broadcast_to([128, nb, 128])
            elif mtype == "U":
                mfn = lambda nb=nb: mU[:, :].unsqueeze(1).broadcast_to([128, nb, 128])
            elif mtype == "L16":
                mfn = lambda Wl=Wl: mL[:, 32 * Wl:32 * (Wl + 1)].unsqueeze(1).broadcast_to([128, 16, 32])
            else:
                mfn = lambda Wl=Wl: mU[:, 32 * Wl:32 * (Wl + 1)].unsqueeze(1).broadcast_to([128, 16, 32])
            meng = "pool" if msk_ctr[0] % 2 == 0 else "dve"
            msk_ctr[0] += 1
            P.op(meng, lambda e, pt=pt, lo=cols_lo, hi=cols_hi, N=N, mfn=mfn: e.tensor_tensor(
                out=PT[pt][:, lo:hi].rearrange("p (a b) -> p a b", b=N),
                in0=PT[pt][:, lo:hi].rearrange("p (a b) -> p a b", b=N), in1=mfn(), op=ALU.mult),
                reads=[("PT", pt), "mL", "mU"], writes=[("PT", pt)])

        def emit_pv(g, hp=hp):
            blk, N, po, accb, ri, hh, W, pt = g["blk"], g["N"], g["po"], g["accb"], g["ri"], g["hh"], g["W"], g["pt"]
            for bi, (ksl, qsl, acols, vbt, pos) in enumerate(blk):
                first = g["first"] and bi == 0
                last = g["last"] and (bi == len(blk) - 1)
                P.op("pe", lambda e, accb=accb, acols=acols, vbt=vbt, hh=hh, pt=pt, pos=pos, N=N, first=first, last=last: e.matmul(
                    ps[:, accb, acols], lhsT=vbt[0][:, vbt[2], hh:hh + 2, :].rearrange("p a b -> p (a b)"),
                    rhs=PT[pt][:, pos * N:(pos + 1) * N], start=first, stop=last, skip_group_check=True),
                    reads=[("PT", pt), ("vb", vbt[1], vbt[2]), ("vbones", vbt[1])], writes=[("ps", accb)])
            if g["last"]:
                dlo = 64 - po
                P.op("act", lambda e, ri=ri, accb=accb, dlo=dlo: e.activation(out=rc[ri][dlo:dlo + 64, :], in_=ps[dlo:dlo + 64, accb, :], func=AF.Ln),
                     reads=[("ps", accb)], writes=[("rc", ri)])
                P.op("act", lambda e, ri=ri, dlo=dlo: e.activation(out=rc[ri][dlo:dlo + 64, :], in_=rc[ri][dlo:dlo + 64, :], func=AF.Exp, scale=-1.0),
                     reads=[("rc", ri)], writes=[("rc", ri)])
                P.op("dve", lambda e, ri=ri, accb=accb, dlo=dlo, po=po, hp=hp, W=W: e.tensor_tensor(
                    out=CC[po:po + 64, hp, 512 * W:512 * (W + 1)], in0=ps[po:po + 64, accb, :], in1=rc[ri][dlo:dlo + 64, :], op=ALU.mult),
                    reads=[("ps", accb), ("rc", ri)], writes=[("CCa", hp, W, hh)])

        LOOK = 2
        for q in range(min(LOOK, len(G))):
            emit_qk(G[q])
        for q in range(len(G)):
            emit_pv(G[q])
            if q + LOOK < len(G):
                emit_qk(G[q + LOOK])
    P.fence()

    g0 = 146 * KB
    wz = sb("wz", [128, 8, 512], BF16, g0)
    wu = [sb("wu%d" % c, [128, 8, 128], BF16, g0 + 8 * KB + 2 * KB * c) for c in range(4)]
    wsf = sb("wsf", [128, 8, 128], F32, g0 + 16 * KB)
    wsm = sb("wsm", [128, 8, 128], BF16, g0 + 20 * KB)
    bsb = sb("bsb", [128, 4, 128], F32, g0 + 22 * KB)
    lzg = sb("lzg", [128, 512], F32, g0 + 24 * KB)
    lzb = sb("lzb", [128, 512], F32, g0 + 26 * KB)
    ug = [sb("ug%d" % i, [128, 4, 512], F32, g0 + 28 * KB + 8 * KB * i) for i in range(2)]
    zg = [sb("zg%d" % i, [128, 512], F32, g0 + 54 * KB + 2 * KB * i) for i in range(4)]
    zn = [sb("zn%d" % i, [128, 512], BF16, g0 + 62 * KB + KB * i) for i in range(4)]
    mx = [sb("mx%d" % i, [128, 4, 128], F32, g0 + 50 * KB + 2 * KB * i) for i in range(2)]

    P.op("pool", lambda e: e.dma_start(out=wz[:], in_=w_in_d[:, 2048:2560].rearrange("(kc p) c -> p kc c", p=128)),
         writes=["wz"], dma=True, lane=("g", 0))
    for c in range(4):
        P.op("pool", lambda e, c=c: e.dma_start(out=wu[c][:], in_=w_in_d[:, 1536 + 128 * c:1536 + 128 * (c + 1)].rearrange("(kc p) c -> p kc c", p=128)),
             writes=[("wu", c)], dma=True, lane=("g", 1 + c))
    P.op("sp", lambda e: e.dma_start(out=wsf[:], in_=wsT_d.rearrange("g j i -> j g i")), writes=["wsf"], dma=True, lane=("g", 5))
    for gp in range(4):
        for hf in range(2):
            P.op("sp", lambda e, gp=gp, hf=hf: e.dma_start(out=bsb[64 * hf:64 * (hf + 1), gp, :], in_=bs_d[2 * gp + hf, :].partition_broadcast(64)),
                 writes=[("bsb", gp, hf)], dma=True, lane=("g", 6 + gp * 2 + hf))
    P.op("sp", lambda e: e.dma_start(out=lzg[:], in_=lzg_d[0, :].partition_broadcast(128)), writes=["lzg"], dma=True, lane=("g", 14))
    P.op("sp", lambda e: e.dma_start(out=lzb[:], in_=lzb_d[0, :].partition_broadcast(128)), writes=["lzb"], dma=True, lane=("g", 15))
    P.op("dve", lambda e: e.tensor_tensor(out=wsm[:], in0=wsf[:], in1=mL[:, :].unsqueeze(1).broadcast_to([128, 8, 128]), op=ALU.mult),
         reads=["wsf", "mL"], writes=["wsm"])

    def layernorm_stats(src_fn, nchunks, slot, key_in):
        for c in range(nchunks):
            P.op("dve", lambda e, c=c: e.bn_stats(out=lnst[:, slot, c, :], in_=src_fn(c)), reads=[key_in], writes=[("lnst", slot, c)])
        P.op("dve", lambda e: e.bn_aggr(out=lnmv[:, slot, 0:2], in_=lnst[:, slot, 0:nchunks, :]),
             reads=[("lnst", slot, c) for c in range(nchunks)], writes=[("mv", slot)])
        P.op("act", lambda e: e.activation(out=lnmv[:, slot, 2:3], in_=lnmv[:, slot, 1:2], func=AF.Sqrt, bias=epsb[:, 0:1]),
             reads=[("mv", slot), "epsb"], writes=[("sd", slot)])
        P.op("dve", lambda e: e.reciprocal(out=lnmv[:, slot, 2:3], in_=lnmv[:, slot, 2:3]), reads=[("sd", slot)], writes=[("sd", slot)])
        P.op("dve", lambda e: e.tensor_scalar(out=lnmv[:, slot, 3:4], in0=lnmv[:, slot, 0:1], scalar1=lnmv[:, slot, 2:3], scalar2=-1.0,
                                              op0=ALU.mult, op1=ALU.mult),
             reads=[("mv", slot), ("sd", slot)], writes=[("nmr", slot)])

    epsb = sb("epsb", [128, 1], F32, c0 + 1856)
    P.op("dve", lambda e: e.memset(epsb[:], EPS), writes=["epsb"])

    ln_ctr = [0]

    def g_uproj(T8):
        ub = T8 % 2
        for c in range(4):
            bk = c % 2
            for kc in range(8):
                P.op("pe", lambda e, bk=bk, c=c, kc=kc, T8=T8: e.matmul(ps[:, bk, :], lhsT=wu[c][:, kc, :], rhs=XT[:, kc, 512 * T8:512 * (T8 + 1)],
                                                                    start=(kc == 0), stop=(kc == 7)),
                     reads=[("wu", c)], writes=[("ps", bk)])
            P.op("act", lambda e, bk=bk, c=c, ub=ub: e.activation(out=ug[ub][:, c, :], in_=ps[:, bk, :], func=AF.Gelu),
                 reads=[("ps", bk)], writes=[("ug", ub, c)])

    def g_zproj(i):
        zb = i % 4
        bk = 2 + i % 2
        for kc in range(8):
            P.op("pe", lambda e, bk=bk, kc=kc, i=i: e.matmul(ps[:, bk, :], lhsT=XT[:, kc, 128 * i:128 * (i + 1)], rhs=wz[:, kc, :],
                                                             start=(kc == 0), stop=(kc == 7)),
                 reads=["wz"], writes=[("ps", bk)])
        P.op("act", lambda e, bk=bk, zb=zb: e.activation(out=zg[zb][:], in_=ps[:, bk, :], func=AF.Gelu),
             reads=[("ps", bk)], writes=[("zg", zb)])
        slot = ln_ctr[0] % NLS
        ln_ctr[0] += 1
        layernorm_stats(lambda c, zb=zb: zg[zb][:], 1, slot, ("zg", zb))
        P.op("act", lambda e, zb=zb, slot=slot: e.activation(out=zg[zb][:], in_=zg[zb][:], func=AF.Identity,
                                                             scale=lnmv[:, slot, 2:3], bias=lnmv[:, slot, 3:4]),
             reads=[("zg", zb), ("sd", slot), ("nmr", slot)], writes=[("zg", zb)])
        P.op("pool", lambda e, zb=zb: e.tensor_tensor(out=zg[zb][:], in0=zg[zb][:], in1=lzg[:], op=ALU.mult),
             reads=[("zg", zb), "lzg"], writes=[("zg", zb)])
        P.op("pool", lambda e, zb=zb: e.tensor_tensor(out=zn[zb][:], in0=zg[zb][:], in1=lzb[:], op=ALU.add),
             reads=[("zg", zb), "lzb"], writes=[("zn", zb)])

    def g_spatial(i):
        zb = i % 4
        mb = i % 2
        T8, tt = i // 4, i % 4
        ub = T8 % 2
        sbk = 4 + i % 2
        for gp in range(4):
            for hf in range(2):
                g = 2 * gp + hf
                P.op("pe", lambda e, sbk=sbk, gp=gp, hf=hf, g=g, zb=zb: e.matmul(
                    ps[64 * hf:64 * (hf + 1), sbk, 128 * gp:128 * (gp + 1)], lhsT=zn[zb][:, 64 * g:64 * (g + 1)], rhs=wsm[:, g, :],
                    start=True, stop=True),
                    reads=[("zn", zb), "wsm"], writes=[("ps", sbk)])
        P.op("dve", lambda e, sbk=sbk, mb=mb: e.tensor_tensor(out=mx[mb][:], in0=ps[:, sbk, :].rearrange("p (a b) -> p a b", b=128),
                                                           in1=bsb[:], op=ALU.add),
             reads=[("ps", sbk)] + [("bsb", gp, hf) for gp in range(4) for hf in range(2)], writes=[("mx", mb)])
        P.op("pool", lambda e, mb=mb, ub=ub, tt=tt, i=i: e.tensor_tensor(out=CC[:, 4:8, 128 * i:128 * (i + 1)], in0=mx[mb][:],
                                                                      in1=ug[ub][:, :, 128 * tt:128 * (tt + 1)], op=ALU.mult),
             reads=[("mx", mb)] + [("ug", ub, c) for c in range(4)], writes=[("CCg", i)])

    GL = 2
    for i in range(32 + GL):
        if i < 32:
            if i % 4 == 0:
                g_uproj(i // 4)
            g_zproj(i)
        if i >= GL:
            g_spatial(i - GL)

    if stop_after == "A":
        P.fence()
        o = P.op("sp", lambda e: e.dma_start(out=out_d, in_=CC[:].rearrange("p a b -> p (a b)")), writes=["out"], dma=True, lane=("o", 0))
        P.final_waits = [o.idx]
        P.emit()
        return nc
    P.fence()

    t0 = 82 * KB
    lnp = {}
    for qi, nm in enumerate(("ln1_g", "ln1_b", "ln2_g", "ln2_b", "ln3_g", "ln3_b", "b_ple_gate")):
        lnp[nm] = sb("lnp_" + nm, [128, D], F32, t0 + 4 * KB * qi)
        P.op("sp", lambda e, nm=nm: e.dma_start(out=lnp[nm][:], in_=ln_d[nm][0, :].partition_broadcast(128)),
             writes=[("lnp", nm)], dma=True, lane=("lnp", qi))
    x1s = [sb("x1_%d" % i, [128, 4, D], F32, 110 * KB + 16 * KB * i) for i in range(2)]
    gT = sb("gT", [128, NF, 512], BF16, 142 * KB)
    pT = sb("pT", [128, 2, 512], BF16, 164 * KB)
    NS = 6
    ringS = [sb("ringS%d" % i, [128, D], BF16, 166 * KB + 2 * KB * i) for i in range(NS)]
    grp = [sb("grp%d" % i, [128, D], BF16, 178 * KB + 2 * KB * i) for i in range(10)]
    xr = [sb("xr0", [128, D], F32, 198 * KB), sb("xr1", [128, D], F32, 217 * KB)]
    pbt = [sb("pbt%d" % i, [128, DPLE], BF16, 202 * KB + 512 * i) for i in range(4)]
    abuf = [sb("abuf%d" % i, [128, 520], F32, 204 * KB + 2080 * i) for i in range(2)]
    cacc = [sb("cacc%d" % i, [128, 512], F32, 209 * KB + 2 * KB * i) for i in range(2)]
    gtmp = [sb("gtmp0", [128, D], F32, 213 * KB)]
    cw = sb("cw", [128, 3 * NF], F32, 221 * KB)
    cb = sb("cb", [128, NF], F32, 221 * KB + 288)
    halo = sb("halo", [128, NF, 2], F32, 221 * KB + 384)
    lnT = sb("lnT", [128, 32], F32, 221 * KB + 576)

    P.op("sp", lambda e: e.dma_start(out=cw[:], in_=cw_d), writes=["cw"], dma=True, lane=("t", 0))
    P.op("sp", lambda e: e.dma_start(out=cb[:], in_=cb_d), writes=["cb"], dma=True, lane=("t", 1))
    P.op("sp", lambda e: e.dma_start(out=lnT[:], in_=lnT_d), writes=["lnT"], dma=True, lane=("t", 2))
    P.op("dve", lambda e: e.memset(halo[:], 0.0), writes=["halo"])

    seqS = []
    for Tt_ in range(8):
        for f in range(NF):
            seqS += [8 + 2 * f, 8 + 2 * f + 1]
        for f in range(NF):
            seqS.append(52 + f)
    PREF = 4
    emitted = [0]
    usectr = [0]

    def ring_next():
        n = usectr[0]
        usectr[0] += 1
        while emitted[0] <= min(n + PREF, len(seqS) - 1):
            m = emitted[0]
            emitted[0] += 1
            s_ = m % NS
            P.op("sp", lambda e, s_=s_, m=m: e.dma_start(out=ringS[s_][:], in_=tape[seqS[m]]),
                 writes=[("ringS", s_)], dma=True, lane=("ringS", s_))
        return n % NS

    def grp_load(first, count, tape0):
        for q in range(count):
            P.op("sp", lambda e, q=q: e.dma_start(out=grp[first + q][:], in_=tape[tape0 + q]),
                 writes=[("grp", first + q)], dma=True, lane=("grp", first + q))

    def ln_norm(x1, par, tt):
        slot = ln_ctr[0] % NLS
        ln_ctr[0] += 1
        layernorm_stats(lambda c: x1[:, tt, 512 * c:512 * (c + 1)], 2, slot, ("x1", par, tt))
        P.op("act", lambda e: e.activation(out=x1[:, tt, :], in_=x1[:, tt, :], func=AF.Identity,
                                           scale=lnmv[:, slot, 2:3], bias=lnmv[:, slot, 3:4]),
             reads=[("x1", par, tt), ("sd", slot), ("nmr", slot)], writes=[("x1", par, tt)])

    def ln_affine(x1, par, tt, gname, bname):
        P.op("pool", lambda e: e.tensor_tensor(out=x1[:, tt, :], in0=x1[:, tt, :], in1=lnp[gname][:], op=ALU.mult),
             reads=[("x1", par, tt), ("lnp", gname)], writes=[("x1", par, tt)])
        P.op("pool", lambda e: e.tensor_tensor(out=x1[:, tt, :], in0=x1[:, tt, :], in1=lnp[bname][:], op=ALU.add),
             reads=[("x1", par, tt), ("lnp", bname)], writes=[("x1", par, tt)])

    def ln_stage(x1, par, tt, stage, st):
        if stage == 0:
            st["slot"] = ln_ctr[0] % NLS
            ln_ctr[0] += 1
            slot = st["slot"]
            for c in range(2):
                P.op("dve", lambda e, c=c: e.bn_stats(out=lnst[:, slot, c, :], in_=x1[:, tt, 512 * c:512 * (c + 1)]),
                     reads=[("x1", par, tt)], writes=[("lnst", slot, c)])
            P.op("dve", lambda e: e.bn_aggr(out=lnmv[:, slot, 0:2], in_=lnst[:, slot, 0:2, :]),
                 reads=[("lnst", slot, 0), ("lnst", slot, 1)], writes=[("mv", slot)])
        elif stage == 1:
            slot = st["slot"]
            P.op("act", lambda e: e.activation(out=lnmv[:, slot, 2:3], in_=lnmv[:, slot, 1:2], func=AF.Sqrt, bias=epsb[:, 0:1]),
                 reads=[("mv", slot), "epsb"], writes=[("sd", slot)])
        elif stage == 2:
            slot = st["slot"]
            P.op("dve", lambda e: e.reciprocal(out=lnmv[:, slot, 2:3], in_=lnmv[:, slot, 2:3]), reads=[("sd", slot)], writes=[("sd", slot)])
            P.op("dve", lambda e: e.tensor_scalar(out=lnmv[:, slot, 3:4], in0=lnmv[:, slot, 0:1], scalar1=lnmv[:, slot, 2:3], scalar2=-1.0,
                                                  op0=ALU.mult, op1=ALU.mult),
                 reads=[("mv", slot), ("sd", slot)], writes=[("nmr", slot)])
        elif stage == 3:
            slot = st["slot"]
            P.op("act", lambda e: e.activation(out=x1[:, tt, :], in_=x1[:, tt, :], func=AF.Identity,
                                               scale=lnmv[:, slot, 2:3], bias=lnmv[:, slot, 3:4]),
                 reads=[("x1", par, tt), ("sd", slot), ("nmr", slot)], writes=[("x1", par, tt)])

    evc = [0]

    def cc_transposes(x1, par, tt, b0):
        for k in range(8):
            P.op("pe", lambda e, k=k: e.transpose(out=ps[:, b0 + k // 4, (k % 4) * 128:(k % 4 + 1) * 128],
                                                  in_=x1[:, tt, 128 * k:128 * (k + 1)], identity=identf[:]),
                 reads=[("x1", par, tt), "identf"], writes=[("ps", b0 + k // 4)])

    def cc_evac(i, b0, lnbase):
        for k in range(8):
            src = ps[:, b0 + k // 4, (k % 4) * 128:(k % 4 + 1) * 128]
            dst = CC[:, k, 128 * i:128 * (i + 1)]
            gcol = lnT[:, lnbase + k:lnbase + k + 1]
            bcol = lnT[:, lnbase + 8 + k:lnbase + 8 + k + 1]
            if (evc[0] + k // 4) % 2 == 0:
                P.op("act", lambda e, src=src, dst=dst, gcol=gcol, bcol=bcol: e.activation(out=dst, in_=src, func=AF.Identity, scale=gcol, bias=bcol),
                     reads=[("ps", b0 + k // 4), "lnT"], writes=[("CT", i, k)])
            else:
                P.op("dve", lambda e, src=src, dst=dst, gcol=gcol, bcol=bcol: e.tensor_scalar(out=dst, in0=src, scalar1=gcol, scalar2=bcol,
                                                                                          op0=ALU.mult, op1=ALU.add),
                     reads=[("ps", b0 + k // 4), "lnT"], writes=[("CT", i, k)])
        evc[0] += 1

    def transpose_to_cc(x1, par, tt, i, b0, lnbase):
        for k in range(8):
            P.op("pe", lambda e, k=k: e.transpose(out=ps[:, b0 + k // 4, (k % 4) * 128:(k % 4 + 1) * 128],
                                                  in_=x1[:, tt, 128 * k:128 * (k + 1)], identity=identf[:]),
                 reads=[("x1", par, tt), "identf"], writes=[("ps", b0 + k // 4)])
        for k in range(8):
            src = ps[:, b0 + k // 4, (k % 4) * 128:(k % 4 + 1) * 128]
            dst = CC[:, k, 128 * i:128 * (i + 1)]
            gcol = lnT[:, lnbase + k:lnbase + k + 1]
            bcol = lnT[:, lnbase + 8 + k:lnbase + 8 + k + 1]
            if (evc[0] + k // 4) % 2 == 0:
                P.op("act", lambda e, src=src, dst=dst, gcol=gcol, bcol=bcol: e.activation(out=dst, in_=src, func=AF.Identity, scale=gcol, bias=bcol),
                     reads=[("ps", b0 + k // 4), "lnT"], writes=[("CT", i, k)])
            else:
                P.op("dve", lambda e, src=src, dst=dst, gcol=gcol, bcol=bcol: e.tensor_scalar(out=dst, in0=src, scalar1=gcol, scalar2=bcol,
                                                                                          op0=ALU.mult, op1=ALU.add),
                     reads=[("ps", b0 + k // 4), "lnT"], writes=[("CT", i, k)])
        evc[0] += 1

    def p_loads(Tt):
        for tt in range(4):
            i = 4 * Tt + tt
            P.op("pool", lambda e, i=i, tt=tt: e.dma_start(out=pbt[tt][:], in_=p_d[128 * i:128 * (i + 1), :]),
                 writes=[("pbt", tt)], dma=True, lane=("pbt", tt))

    def stB_main(Tt, tt, b0):
        par = Tt % 2
        x1 = x1s[par]
        i = 4 * Tt + tt
        xj = i % 2
        P.op("sp", lambda e, i=i, xj=xj: e.dma_start(out=xr[xj][:], in_=x_d[128 * i:128 * (i + 1), :]),
             writes=[("xr", xj)], dma=True, lane=("xr", xj))
        for hf in range(2):
            for k in range(8):
                P.op("pe", lambda e, hf=hf, k=k, i=i, b0=b0: e.matmul(ps[:, b0 + hf, :], lhsT=CC[:, k, 128 * i:128 * (i + 1)],
                                                                   rhs=grp[k][:, 512 * hf:512 * (hf + 1)],
                                                                   start=(k == 0), stop=(k == 7)),
                     reads=[("CT", i, k), ("grp", k)], writes=[("ps", b0 + hf)])
        P.op("dve", lambda e, tt=tt, b0=b0, xj=xj, x1=x1: e.scalar_tensor_tensor(
            out=x1[:, tt, :].rearrange("p (a b) -> p a b", a=2), in0=xr[xj][:].rearrange("p (a b) -> p a b", a=2), scalar=ALPHA,
            in1=ps[:, b0:b0 + 2, :], op0=ALU.mult, op1=ALU.add),
            reads=[("xr", xj), ("ps", b0), ("ps", b0 + 1)], writes=[("x1", par, tt)])
        ln_norm(x1, par, tt)

    def stB_post(Tt, tt, b0):
        par = Tt % 2
        x1 = x1s[par]
        transpose_to_cc(x1, par, tt, 4 * Tt + tt, b0, 0)
        ln_affine(x1, par, tt, "ln1_g", "ln1_b")

    def ffn_up_chunk(Tt, f):
        tok0 = 512 * Tt
        sa = ring_next()
        sbb = ring_next()
        ba = f % 2
        bb = 2 + f % 2
        aj = f % 2
        for (bank, slot) in ((ba, sa), (bb, sbb)):
            for kc in range(8):
                P.op("pe", lambda e, bank=bank, slot=slot, kc=kc, tok0=tok0: e.matmul(
                    ps[:, bank, :], lhsT=ringS[slot][:, 128 * kc:128 * (kc + 1)], rhs=CC[:, kc, tok0:tok0 + 512],
                    start=(kc == 0), stop=(kc == 7)),
                    reads=[("ringS", slot)] + [("CT", 4 * Tt + q, kc) for q in range(4)], writes=[("ps", bank)])
        P.op("dve", lambda e, aj=aj, f=f: e.tensor_copy(out=abuf[aj][:, 0:2], in_=halo[:, f, :]),
             reads=[("halo", f), "halo"], writes=[("abh", aj)])
        P.op("act", lambda e, aj=aj, ba=ba: e.copy(out=abuf[aj][:, 2:514], in_=ps[:, ba, :]),
             reads=[("ps", ba)], writes=[("ab", aj)])
        P.op("dve", lambda e, aj=aj, f=f: e.tensor_copy(out=halo[:, f, :], in_=abuf[aj][:, 512:514]),
             reads=[("ab", aj)], writes=[("halo", f)])
        P.op("act", lambda e, aj=aj, ba=ba, f=f: e.activation(out=cacc[aj][:], in_=ps[:, ba, :], func=AF.Identity,
                                                              scale=cw[:, 2 * NF + f:2 * NF + f + 1], bias=cb[:, f:f + 1]),
             reads=[("ps", ba), "cw", "cb"], writes=[("cacc", aj)])
        P.op("dve", lambda e, aj=aj, f=f: e.scalar_tensor_tensor(out=cacc[aj][:], in0=abuf[aj][:, 1:513], scalar=cw[:, NF + f:NF + f + 1],
                                                                in1=cacc[aj][:], op0=ALU.mult, op1=ALU.add),
             reads=[("ab", aj), ("abh", aj), ("cacc", aj), "cw"], writes=[("cacc", aj)])
        P.op("dve", lambda e, aj=aj, f=f: e.scalar_tensor_tensor(out=cacc[aj][:], in0=abuf[aj][:, 0:512], scalar=cw[:, f:f + 1],
                                                                in1=cacc[aj][:], op0=ALU.mult, op1=ALU.add),
             reads=[("ab", aj), ("abh", aj), ("cacc", aj), "cw"], writes=[("cacc", aj)])
        P.op("act", lambda e, aj=aj: e.activation(out=cacc[aj][:], in_=cacc[aj][:], func=AF.Gelu),
             reads=[("cacc", aj)], writes=[("cacc", aj)])
        P.op("dve", lambda e, aj=aj, bb=bb, f=f: e.tensor_tensor(out=gT[:, f, :], in0=cacc[aj][:], in1=ps[:, bb, :], op=ALU.mult),
             reads=[("cacc", aj), ("ps", bb)], writes=[("gT", f)])

    def p_transposes():
        for tt in range(4):
            for kc in range(2):
                P.op("pe", lambda e, tt=tt, kc=kc: e.transpose(out=psb[:, 4, (tt * 2 + kc) * 128:(tt * 2 + kc + 1) * 128],
                                                               in_=pbt[tt][:, 128 * kc:128 * (kc + 1)], identity=identb[:]),
                     reads=[("pbt", tt), "identb"], writes=[("ps", 4)])
        P.op("dve", lambda e: e.tensor_copy(out=pT[:].rearrange("p k (t c) -> p k t c", c=128),
                                            in_=psb[:, 4, :].rearrange("p (t k c) -> p k t c", t=4, k=2)),
             reads=[("ps", 4)], writes=["pT"])

    def ffn_down(Tt):
        for f in range(NF):
            sd = ring_next()
            for tt in range(4):
                for hf in range(2):
                    P.op("pe", lambda e, f=f, tt=tt, hf=hf, sd=sd: e.matmul(ps[:, 2 * tt + hf, :], lhsT=gT[:, f, 128 * tt:128 * (tt + 1)],
                                                                        rhs=ringS[sd][:, 512 * hf:512 * (hf + 1)],
                                                                        start=(f == 0), stop=(f == NF - 1)),
                         reads=[("gT", f), ("ringS", sd)], writes=[("ps", 2 * tt + hf)])

    def ln2_main(Tt, tt):
        par = Tt % 2
        x1 = x1s[par]
        b0 = 2 * tt
        P.op("dve", lambda e, tt=tt, b0=b0, x1=x1: e.scalar_tensor_tensor(
            out=x1[:, tt, :].rearrange("p (a b) -> p a b", a=2), in0=x1[:, tt, :].rearrange("p (a b) -> p a b", a=2), scalar=ALPHA,
            in1=ps[:, b0:b0 + 2, :], op0=ALU.mult, op1=ALU.add),
            reads=[("x1", par, tt), ("ps", b0), ("ps", b0 + 1)], writes=[("x1", par, tt)])
        ln_norm(x1, par, tt)

    def ln2_post(Tt, tt):
        par = Tt % 2
        x1 = x1s[par]
        transpose_to_cc(x1, par, tt, 4 * Tt + tt, 2 * tt, 16)
        ln_affine(x1, par, tt, "ln2_g", "ln2_b")

    def gate_a(Tt, tt):
        par = Tt % 2
        x1 = x1s[par]
        i = 4 * Tt + tt
        bg_, bp_ = 4 * (tt % 2), 4 * (tt % 2) + 2
        for hf in range(2):
            for k in range(8):
                P.op("pe", lambda e, hf=hf, k=k, i=i, bg_=bg_: e.matmul(ps[:, bg_ + hf, :], lhsT=CC[:, k, 128 * i:128 * (i + 1)],
                                                                     rhs=grp[k][:, 512 * hf:512 * (hf + 1)],
                                                                     start=(k == 0), stop=(k == 7)),
                     reads=[("CT", i, k), ("grp", k)], writes=[("ps", bg_ + hf)])
        for hf in range(2):
            for k in range(2):
                P.op("pe", lambda e, hf=hf, k=k, tt=tt, bp_=bp_: e.matmul(ps[:, bp_ + hf, :], lhsT=pT[:, k, 128 * tt:128 * (tt + 1)],
                                                                      rhs=grp[8 + k][:, 512 * hf:512 * (hf + 1)],
                                                                      start=(k == 0), stop=(k == 1)),
                     reads=["pT", ("grp", 8 + k)], writes=[("ps", bp_ + hf)])
        gt = gtmp[0]
        P.op("dve", lambda e, gt=gt, bg_=bg_: e.tensor_tensor(out=gt[:].rearrange("p (a b) -> p a b", a=2), in0=ps[:, bg_:bg_ + 2, :],
                                                          in1=lnp["b_ple_gate"][:].rearrange("p (a b) -> p a b", a=2), op=ALU.add),
             reads=[("ps", bg_), ("ps", bg_ + 1), ("lnp", "b_ple_gate")], writes=[("gtmp", 0)])
        P.op("act", lambda e, gt=gt: e.activation(out=gt[:], in_=gt[:], func=AF.Sigmoid), reads=[("gtmp", 0)], writes=[("gtmp", 0)])
        P.op("dve", lambda e, gt=gt, bp_=bp_: e.tensor_tensor(out=gt[:].rearrange("p (a b) -> p a b", a=2),
                                                          in0=gt[:].rearrange("p (a b) -> p a b", a=2), in1=ps[:, bp_:bp_ + 2, :], op=ALU.mult),
             reads=[("gtmp", 0), ("ps", bp_), ("ps", bp_ + 1)], writes=[("gtmp", 0)])
        P.op("dve", lambda e, gt=gt, tt=tt, x1=x1: e.scalar_tensor_tensor(out=x1[:, tt, :], in0=x1[:, tt, :], scalar=ALPHA, in1=gt[:],
                                                                       op0=ALU.mult, op1=ALU.add),
             reads=[("x1", par, tt), ("gtmp", 0)], writes=[("x1", par, tt)])

    def gate_b(Tt, tt):
        par = Tt % 2
        x1 = x1s[par]
        i = 4 * Tt + tt
        ln_norm(x1, par, tt)
        ln_affine(x1, par, tt, "ln3_g", "ln3_b")
        o = P.op("pool", lambda e, i=i, tt=tt, x1=x1: e.dma_start(out=out_d[128 * i:128 * (i + 1), :], in_=x1[:, tt, :]),
                 reads=[("x1", par, tt)], writes=[("out", i)], dma=True, lane=("o", par, tt))
        P.final_waits.append(o.idx)

    def stB_staged(Tt, tt, b0m, b0p):
        par = Tt % 2
        x1 = x1s[par]
        i = 4 * Tt + tt
        xj = i % 2
        st = {}

        def s1():
            P.op("sp", lambda e: e.dma_start(out=xr[xj][:], in_=x_d[128 * i:128 * (i + 1), :]),
                 writes=[("xr", xj)], dma=True, lane=("xr", xj))
            for hf in range(2):
                for k in range(8):
                    P.op("pe", lambda e, hf=hf, k=k: e.matmul(ps[:, b0m + hf, :], lhsT=CC[:, k, 128 * i:128 * (i + 1)],
                                                             rhs=grp[k][:, 512 * hf:512 * (hf + 1)], start=(k == 0), stop=(k == 7)),
                         reads=[("CT", i, k), ("grp", k)], writes=[("ps", b0m + hf)])

        def s2():
            P.op("dve", lambda e: e.scalar_tensor_tensor(
                out=x1[:, tt, :].rearrange("p (a b) -> p a b", a=2), in0=xr[xj][:].rearrange("p (a b) -> p a b", a=2), scalar=ALPHA,
                in1=ps[:, b0m:b0m + 2, :], op0=ALU.mult, op1=ALU.add),
                reads=[("xr", xj), ("ps", b0m), ("ps", b0m + 1)], writes=[("x1", par, tt)])
            ln_stage(x1, par, tt, 0, st)

        def s7():
            cc_evac(i, b0p, 0)
            ln_affine(x1, par, tt, "ln1_g", "ln1_b")
        return [s1, s2, lambda: ln_stage(x1, par, tt, 1, st), lambda: ln_stage(x1, par, tt, 2, st),
                lambda: ln_stage(x1, par, tt, 3, st), lambda: cc_transposes(x1, par, tt, b0p), s7]

    def gate_b_staged(Tt, tt):
        par = Tt % 2
        x1 = x1s[par]
        i = 4 * Tt + tt
        st = {}

        def g5():
            ln_affine(x1, par, tt, "ln3_g", "ln3_b")
            o = P.op("pool", lambda e: e.dma_start(out=out_d[128 * i:128 * (i + 1), :], in_=x1[:, tt, :]),
                     reads=[("x1", par, tt)], writes=[("out", i)], dma=True, lane=("o", par, tt))
            P.final_waits.append(o.idx)
        return [lambda: ln_stage(x1, par, tt, 0, st), lambda: ln_stage(x1, par, tt, 1, st), lambda: ln_stage(x1, par, tt, 2, st),
                lambda: ln_stage(x1, par, tt, 3, st), g5]

    grp_load(0, 8, 0)
    p_loads(0)
    stB_main(0, 0, 4)
    stB_main(0, 1, 6)
    stB_post(0, 0, 0)
    stB_main(0, 2, 4)
    stB_post(0, 1, 2)
    stB_main(0, 3, 6)
    stB_post(0, 2, 0)
    stB_post(0, 3, 2)

    for Tt in range(8):
        nxt = Tt + 1 < 8
        extra = {}
        if Tt > 0:
            for tt in range(4):
                for q, fn in enumerate(gate_b_staged(Tt - 1, tt)):
                    extra.setdefault(tt + q, []).append(fn)
        if nxt:
            extra.setdefault(0, []).insert(0, lambda Tt=Tt: grp_load(0, 8, 0))
            for tt in range(4):
                stg = stB_staged(Tt + 1, tt, 4, 6)
                sched = [2, 4, 5, 6, 7, 8, 9]
                for q, fn in enumerate(stg):
                    extra.setdefault(sched[q] + 4 * tt, []).append(fn)
        if Tt > 0:
            p_loads(Tt)
        for f in range(NF):
            ffn_up_chunk(Tt, f)
            for fn in extra.get(f, []):
                fn()
        grp_load(0, 10, 74)
        p_transposes()
        ffn_down(Tt)
        ln2_main(Tt, 0)
        ln2_main(Tt, 1)
        ln2_post(Tt, 0)
        ln2_main(Tt, 2)
        ln2_post(Tt, 1)
        ln2_main(Tt, 3)
        ln2_post(Tt, 2)
        ln2_post(Tt, 3)
        for tt in range(4):
            gate_a(Tt, tt)
    for tt in range(4):
        gate_b(7, tt)
    P.emit()
    return nc


def _consts():
    j = np.arange(128)[:, None]
    i = np.arange(128)[None, :]
    maskL = (j <= i).astype(np.float32)
    maskU = (j >= i).astype(np.float32)
    ident = np.eye(128, dtype=np.float32)
    inv8 = np.float32(500000.0) ** (-(np.arange(0, 16, 2, dtype=np.float32)) / np.float32(16.0))
    ropec = np.zeros((128, 2), np.float32)
    for p_ in range(128):
        d = p_ % 64
        if d < 16:
            ropec[p_, 0] = inv8[d % 8]
            ropec[p_, 1] = -1.0 if d < 8 else 1.0
    return ident, maskL, maskU, ropec


def _prep_shared(inp):
    f = lambda a: np.ascontiguousarray(np.asarray(a, dtype=np.float32))
    w_in = f(inp["w_in"][0])
    perm = np.arange(1024)
    for h in range(16):
        base = h * 64
        perm[base:base + 8] = np.arange(base + 8, base + 16)
        perm[base + 8:base + 16] = np.arange(base, base + 8)
    w_sw = np.ascontiguousarray(w_in[:, :1024][:, perm])
    conv_w = f(inp["conv_w"][0])
    conv_b = f(inp["conv_b"][0])
    cw = np.ascontiguousarray(conv_w.reshape(3, NF, 128).transpose(2, 0, 1).reshape(128, 3 * NF))
    cb = np.ascontiguousarray(conv_b.reshape(NF, 128).T)
    ident, maskL, maskU, ropec = _consts()
    sh = {
        "w_in": w_in, "w_sw": w_sw,
        "ln_z_g": f(inp["ln_z_g"]), "ln_z_b": f(inp["ln_z_b"]),
        "w_sT": np.ascontiguousarray(f(inp["w_s"][0]).transpose(0, 2, 1)),
        "b_s": f(inp["b_s"][0]),
        "w_o": f(inp["w_o"][0]),
        "w_ff_a": f(inp["w_ff_a"][0]), "w_ff_b": f(inp["w_ff_b"][0]),
        "cw": cw, "cb": cb,
        "w_ff_down": f(inp["w_ff_down"][0]),
        "w_ple_gate": f(inp["w_ple_gate"][0]),
        "w_ple_in": f(inp["w_ple_in"][0]),
        "ident": ident, "maskL": maskL, "maskU": maskU, "ropec": ropec,
    }
    for nm in ("ln1_g", "ln1_b", "ln2_g", "ln2_b", "ln3_g", "ln3_b", "b_ple_gate"):
        sh[nm] = f(inp[nm])
    lnT = np.zeros((128, 32), np.float32)
    for j, (gn, bn) in enumerate((("ln1_g", "ln1_b"), ("ln2_g", "ln2_b"))):
        lnT[:, 16 * j:16 * j + 8] = sh[gn].reshape(8, 128).T
        lnT[:, 16 * j + 8:16 * j + 16] = sh[bn].reshape(8, 128).T
    sh["lnT"] = np.ascontiguousarray(lnT)
    return sh


_NC_CACHE = {}


def kernel(**inputs):
    sh = _prep_shared(inputs)
    x = np.asarray(inputs["x"], dtype=np.float32)
    p = np.asarray(inputs["p"], dtype=np.float32)
    pos = np.asarray(inputs["positions"], dtype=np.int32)
    in_maps = []
    for b in range(8):
        m = dict(sh)
        m["x"] = np.ascontiguousarray(x[b])
        m["p"] = np.ascontiguousarray(p[0, b])
        m["pos"] = np.ascontiguousarray(pos[b:b + 1])
        in_maps.append(m)
    nc = build()
    res = run_bass_kernel_spmd(nc, in_maps, core_ids=list(range(8)))
    out = np.stack([np.asarray(r["out"], dtype=np.float32) for r in res.results], axis=0)
    return out
```

```python
import numpy as np
import concourse.bass as bass
import concourse.mybir as mybir
from concourse.bass_utils import run_bass_kernel_spmd

F32 = mybir.dt.float32
BF16 = mybir.dt.bfloat16
I32 = mybir.dt.int32
AF = mybir.ActivationFunctionType
ALU = mybir.AluOpType

S = 4096
D = 1024
DFF = 2816
NF = 22
DPLE = 256
ALPHA = float(2.0 ** 0.25)
EPS = 1e-5
PI = float(np.pi)
TWO_PI = float(2 * np.pi)
KB = 1024


class Op:
    __slots__ = ("idx", "eng", "fn", "deps", "is_dma", "lane", "sig", "signals")

    def __init__(self, idx, eng, fn, is_dma, lane):
        self.idx = idx
        self.eng = eng
        self.fn = fn
        self.deps = set()
        self.is_dma = is_dma
        self.lane = lane
        self.sig = None
        self.signals = False


class Prog:
    ENGS = ("pe", "act", "dve", "pool", "sp")

    def __init__(self, nc):
        self.nc = nc
        self.ops = []
        self.state = {}
        self.final_waits = []
        self.last_eng = {}
        self.last_lane = {}
        self.pending_fence = {}

    def op(self, eng, fn, reads=(), writes=(), dma=False, lane=None):
        o = Op(len(self.ops), eng, fn, dma, lane)
        self.ops.append(o)
        pf = self.pending_fence.pop(eng, None)
        if pf is not None:
            o.deps.update(pf)
        for k in reads:
            st = self.state.get(k)
            if st is None:
                st = [None, []]
                self.state[k] = st
            if st[0] is not None:
                self._dep(o, st[0], "raw")
            if isinstance(k, tuple) and k[0] == "ps":
                for r in st[1]:
                    if self.ops[r].eng != eng:
                        o.deps.add(r)
            if not dma:
                st[1] = [r for r in st[1] if self.ops[r].is_dma or self.ops[r].eng != eng]
            st[1].append(o.idx)
        for k in writes:
            st = self.state.get(k)
            if st is None:
                st = [None, []]
                self.state[k] = st
            if st[0] is not None:
                self._dep(o, st[0], "waw")
            for r in st[1]:
                if r != o.idx:
                    self._dep(o, r, "war")
            st[0] = o.idx
            st[1] = []
        if dma:
            self.last_lane[tuple(lane)] = o.idx
        else:
            self.last_eng[eng] = o.idx
        return o

    def _dep(self, o, j, kind):
        t = self.ops[j]
        if t.eng == o.eng and not t.is_dma and not o.is_dma:
            if o.eng == "pe":
                return
        o.deps.add(j)

    def fence(self):
        deps = set(self.last_eng.values()) | set(self.last_lane.values())
        for e in self.ENGS:
            self.pending_fence[e] = set(deps)
        self.state = {}

    def emit(self):
        nc = self.nc
        ops = self.ops
        for o in ops:
            o.deps.discard(o.idx)
            if o.is_dma:
                o.signals = True
            for j in o.deps:
                ops[j].signals = True
        for j in self.final_waits:
            ops[j].signals = True
        sems = {}
        counters = {}
        for o in ops:
            if not o.signals:
                continue
            key = ("dma",) + tuple(o.lane) if o.is_dma else ("eng", o.eng)
            if key not in sems:
                sems[key] = nc.alloc_semaphore("s_" + "_".join(str(x) for x in key))
                counters[key] = 0
            counters[key] += 16 if o.is_dma else 1
            o.sig = (key, counters[key])
        streams = {e: [o for o in ops if o.eng == e] for e in self.ENGS}
        final = [ops[j].sig for j in self.final_waits]

        def run_stream(e, engh):
            waited = {}
            for o in streams[e]:
                need = {}
                for j in o.deps:
                    t = ops[j]
                    if t.eng == e and not t.is_dma and not o.is_dma and e == "pe":
                        continue
                    k, v = t.sig
                    if need.get(k, 0) < v:
                        need[k] = v
                for k, v in need.items():
                    if waited.get(k, 0) >= v:
                        continue
                    engh.wait_ge(sems[k], v)
                    waited[k] = v
                ins = o.fn(engh)
                if o.signals:
                    ins.then_inc(sems[o.sig[0]], 16 if o.is_dma else 1)
            if e == "sp":
                need = {}
                for k, v in final:
                    if need.get(k, 0) < v:
                        need[k] = v
                for k, v in need.items():
                    engh.wait_ge(sems[k], v)

        with nc.Block() as block:
            @block.tensor
            def _(pe):
                run_stream("pe", pe)

            @block.scalar
            def _(act):
                run_stream("act", act)

            @block.vector
            def _(dve):
                run_stream("dve", dve)

            @block.gpsimd
            def _(pool):
                run_stream("pool", pool)

            @block.sync
            def _(sp):
                run_stream("sp", sp)


def build(stop_after=None):
    nc = bass.Bass("TRN2", target_bir_lowering=False)
    P = Prog(nc)

    def din(name, shape, dt=F32):
        return nc.dram_tensor(name, list(shape), dt, kind="ExternalInput").ap()

    x_d = din("x", [S, D])
    p_d = din("p", [S, DPLE])
    pos_d = din("pos", [1, S], I32)
    w_in_d = din("w_in", [D, 2560])
    w_sw_d = din("w_sw", [D, 1024])
    lzg_d = din("ln_z_g", [1, 512])
    lzb_d = din("ln_z_b", [1, 512])
    wsT_d = din("w_sT", [8, 128, 128])
    bs_d = din("b_s", [8, 128])
    wo_d = din("w_o", [D, D])
    ln_d = {}
    for nm in ("ln1_g", "ln1_b", "ln2_g", "ln2_b", "ln3_g", "ln3_b", "b_ple_gate"):
        ln_d[nm] = din(nm, [1, D])
    wa_d = din("w_ff_a", [D, DFF])
    wb_d = din("w_ff_b", [D, DFF])
    cw_d = din("cw", [128, 3 * NF])
    cb_d = din("cb", [128, NF])
    wd_d = din("w_ff_down", [DFF, D])
    wg_d = din("w_ple_gate", [D, D])
    wp_d = din("w_ple_in", [DPLE, D])
    ident_d = din("ident", [128, 128])
    mL_d = din("maskL", [128, 128])
    mU_d = din("maskU", [128, 128])
    ropec_d = din("ropec", [128, 2])
    lnT_d = din("lnT", [128, 32])
    if stop_after == "A":
        out_d = nc.dram_tensor("out", [128, 8 * S], BF16, kind="ExternalOutput").ap()
    elif stop_after in ("B", "D", "G"):
        out_d = nc.dram_tensor("out", [128, 4 * D], F32, kind="ExternalOutput").ap()
    elif stop_after == "F":
        out_d = nc.dram_tensor("out", [128, NF * 512], BF16, kind="ExternalOutput").ap()
    else:
        out_d = nc.dram_tensor("out", [S, D], F32, kind="ExternalOutput").ap()

    tape = nc.dram_tensor("tape", [84, 128, D], BF16).ap()
    chunks84 = []
    for k in range(8):
        chunks84.append((wo_d[128 * k:128 * (k + 1), :], False))
    for f in range(NF):
        chunks84.append((wa_d[:, 128 * f:128 * (f + 1)], True))
        chunks84.append((wb_d[:, 128 * f:128 * (f + 1)], True))
    for f in range(NF):
        chunks84.append((wd_d[128 * f:128 * (f + 1), :], False))
    for k in range(8):
        chunks84.append((wg_d[128 * k:128 * (k + 1), :], False))
    for k in range(2):
        chunks84.append((wp_d[128 * k:128 * (k + 1), :], False))

    def sb(name, shape, dt, off):
        assert off % 32 == 0, (name, off)
        return nc.alloc_sbuf_tensor_at(name, list(shape), dt, offset=int(off))

    ps = nc.alloc_psum_tensor("ps", [128, 8, 512], F32)
    psb = ps.bitcast(BF16)

    CC = sb("CC", [128, 8, S], BF16, 16 * KB)
    TC = sb("TC", [128, S], F32, 48 * KB)
    TS = sb("TS", [128, S], F32, 64 * KB)
    c0 = 80 * KB
    identf = sb("identf", [128, 128], F32, c0)
    identb = sb("identb", [128, 128], BF16, c0 + 512)
    mL = sb("mL", [128, 128], BF16, c0 + 768)
    mU = sb("mU", [128, 128], BF16, c0 + 1024)
    ropec = sb("ropec", [128, 2], F32, c0 + 1280)
    NLS = 8
    lnst = sb("lnst", [128, NLS, 2, 6], F32, c0 + 1312)
    lnmv = sb("lnmv", [128, NLS, 4], F32, c0 + 1696)

    P.op("sp", lambda e: e.dma_start(out=identf[:], in_=ident_d), writes=["identf"], dma=True, lane=("c", 0))
    P.op("pool", lambda e: e.dma_start(out=identb[:], in_=ident_d), writes=["identb"], dma=True, lane=("c", 1))
    P.op("pool", lambda e: e.dma_start(out=mL[:], in_=mL_d), writes=["mL"], dma=True, lane=("c", 2))
    P.op("pool", lambda e: e.dma_start(out=mU[:], in_=mU_d), writes=["mU"], dma=True, lane=("c", 3))
    P.op("sp", lambda e: e.dma_start(out=ropec[:], in_=ropec_d), writes=["ropec"], dma=True, lane=("c", 4))

    e0 = 146 * KB
    HT = 1024
    pos_i = sb("pos_i", [128, HT], I32, e0)
    pos_f = sb("pos_f", [128, HT], F32, e0 + 4 * KB)
    ang = sb("ang", [128, HT], F32, e0 + 8 * KB)
    a2 = sb("a2", [128, HT], F32, e0 + 12 * KB)
    ki = sb("ki", [128, HT], I32, e0 + 16 * KB)
    rr = sb("rr", [128, HT], F32, e0 + 20 * KB)
    tt_ = sb("tt_", [128, HT], F32, e0 + 24 * KB)
    C1 = 6.28125
    C2 = TWO_PI - C1
    for h in range(4):
        cs = slice(h * HT, (h + 1) * HT)
        P.op("sp", lambda e, cs=cs: e.dma_start(out=pos_i[:], in_=pos_d[0, cs].partition_broadcast(128)),
             writes=["pos_i"], dma=True, lane=("c", 5))
        P.op("dve", lambda e: e.tensor_copy(out=pos_f[:], in_=pos_i[:]), reads=["pos_i"], writes=["pos_f"])
        P.op("dve", lambda e: e.tensor_scalar(out=ang[:], in0=pos_f[:], scalar1=ropec[:, 0:1], scalar2=None, op0=ALU.mult),
             reads=["pos_f", "ropec"], writes=["ang"])
        for tab, shift in ((TS, 0.0), (TC, PI / 2)):
            if shift != 0.0:
                P.op("dve", lambda e, shift=shift: e.tensor_scalar(out=a2[:], in0=ang[:], scalar1=shift, scalar2=None, op0=ALU.add),
                     reads=["ang"], writes=["a2"])
                src = a2
                srck = "a2"
            else:
                src = ang
                srck = "ang"
            P.op("dve", lambda e, src=src: e.tensor_scalar(out=ki[:], in0=src[:], scalar1=1.0 / TWO_PI, scalar2=None, op0=ALU.mult),
                 reads=[srck], writes=["ki"])
            P.op("dve", lambda e, src=src: e.scalar_tensor_tensor(out=rr[:], in0=ki[:], scalar=-C1, in1=src[:], op0=ALU.mult, op1=ALU.add),
                 reads=["ki", srck], writes=["rr"])
            P.op("dve", lambda e: e.scalar_tensor_tensor(out=rr[:], in0=ki[:], scalar=-C2, in1=rr[:], op0=ALU.mult, op1=ALU.add),
                 reads=["ki", "rr"], writes=["rr"])
            P.op("dve", lambda e: e.tensor_scalar(out=tt_[:], in0=rr[:], scalar1=PI, scalar2=TWO_PI, op0=ALU.is_gt, op1=ALU.mult),
                 reads=["rr"], writes=["tt_"])
            P.op("dve", lambda e: e.tensor_tensor(out=rr[:], in0=rr[:], in1=tt_[:], op=ALU.subtract), reads=["rr", "tt_"], writes=["rr"])
            P.op("dve", lambda e: e.tensor_scalar(out=tt_[:], in0=rr[:], scalar1=-PI, scalar2=TWO_PI, op0=ALU.is_lt, op1=ALU.mult),
                 reads=["rr"], writes=["tt_"])
            P.op("dve", lambda e: e.tensor_tensor(out=rr[:], in0=rr[:], in1=tt_[:], op=ALU.add), reads=["rr", "tt_"], writes=["rr"])
            P.op("dve", lambda e: e.tensor_scalar(out=rr[:], in0=rr[:], scalar1=PI, scalar2=-PI, op0=ALU.min, op1=ALU.max),
                 reads=["rr"], writes=["rr"])
            if shift == 0.0:
                P.op("act", lambda e, tab=tab, cs=cs: e.activation(out=tab[:, cs], in_=rr[:], func=AF.Sin, scale=ropec[:, 1:2]),
                     reads=["rr", "ropec"], writes=[("tab", "S", h)])
            else:
                P.op("act", lambda e, tab=tab, cs=cs: e.activation(out=tab[:, cs], in_=rr[:], func=AF.Sin),
                     reads=["rr"], writes=[("tab", "C", h)])

    P.fence()

    XT = sb("XT", [128, 8, S], BF16, 82 * KB)
    Qt = sb("Qt", [128, S], BF16, 146 * KB)
    Kt = sb("Kt", [128, S], BF16, 154 * KB)
    Vt = sb("Vt", [128, S], BF16, 162 * KB)
    wbuf = [sb("wbuf%d" % i, [128, 8, 128], BF16, 170 * KB + 2 * KB * i) for i in range(4)]
    t1 = [sb("t1_%d" % i, [128, 512], F32, 178 * KB + 4 * KB * i) for i in range(2)]
    t2 = [sb("t2_%d" % i, [128, 512], F32, 180 * KB + 4 * KB * i) for i in range(2)]
    xs = [sb("xs%d" % i, [128, D], BF16, 186 * KB + 2 * KB * i) for i in range(2)]
    NPT = 6
    PT = [sb("PT%d" % i, [128, 512], BF16, 190 * KB + KB * i) for i in range(NPT)]
    VB1 = sb("VB1", [128, 8, 3, 64], BF16, 196 * KB)
    VB4 = sb("VB4", [128, 8, 3, 64], BF16, 199 * KB)
    VB16 = sb("VB16", [128, 32, 3, 64], BF16, 202 * KB)
    rc = [sb("rc%d" % i, [128, 512], F32, 214 * KB + 2 * KB * i) for i in range(2)]

    for nm_, t in (("VB1", VB1), ("VB4", VB4), ("VB16", VB16)):
        P.op("pool", lambda e, t=t: e.memset(t[:, :, 1, :], 1.0), writes=[("vbones", nm_)])

    for i in range(32):
        j = i % 2
        P.op("pool", lambda e, i=i, j=j: e.dma_start(out=xs[j][:], in_=x_d[128 * i:128 * (i + 1), :]),
             writes=[("xs", j)], dma=True, lane=("xs", j))
        bk = 4 + j
        for k in range(8):
            P.op("pe", lambda e, j=j, k=k, bk=bk: e.transpose(out=psb[:, bk, k * 128:(k + 1) * 128],
                                                              in_=xs[j][:, k * 128:(k + 1) * 128], identity=identb[:]),
                 reads=[("xs", j), "identb"], writes=[("ps", bk)])
        eng = "act" if i % 2 == 0 else "dve"
        src = psb[:, bk, :].rearrange("p (k c) -> p k c", c=128)
        dst = XT[:, :, 128 * i:128 * (i + 1)]
        if eng == "act":
            P.op("act", lambda e, src=src, dst=dst: e.copy(out=dst, in_=src), reads=[("ps", bk)], writes=[("XT", i)])
        else:
            P.op("dve", lambda e, src=src, dst=dst: e.tensor_copy(out=dst, in_=src), reads=[("ps", bk)], writes=[("XT", i)])

    wcount = [0]

    def load_w(src_ap):
        s_ = wcount[0] % 4
        wcount[0] += 1
        P.op("pool", lambda e, s_=s_, src_ap=src_ap: e.dma_start(out=wbuf[s_][:], in_=src_ap.rearrange("(kc p) c -> p kc c", p=128)),
             writes=[("wb", s_)], dma=True, lane=("wb", s_))
        return s_

    def proj_mm(bank, slot, T8):
        for kc in range(8):
            P.op("pe", lambda e, bank=bank, slot=slot, kc=kc, T8=T8: e.matmul(
                ps[:, bank, :], lhsT=wbuf[slot][:, kc, :], rhs=XT[:, kc, 512 * T8:512 * (T8 + 1)],
                start=(kc == 0), stop=(kc == 7)),
                reads=[("wb", slot)] + [("XT", 4 * T8 + q) for q in range(4)], writes=[("ps", bank)])

    pt_ctr = [0]
    grp_ctr = [0]
    acc_ctr = [0]
    tr_ctr = [0]
    msk_ctr = [0]

    def build_vb(vbt, nm_, idx0, tok_ap_fns):
        cnt = len(tok_ap_fns)
        bk = tr_ctr[0] % 2
        tr_ctr[0] += 1
        for q, fn in enumerate(tok_ap_fns):
            P.op("pe", lambda e, q=q, fn=fn: e.transpose(out=psb[:, bk, q * 128:(q + 1) * 128], in_=fn(), identity=identb[:]),
                 reads=["Vall", "identb"], writes=[("ps", bk)])
        src = psb[:, bk, 0:cnt * 128].rearrange("p (q a b) -> p q a b", a=2, b=64)
        P.op("dve", lambda e: e.tensor_copy(out=vbt[:, idx0:idx0 + cnt, 0:3:2, :], in_=src), reads=[("ps", bk)],
             writes=[("vb", nm_, idx0 + q) for q in range(cnt)])

    for hp in range(4):
        for (dest, dkey, c_main, c_sw) in ((Qt, "Q", hp * 128, hp * 128), (Kt, "K", 512 + hp * 128, 512 + hp * 128)):
            sa = load_w(w_in_d[:, c_main:c_main + 128])
            sbw = load_w(w_sw_d[:, c_sw:c_sw + 128])
            for T8 in range(8):
                st_ = T8 % 2
                ba, bb = 2 * st_, 2 * st_ + 1
                proj_mm(ba, sa, T8)
                proj_mm(bb, sbw, T8)
                cs = slice(512 * T8, 512 * (T8 + 1))
                P.op("dve", lambda e, st_=st_, ba=ba, cs=cs: e.tensor_tensor(out=t1[st_][:], in0=ps[:, ba, :], in1=TC[:, cs], op=ALU.mult),
                     reads=[("ps", ba)] + [("tab", "C", h_) for h_ in range(4)], writes=[("t1", st_)])
                P.op("dve", lambda e, st_=st_, bb=bb, cs=cs: e.tensor_tensor(out=t2[st_][:], in0=ps[:, bb, :], in1=TS[:, cs], op=ALU.mult),
                     reads=[("ps", bb)] + [("tab", "S", h_) for h_ in range(4)], writes=[("t2", st_)])
                P.op("pool", lambda e, st_=st_, dest=dest, cs=cs: e.tensor_tensor(out=dest[:, cs], in0=t1[st_][:], in1=t2[st_][:], op=ALU.add),
                     reads=[("t1", st_), ("t2", st_)], writes=[(dkey, T8), dkey + "all"])
        sv = load_w(w_in_d[:, 1024 + hp * 128:1024 + (hp + 1) * 128])
        for T8 in range(8):
            bk = 2 * (T8 % 2)
            proj_mm(bk, sv, T8)
            cs = slice(512 * T8, 512 * (T8 + 1))
            P.op("act", lambda e, bk=bk, cs=cs: e.copy(out=Vt[:, cs], in_=ps[:, bk, :]), reads=[("ps", bk)], writes=["Vall"])

        for m in range(21 * hp, 21 * (hp + 1)):
            src_ap, view3 = chunks84[m]
            if view3:
                P.op("pool", lambda e, m=m, src_ap=src_ap: e.dma_start(out=tape[m].rearrange("p (kc c) -> p kc c", c=128),
                                                                   in_=src_ap.rearrange("(kc p) c -> p kc c", p=128)),
                     writes=[("tape", m)], dma=True, lane=("tape", m % 4))
            else:
                P.op("pool", lambda e, m=m, src_ap=src_ap: e.dma_start(out=tape[m], in_=src_ap),
                     writes=[("tape", m)], dma=True, lane=("tape", m % 4))

        for n16 in range(2):
            for r8 in range(2):
                build_vb(VB16, "VB16", n16 * 16 + r8 * 8,
                         [(lambda n16=n16, r16=r8 * 8 + q: Vt[:, 2048 * n16 + r16:2048 * (n16 + 1):16]) for q in range(8)])

        G = []
        for W in range(8):
            n16 = W // 4
            Wl = W % 4
            for hh in range(2):
                po = 64 * hh
                accb = 6 + acc_ctr[0] % 2
                acc_ctr[0] += 1
                ri = acc_ctr[0] % 2
                groups = []
                blk = []
                for b in range(4):
                    n = 4 * W + b
                    sl = slice(128 * n, 128 * (n + 1))
                    blk.append((sl, sl, slice(128 * b, 128 * (b + 1)), (VB1, 'VB1', n % 8), b))
                groups.append((blk, 128, "L"))
                blk = []
                for b in range(4):
                    n = 4 * W + b
                    if n == 0:
                        continue
                    blk.append((slice(128 * (n - 1), 128 * n), slice(128 * n, 128 * (n + 1)), slice(128 * b, 128 * (b + 1)),
                                (VB1, 'VB1', (n - 1) % 8), b))
                groups.append((blk, 128, "U"))
                blk = []
                for r4 in range(4):
                    sl = slice(512 * W + r4, 512 * (W + 1), 4)
                    blk.append((sl, sl, slice(r4, 512, 4), (VB4, 'VB4', (W % 2) * 4 + r4), r4))
                groups.append((blk, 128, "L"))
                if W > 0:
                    blk = []
                    for r4 in range(4):
                        blk.append((slice(512 * (W - 1) + r4, 512 * W, 4), slice(512 * W + r4, 512 * (W + 1), 4),
                                    slice(r4, 512, 4), (VB4, 'VB4', ((W - 1) % 2) * 4 + r4), r4))
                    groups.append((blk, 128, "U"))
                blk = []
                for r16 in range(16):
                    ksl = slice(2048 * n16 + r16, 2048 * (n16 + 1), 16)
                    qsl = slice(512 * W + r16, 512 * (W + 1), 16)
                    blk.append((ksl, qsl, slice(r16, 512, 16), (VB16, 'VB16', n16 * 16 + r16), r16))
                groups.append((blk, 32, "L16"))
                if n16 == 1:
                    blk = []
                    for r16 in range(16):
                        ksl = slice(r16, 2048, 16)
                        qsl = slice(512 * W + r16, 512 * (W + 1), 16)
                        blk.append((ksl, qsl, slice(r16, 512, 16), (VB16, 'VB16', r16), r16))
                    groups.append((blk, 32, "U16"))
                ngr = len(groups)
                for gi, (blk, N, mtype) in enumerate(groups):
                    G.append(dict(blk=blk, N=N, mtype=mtype, W=W, hh=hh, po=po, Wl=Wl, accb=accb, ri=ri,
                                  first=(gi == 0), last=(gi == ngr - 1), newwin=(gi == 0 and hh == 0)))

        def emit_qk(g):
            W = g["W"]
            if g["newwin"]:
                build_vb(VB1, "VB1", (4 * W) % 8, [(lambda n=4 * W + b: Vt[:, 128 * n:128 * (n + 1)]) for b in range(4)])
                build_vb(VB4, "VB4", (W % 2) * 4, [(lambda W=W, r4=r4: Vt[:, 512 * W + r4:512 * (W + 1):4]) for r4 in range(4)])
            blk, N, mtype, po, Wl = g["blk"], g["N"], g["mtype"], g["po"], g["Wl"]
            sbk = 4 + grp_ctr[0] % 2
            grp_ctr[0] += 1
            pt = pt_ctr[0] % NPT
            pt_ctr[0] += 1
            g["pt"] = pt
            cols_lo = blk[0][4] * N
            cols_hi = (blk[-1][4] + 1) * N
            for (ksl, qsl, acols, vbt, pos) in blk:
                P.op("pe", lambda e, sbk=sbk, pos=pos, N=N, ksl=ksl, qsl=qsl, po=po: e.matmul(
                    ps[:, sbk, pos * N:(pos + 1) * N], lhsT=Kt[po:po + 64, ksl], rhs=Qt[po:po + 64, qsl],
                    start=True, stop=True),
                    reads=["Kall", "Qall"], writes=[("ps", sbk)])
            P.op("act", lambda e, sbk=sbk, pt=pt, lo=cols_lo, hi=cols_hi: e.activation(
                out=PT[pt][:, lo:hi], in_=ps[:, sbk, lo:hi], func=AF.Exp, scale=0.125),
                reads=[("ps", sbk)], writes=[("PT", pt)])
            nb = (cols_hi - cols_lo) // N
            if mtype == "L":
                mfn = lambda nb=nb: mL[:, :].unsqueeze(1).broadcast_to([128, nb, 128])
            elif mtype == "U":
                mfn = lambda nb=nb: mU[:, :].unsqueeze(1).broadcast_to([128, nb, 128])
            elif mtype == "L16":
                mfn = lambda Wl=Wl: mL[:, 32 * Wl:32 * (Wl + 1)].unsqueeze(1).broadcast_to([128, 16, 32])
            else:
                mfn = lambda Wl=Wl: mU[:, 32 * Wl:32 * (Wl + 1)].unsqueeze(1).broadcast_to([128, 16, 32])
            meng = "pool" if msk_ctr[0] % 2 == 0 else "dve"
            msk_ctr[0] += 1
            P.op(meng, lambda e, pt=pt, lo=cols_lo, hi=cols_hi, N=N, mfn=mfn: e.tensor_tensor(
                out=PT[pt][:, lo:hi].rearrange("p (a b) -> p a b", b=N),
                in0=PT[pt][:, lo:hi].rearrange("p (a b) -> p a b", b=N), in1=mfn(), op=ALU.mult),
                reads=[("PT", pt), "mL", "mU"], writes=[("PT", pt)])

        def emit_pv(g, hp=hp):
            blk, N, po, accb, ri, hh, W, pt = g["blk"], g["N"], g["po"], g["accb"], g["ri"], g["hh"], g["W"], g["pt"]
            for bi, (ksl, qsl, acols, vbt, pos) in enumerate(blk):
                first = g["first"] and bi == 0
                last = g["last"] and (bi == len(blk) - 1)
                P.op("pe", lambda e, accb=accb, acols=acols, vbt=vbt, hh=hh, pt=pt, pos=pos, N=N, first=first, last=last: e.matmul(
                    ps[:, accb, acols], lhsT=vbt[0][:, vbt[2], hh:hh + 2, :].rearrange("p a b -> p (a b)"),
                    rhs=PT[pt][:, pos * N:(pos + 1) * N], start=first, stop=last, skip_group_check=True),
                    reads=[("PT", pt), ("vb", vbt[1], vbt[2]), ("vbones", vbt[1])], writes=[("ps", accb)])
            if g["last"]:
                dlo = 64 - po
                P.op("act", lambda e, ri=ri, accb=accb, dlo=dlo: e.activation(out=rc[ri][dlo:dlo + 64, :], in_=ps[dlo:dlo + 64, accb, :], func=AF.Ln),
                     reads=[("ps", accb)], writes=[("rc", ri)])
                P.op("act", lambda e, ri=ri, dlo=dlo: e.activation(out=rc[ri][dlo:dlo + 64, :], in_=rc[ri][dlo:dlo + 64, :], func=AF.Exp, scale=-1.0),
                     reads=[("rc", ri)], writes=[("rc", ri)])
                P.op("dve", lambda e, ri=ri, accb=accb, dlo=dlo, po=po, hp=hp, W=W: e.tensor_tensor(
                    out=CC[po:po + 64, hp, 512 * W:512 * (W + 1)], in0=ps[po:po + 64, accb, :], in1=rc[ri][dlo:dlo + 64, :], op=ALU.mult),
                    reads=[("ps", accb), ("rc", ri)], writes=[("CCa", hp, W, hh)])

        LOOK = 2
        for q in range(min(LOOK, len(G))):
            emit_qk(G[q])
        for q in range(len(G)):
            emit_pv(G[q])
            if q + LOOK < len(G):
                emit_qk(G[q + LOOK])
    P.fence()

    g0 = 146 * KB
    wz = sb("wz", [128, 8, 512], BF16, g0)
    wu = [sb("wu%d" % c, [128, 8, 128], BF16, g0 + 8 * KB + 2 * KB * c) for c in range(4)]
    wsf = sb("wsf", [128, 8, 128], F32, g0 + 16 * KB)
    wsm = sb("wsm", [128, 8, 128], BF16, g0 + 20 * KB)
    bsb = sb("bsb", [128, 4, 128], F32, g0 + 22 * KB)
    lzg = sb("lzg", [128, 512], F32, g0 + 24 * KB)
    lzb = sb("lzb", [128, 512], F32, g0 + 26 * KB)
    ug = [sb("ug%d" % i, [128, 4, 512], F32, g0 + 28 * KB + 8 * KB * i) for i in range(2)]
    zg = [sb("zg%d" % i, [128, 512], F32, g0 + 54 * KB + 2 * KB * i) for i in range(4)]
    zn = [sb("zn%d" % i, [128, 512], BF16, g0 + 62 * KB + KB * i) for i in range(4)]
    mx = [sb("mx%d" % i, [128, 4, 128], F32, g0 + 50 * KB + 2 * KB * i) for i in range(2)]

    P.op("pool", lambda e: e.dma_start(out=wz[:], in_=w_in_d[:, 2048:2560].rearrange("(kc p) c -> p kc c", p=128)),
         writes=["wz"], dma=True, lane=("g", 0))
    for c in range(4):
        P.op("pool", lambda e, c=c: e.dma_start(out=wu[c][:], in_=w_in_d[:, 1536 + 128 * c:1536 + 128 * (c + 1)].rearrange("(kc p) c -> p kc c", p=128)),
             writes=[("wu", c)], dma=True, lane=("g", 1 + c))
    P.op("sp", lambda e: e.dma_start(out=wsf[:], in_=wsT_d.rearrange("g j i -> j g i")), writes=["wsf"], dma=True, lane=("g", 5))
    for gp in range(4):
        for hf in range(2):
            P.op("sp", lambda e, gp=gp, hf=hf: e.dma_start(out=bsb[64 * hf:64 * (hf + 1), gp, :], in_=bs_d[2 * gp + hf, :].partition_broadcast(64)),
                 writes=[("bsb", gp, hf)], dma=True, lane=("g", 6 + gp * 2 + hf))
    P.op("sp", lambda e: e.dma_start(out=lzg[:], in_=lzg_d[0, :].partition_broadcast(128)), writes=["lzg"], dma=True, lane=("g", 14))
    P.op("sp", lambda e: e.dma_start(out=lzb[:], in_=lzb_d[0, :].partition_broadcast(128)), writes=["lzb"], dma=True, lane=("g", 15))
    P.op("dve", lambda e: e.tensor_tensor(out=wsm[:], in0=wsf[:], in1=mL[:, :].unsqueeze(1).broadcast_to([128, 8, 128]), op=ALU.mult),
         reads=["wsf", "mL"], writes=["wsm"])

    def layernorm_stats(src_fn, nchunks, slot, key_in, epst=None):
        for c in range(nchunks):
            P.op("dve", lambda e, c=c: e.bn_stats(out=lnst[:, slot, c, :], in_=src_fn(c)), reads=[key_in], writes=[("lnst", slot, c)])
        P.op("dve", lambda e: e.bn_aggr(out=lnmv[:, slot, 0:2], in_=lnst[:, slot, 0:nchunks, :]),
             reads=[("lnst", slot, c) for c in range(nchunks)], writes=[("mv", slot)])
        P.op("act", lambda e: e.activation(out=lnmv[:, slot, 2:3], in_=lnmv[:, slot, 1:2], func=AF.Sqrt,
                                           bias=(epst if epst is not None else epsb)[:, 0:1]),
             reads=[("mv", slot), "epsb", "eps4b"], writes=[("sd", slot)])
        P.op("dve", lambda e: e.reciprocal(out=lnmv[:, slot, 2:3], in_=lnmv[:, slot, 2:3]), reads=[("sd", slot)], writes=[("sd", slot)])
        P.op("dve", lambda e: e.tensor_scalar(out=lnmv[:, slot, 3:4], in0=lnmv[:, slot, 0:1], scalar1=lnmv[:, slot, 2:3], scalar2=-1.0,
                                              op0=ALU.mult, op1=ALU.mult),
             reads=[("mv", slot), ("sd", slot)], writes=[("nmr", slot)])

    epsb = sb("epsb", [128, 1], F32, c0 + 1856)
    P.op("dve", lambda e: e.memset(epsb[:], EPS), writes=["epsb"])

    ln_ctr = [0]

    def g_uproj(T8):
        ub = T8 % 2
        for c in range(4):
            bk = c % 2
            for kc in range(8):
                P.op("pe", lambda e, bk=bk, c=c, kc=kc, T8=T8: e.matmul(ps[:, bk, :], lhsT=wu[c][:, kc, :], rhs=XT[:, kc, 512 * T8:512 * (T8 + 1)],
                                                                    start=(kc == 0), stop=(kc == 7)),
                     reads=[("wu", c)], writes=[("ps", bk)])
            P.op("act", lambda e, bk=bk, c=c, ub=ub: e.activation(out=ug[ub][:, c, :], in_=ps[:, bk, :], func=AF.Gelu),
                 reads=[("ps", bk)], writes=[("ug", ub, c)])

    def g_zproj(i):
        zb = i % 4
        bk = 2 + i % 2
        for kc in range(8):
            P.op("pe", lambda e, bk=bk, kc=kc, i=i: e.matmul(ps[:, bk, :], lhsT=XT[:, kc, 128 * i:128 * (i + 1)], rhs=wz[:, kc, :],
                                                             start=(kc == 0), stop=(kc == 7)),
                 reads=["wz"], writes=[("ps", bk)])
        P.op("act", lambda e, bk=bk, zb=zb: e.activation(out=zg[zb][:], in_=ps[:, bk, :], func=AF.Gelu),
             reads=[("ps", bk)], writes=[("zg", zb)])
        slot = ln_ctr[0] % NLS
        ln_ctr[0] += 1
        layernorm_stats(lambda c, zb=zb: zg[zb][:], 1, slot, ("zg", zb))
        P.op("act", lambda e, zb=zb, slot=slot: e.activation(out=zg[zb][:], in_=zg[zb][:], func=AF.Identity,
                                                             scale=lnmv[:, slot, 2:3], bias=lnmv[:, slot, 3:4]),
             reads=[("zg", zb), ("sd", slot), ("nmr", slot)], writes=[("zg", zb)])
        P.op("pool", lambda e, zb=zb: e.tensor_tensor(out=zg[zb][:], in0=zg[zb][:], in1=lzg[:], op=ALU.mult),
             reads=[("zg", zb), "lzg"], writes=[("zg", zb)])
        P.op("pool", lambda e, zb=zb: e.tensor_tensor(out=zn[zb][:], in0=zg[zb][:], in1=lzb[:], op=ALU.add),
             reads=[("zg", zb), "lzb"], writes=[("zn", zb)])

    def g_spatial(i):
        zb = i % 4
        mb = i % 2
        T8, tt = i // 4, i % 4
        ub = T8 % 2
        sbk = 4 + i % 2
        for gp in range(4):
            for hf in range(2):
                g = 2 * gp + hf
                P.op("pe", lambda e, sbk=sbk, gp=gp, hf=hf, g=g, zb=zb: e.matmul(
                    ps[64 * hf:64 * (hf + 1), sbk, 128 * gp:128 * (gp + 1)], lhsT=zn[zb][:, 64 * g:64 * (g + 1)], rhs=wsm[:, g, :],
                    start=True, stop=True),
                    reads=[("zn", zb), "wsm"], writes=[("ps", sbk)])
        P.op("dve", lambda e, sbk=sbk, mb=mb: e.tensor_tensor(out=mx[mb][:], in0=ps[:, sbk, :].rearrange("p (a b) -> p a b", b=128),
                                                           in1=bsb[:], op=ALU.add),
             reads=[("ps", sbk)] + [("bsb", gp, hf) for gp in range(4) for hf in range(2)], writes=[("mx", mb)])
        P.op("pool", lambda e, mb=mb, ub=ub, tt=tt, i=i: e.tensor_tensor(out=CC[:, 4:8, 128 * i:128 * (i + 1)], in0=mx[mb][:],
                                                                      in1=ug[ub][:, :, 128 * tt:128 * (tt + 1)], op=ALU.mult),
             reads=[("mx", mb)] + [("ug", ub, c) for c in range(4)], writes=[("CCg", i)])

    GL = 2
    for i in range(32 + GL):
        if i < 32:
            if i % 4 == 0:
                g_uproj(i // 4)
            g_zproj(i)
        if i >= GL:
            g_spatial(i - GL)

    if stop_after == "A":
        P.fence()
        o = P.op("sp", lambda e: e.dma_start(out=out_d, in_=CC[:].rearrange("p a b -> p (a b)")), writes=["out"], dma=True, lane=("o", 0))
        P.final_waits = [o.idx]
        P.emit()
        return nc
    P.fence()

    t0 = 82 * KB
    lnp = {}
    for qi, nm in enumerate(("ln1_g", "ln1_b", "ln2_g", "ln2_b", "ln3_g", "ln3_b", "b_ple_gate")):
        lnp[nm] = sb("lnp_" + nm, [128, D], F32, t0 + 4 * KB * qi)
        P.op("sp", lambda e, nm=nm: e.dma_start(out=lnp[nm][:], in_=ln_d[nm][0, :].partition_broadcast(128)),
             writes=[("lnp", nm)], dma=True, lane=("lnp", qi))
    x1s = [sb("x1_%d" % i, [128, 4, D], F32, 110 * KB + 16 * KB * i) for i in range(2)]
    gT = sb("gT", [128, NF, 512], BF16, 142 * KB)
    pT = sb("pT", [128, 2, 512], BF16, 164 * KB)
    NS = 8
    ringS = [sb("ringS%d" % i, [128, D], BF16, 166 * KB + 2 * KB * i) for i in range(NS)]
    grp = [sb("grp%d" % i, [128, D], BF16, 182 * KB + 2 * KB * i) for i in range(10)]
    xr = [sb("xr0", [128, D], F32, 217 * KB)]
    lnstB = [sb("lnstB%d" % i, [128, 4, 2, 6], F32, 222 * KB + 192 * i) for i in range(2)]
    lnmvB = [sb("lnmvB%d" % i, [128, 4, 4], F32, 222 * KB + 384 + 64 * i) for i in range(2)]
    eps4b = sb("eps4b", [128, 1], F32, 222 * KB + 512)
    pbt = [sb("pbt%d" % i, [128, DPLE], BF16, 202 * KB + 512 * i) for i in range(4)]
    abuf = [sb("abuf%d" % i, [128, 520], F32, 204 * KB + 2080 * i) for i in range(2)]
    cacc = [sb("cacc%d" % i, [128, 512], F32, 209 * KB + 2 * KB * i) for i in range(2)]
    gtmp = [sb("gtmp0", [128, D], F32, 213 * KB)]
    cw = sb("cw", [128, 3 * NF], F32, 221 * KB)
    cb = sb("cb", [128, NF], F32, 221 * KB + 288)
    halo = sb("halo", [128, NF, 2], F32, 221 * KB + 384)
    lnT = sb("lnT", [128, 32], F32, 221 * KB + 576)

    P.op("sp", lambda e: e.dma_start(out=cw[:], in_=cw_d), writes=["cw"], dma=True, lane=("t", 0))
    P.op("sp", lambda e: e.dma_start(out=cb[:], in_=cb_d), writes=["cb"], dma=True, lane=("t", 1))
    P.op("sp", lambda e: e.dma_start(out=lnT[:], in_=lnT_d), writes=["lnT"], dma=True, lane=("t", 2))
    P.op("dve", lambda e: e.memset(halo[:], 0.0), writes=["halo"])
    P.op("dve", lambda e: e.memset(eps4b[:], 4.0 * EPS), writes=["eps4b"])

    seqS = []
    for Tt_ in range(8):
        for f in range(NF):
            seqS += [8 + 2 * f, 8 + 2 * f + 1]
        for f in range(NF):
            seqS.append(52 + f)
    PREF = 6
    emitted = [0]
    usectr = [0]

    def ring_next():
        n = usectr[0]
        usectr[0] += 1
        while emitted[0] <= min(n + PREF, len(seqS) - 1):
            m = emitted[0]
            emitted[0] += 1
            s_ = m % NS
            P.op("sp", lambda e, s_=s_, m=m: e.dma_start(out=ringS[s_][:], in_=tape[seqS[m]]),
                 writes=[("ringS", s_)], dma=True, lane=("ringS", s_))
        return n % NS

    def grp_load(first, count, tape0):
        for q in range(count):
            P.op("sp", lambda e, q=q: e.dma_start(out=grp[first + q][:], in_=tape[tape0 + q]),
                 writes=[("grp", first + q)], dma=True, lane=("grp", first + q))

    def ln_norm(x1, par, tt, epst=None):
        slot = ln_ctr[0] % NLS
        ln_ctr[0] += 1
        layernorm_stats(lambda c: x1[:, tt, 512 * c:512 * (c + 1)], 2, slot, ("x1", par, tt), epst)
        P.op("act", lambda e: e.activation(out=x1[:, tt, :], in_=x1[:, tt, :], func=AF.Identity,
                                           scale=lnmv[:, slot, 2:3], bias=lnmv[:, slot, 3:4]),
             reads=[("x1", par, tt), ("sd", slot), ("nmr", slot)], writes=[("x1", par, tt)])

    def ln_affine(x1, par, tt, gname, bname):
        P.op("pool", lambda e: e.tensor_tensor(out=x1[:, tt, :], in0=x1[:, tt, :], in1=lnp[gname][:], op=ALU.mult),
             reads=[("x1", par, tt), ("lnp", gname)], writes=[("x1", par, tt)])
        P.op("pool", lambda e: e.tensor_tensor(out=x1[:, tt, :], in0=x1[:, tt, :], in1=lnp[bname][:], op=ALU.add),
             reads=[("x1", par, tt), ("lnp", bname)], writes=[("x1", par, tt)])

    def ln_stage(x1, par, tt, stage, st):
        if stage == 0:
            st["slot"] = ln_ctr[0] % NLS
            ln_ctr[0] += 1
            slot = st["slot"]
            for c in range(2):
                P.op("dve", lambda e, c=c: e.bn_stats(out=lnst[:, slot, c, :], in_=x1[:, tt, 512 * c:512 * (c + 1)]),
                     reads=[("x1", par, tt)], writes=[("lnst", slot, c)])
            P.op("dve", lambda e: e.bn_aggr(out=lnmv[:, slot, 0:2], in_=lnst[:, slot, 0:2, :]),
                 reads=[("lnst", slot, 0), ("lnst", slot, 1)], writes=[("mv", slot)])
        elif stage == 1:
            slot = st["slot"]
            P.op("act", lambda e: e.activation(out=lnmv[:, slot, 2:3], in_=lnmv[:, slot, 1:2], func=AF.Sqrt, bias=epsb[:, 0:1]),
                 reads=[("mv", slot), "epsb"], writes=[("sd", slot)])
        elif stage == 2:
            slot = st["slot"]
            P.op("dve", lambda e: e.reciprocal(out=lnmv[:, slot, 2:3], in_=lnmv[:, slot, 2:3]), reads=[("sd", slot)], writes=[("sd", slot)])
            P.op("dve", lambda e: e.tensor_scalar(out=lnmv[:, slot, 3:4], in0=lnmv[:, slot, 0:1], scalar1=lnmv[:, slot, 2:3], scalar2=-1.0,
                                                  op0=ALU.mult, op1=ALU.mult),
                 reads=[("mv", slot), ("sd", slot)], writes=[("nmr", slot)])
        elif stage == 3:
            slot = st["slot"]
            P.op("act", lambda e: e.activation(out=x1[:, tt, :], in_=x1[:, tt, :], func=AF.Identity,
                                               scale=lnmv[:, slot, 2:3], bias=lnmv[:, slot, 3:4]),
                 reads=[("x1", par, tt), ("sd", slot), ("nmr", slot)], writes=[("x1", par, tt)])

    evc = [0]

    def cc_transposes(x1, par, tt, b0):
        for k in range(8):
            P.op("pe", lambda e, k=k: e.transpose(out=ps[:, b0 + k // 4, (k % 4) * 128:(k % 4 + 1) * 128],
                                                  in_=x1[:, tt, 128 * k:128 * (k + 1)], identity=identf[:]),
                 reads=[("x1", par, tt), "identf"], writes=[("ps", b0 + k // 4)])

    def cc_evac(i, b0, lnbase):
        for k in range(8):
            src = ps[:, b0 + k // 4, (k % 4) * 128:(k % 4 + 1) * 128]
            dst = CC[:, k, 128 * i:128 * (i + 1)]
            gcol = lnT[:, lnbase + k:lnbase + k + 1]
            bcol = lnT[:, lnbase + 8 + k:lnbase + 8 + k + 1]
            if (evc[0] + k // 4) % 2 == 0:
                P.op("act", lambda e, src=src, dst=dst, gcol=gcol, bcol=bcol: e.activation(out=dst, in_=src, func=AF.Identity, scale=gcol, bias=bcol),
                     reads=[("ps", b0 + k // 4), "lnT"], writes=[("CT", i, k)])
            else:
                P.op("dve", lambda e, src=src, dst=dst, gcol=gcol, bcol=bcol: e.tensor_scalar(out=dst, in0=src, scalar1=gcol, scalar2=bcol,
                                                                                          op0=ALU.mult, op1=ALU.add),
                     reads=[("ps", b0 + k // 4), "lnT"], writes=[("CT", i, k)])
        evc[0] += 1

    def transpose_to_cc(x1, par, tt, i, b0, lnbase):
        for k in range(8):
            P.op("pe", lambda e, k=k: e.transpose(out=ps[:, b0 + k // 4, (k % 4) * 128:(k % 4 + 1) * 128],
                                                  in_=x1[:, tt, 128 * k:128 * (k + 1)], identity=identf[:]),
                 reads=[("x1", par, tt), "identf"], writes=[("ps", b0 + k // 4)])
        for k in range(8):
            src = ps[:, b0 + k // 4, (k % 4) * 128:(k % 4 + 1) * 128]
            dst = CC[:, k, 128 * i:128 * (i + 1)]
            gcol = lnT[:, lnbase + k:lnbase + k + 1]
            bcol = lnT[:, lnbase + 8 + k:lnbase + 8 + k + 1]
            if (evc[0] + k // 4) % 2 == 0:
                P.op("act", lambda e, src=src, dst=dst, gcol=gcol, bcol=bcol: e.activation(out=dst, in_=src, func=AF.Identity, scale=gcol, bias=bcol),
                     reads=[("ps", b0 + k // 4), "lnT"], writes=[("CT", i, k)])
            else:
                P.op("dve", lambda e, src=src, dst=dst, gcol=gcol, bcol=bcol: e.tensor_scalar(out=dst, in0=src, scalar1=gcol, scalar2=bcol,
                                                                                          op0=ALU.mult, op1=ALU.add),
                     reads=[("ps", b0 + k // 4), "lnT"], writes=[("CT", i, k)])
        evc[0] += 1

    def p_loads(Tt):
        for tt in range(4):
            i = 4 * Tt + tt
            P.op("pool", lambda e, i=i, tt=tt: e.dma_start(out=pbt[tt][:], in_=p_d[128 * i:128 * (i + 1), :]),
                 writes=[("pbt", tt)], dma=True, lane=("pbt", tt))

    def stB_main(Tt, tt, b0):
        par = Tt % 2
        x1 = x1s[par]
        i = 4 * Tt + tt
        xj = 0
        P.op("sp", lambda e, i=i, xj=xj: e.dma_start(out=xr[xj][:], in_=x_d[128 * i:128 * (i + 1), :]),
             writes=[("xr", xj)], dma=True, lane=("xr", xj))
        for hf in range(2):
            for k in range(8):
                P.op("pe", lambda e, hf=hf, k=k, i=i, b0=b0: e.matmul(ps[:, b0 + hf, :], lhsT=CC[:, k, 128 * i:128 * (i + 1)],
                                                                   rhs=grp[k][:, 512 * hf:512 * (hf + 1)],
                                                                   start=(k == 0), stop=(k == 7)),
                     reads=[("CT", i, k), ("grp", k)], writes=[("ps", b0 + hf)])
        P.op("dve", lambda e, tt=tt, b0=b0, xj=xj, x1=x1: e.scalar_tensor_tensor(
            out=x1[:, tt, :].rearrange("p (a b) -> p a b", a=2), in0=xr[xj][:].rearrange("p (a b) -> p a b", a=2), scalar=ALPHA,
            in1=ps[:, b0:b0 + 2, :], op0=ALU.mult, op1=ALU.add),
            reads=[("xr", xj), ("ps", b0), ("ps", b0 + 1)], writes=[("x1", par, tt)])
        ln_norm(x1, par, tt)

    def stB_post(Tt, tt, b0):
        par = Tt % 2
        x1 = x1s[par]
        transpose_to_cc(x1, par, tt, 4 * Tt + tt, b0, 0)
        ln_affine(x1, par, tt, "ln1_g", "ln1_b")

    def ffn_up_chunk(Tt, f):
        tok0 = 512 * Tt
        sa = ring_next()
        sbb = ring_next()
        ba = f % 2
        bb = 2 + f % 2
        aj = f % 2
        for (bank, slot) in ((ba, sa), (bb, sbb)):
            for kc in range(8):
                P.op("pe", lambda e, bank=bank, slot=slot, kc=kc, tok0=tok0: e.matmul(
                    ps[:, bank, :], lhsT=ringS[slot][:, 128 * kc:128 * (kc + 1)], rhs=CC[:, kc, tok0:tok0 + 512],
                    start=(kc == 0), stop=(kc == 7)),
                    reads=[("ringS", slot)] + [("CT", 4 * Tt + q, kc) for q in range(4)], writes=[("ps", bank)])
        P.op("dve", lambda e, aj=aj, f=f: e.tensor_copy(out=abuf[aj][:, 0:2], in_=halo[:, f, :]),
             reads=[("halo", f), "halo"], writes=[("abh", aj)])
        P.op("act", lambda e, aj=aj, ba=ba: e.copy(out=abuf[aj][:, 2:514], in_=ps[:, ba, :]),
             reads=[("ps", ba)], writes=[("ab", aj)])
        P.op("dve", lambda e, aj=aj, f=f: e.tensor_copy(out=halo[:, f, :], in_=abuf[aj][:, 512:514]),
             reads=[("ab", aj)], writes=[("halo", f)])
        P.op("act", lambda e, aj=aj, ba=ba, f=f: e.activation(out=cacc[aj][:], in_=ps[:, ba, :], func=AF.Identity,
                                                              scale=cw[:, 2 * NF + f:2 * NF + f + 1], bias=cb[:, f:f + 1]),
             reads=[("ps", ba), "cw", "cb"], writes=[("cacc", aj)])
        P.op("dve", lambda e, aj=aj, f=f: e.scalar_tensor_tensor(out=cacc[aj][:], in0=abuf[aj][:, 1:513], scalar=cw[:, NF + f:NF + f + 1],
                                                                in1=cacc[aj][:], op0=ALU.mult, op1=ALU.add),
             reads=[("ab", aj), ("abh", aj), ("cacc", aj), "cw"], writes=[("cacc", aj)])
        P.op("dve", lambda e, aj=aj, f=f: e.scalar_tensor_tensor(out=cacc[aj][:], in0=abuf[aj][:, 0:512], scalar=cw[:, f:f + 1],
                                                                in1=cacc[aj][:], op0=ALU.mult, op1=ALU.add),
             reads=[("ab", aj), ("abh", aj), ("cacc", aj), "cw"], writes=[("cacc", aj)])
        P.op("act", lambda e, aj=aj: e.activation(out=cacc[aj][:], in_=cacc[aj][:], func=AF.Gelu),
             reads=[("cacc", aj)], writes=[("cacc", aj)])
        P.op("dve", lambda e, aj=aj, bb=bb, f=f: e.tensor_tensor(out=gT[:, f, :], in0=cacc[aj][:], in1=ps[:, bb, :], op=ALU.mult),
             reads=[("cacc", aj), ("ps", bb)], writes=[("gT", f)])

    def p_transposes():
        for tt in range(4):
            for kc in range(2):
                P.op("pe", lambda e, tt=tt, kc=kc: e.transpose(out=psb[:, 4, (tt * 2 + kc) * 128:(tt * 2 + kc + 1) * 128],
                                                               in_=pbt[tt][:, 128 * kc:128 * (kc + 1)], identity=identb[:]),
                     reads=[("pbt", tt), "identb"], writes=[("ps", 4)])
        P.op("dve", lambda e: e.tensor_copy(out=pT[:].rearrange("p k (t c) -> p k t c", c=128),
                                            in_=psb[:, 4, :].rearrange("p (t k c) -> p k t c", t=4, k=2)),
             reads=[("ps", 4)], writes=["pT"])

    def ffn_down(Tt):
        for f in range(NF):
            sd = ring_next()
            for tt in range(4):
                for hf in range(2):
                    P.op("pe", lambda e, f=f, tt=tt, hf=hf, sd=sd: e.matmul(ps[:, 2 * tt + hf, :], lhsT=gT[:, f, 128 * tt:128 * (tt + 1)],
                                                                        rhs=ringS[sd][:, 512 * hf:512 * (hf + 1)],
                                                                        start=(f == 0), stop=(f == NF - 1)),
                         reads=[("gT", f), ("ringS", sd)], writes=[("ps", 2 * tt + hf)])

    def ln2_main(Tt, tt):
        par = Tt % 2
        x1 = x1s[par]
        b0 = 2 * tt
        P.op("dve", lambda e, tt=tt, b0=b0, x1=x1: e.scalar_tensor_tensor(
            out=x1[:, tt, :].rearrange("p (a b) -> p a b", a=2), in0=x1[:, tt, :].rearrange("p (a b) -> p a b", a=2), scalar=ALPHA,
            in1=ps[:, b0:b0 + 2, :], op0=ALU.mult, op1=ALU.add),
            reads=[("x1", par, tt), ("ps", b0), ("ps", b0 + 1)], writes=[("x1", par, tt)])
        ln_norm(x1, par, tt)

    def ln2_post(Tt, tt):
        par = Tt % 2
        x1 = x1s[par]
        transpose_to_cc(x1, par, tt, 4 * Tt + tt, 2 * tt, 16)
        ln_affine(x1, par, tt, "ln2_g", "ln2_b")

    def gate_a(Tt, tt):
        par = Tt % 2
        x1 = x1s[par]
        i = 4 * Tt + tt
        bg_, bp_ = 4 * (tt % 2), 4 * (tt % 2) + 2
        for hf in range(2):
            for k in range(8):
                P.op("pe", lambda e, hf=hf, k=k, i=i, bg_=bg_: e.matmul(ps[:, bg_ + hf, :], lhsT=CC[:, k, 128 * i:128 * (i + 1)],
                                                                     rhs=grp[k][:, 512 * hf:512 * (hf + 1)],
                                                                     start=(k == 0), stop=(k == 7)),
                     reads=[("CT", i, k), ("grp", k)], writes=[("ps", bg_ + hf)])
        for hf in range(2):
            for k in range(2):
                P.op("pe", lambda e, hf=hf, k=k, tt=tt, bp_=bp_: e.matmul(ps[:, bp_ + hf, :], lhsT=pT[:, k, 128 * tt:128 * (tt + 1)],
                                                                      rhs=grp[8 + k][:, 512 * hf:512 * (hf + 1)],
                                                                      start=(k == 0), stop=(k == 1)),
                     reads=["pT", ("grp", 8 + k)], writes=[("ps", bp_ + hf)])
        gt = gtmp[0]
        P.op("dve", lambda e, gt=gt, bg_=bg_: e.tensor_tensor(out=gt[:].rearrange("p (a b) -> p a b", a=2), in0=ps[:, bg_:bg_ + 2, :],
                                                          in1=lnp["b_ple_gate"][:].rearrange("p (a b) -> p a b", a=2), op=ALU.add),
             reads=[("ps", bg_), ("ps", bg_ + 1), ("lnp", "b_ple_gate")], writes=[("gtmp", 0)])
        P.op("act", lambda e, gt=gt: e.activation(out=gt[:], in_=gt[:], func=AF.Tanh, scale=0.5), reads=[("gtmp", 0)], writes=[("gtmp", 0)])
        P.op("dve", lambda e, gt=gt, bp_=bp_: e.scalar_tensor_tensor(out=gt[:].rearrange("p (a b) -> p a b", a=2),
                                                                  in0=gt[:].rearrange("p (a b) -> p a b", a=2), scalar=1.0,
                                                                  in1=ps[:, bp_:bp_ + 2, :], op0=ALU.add, op1=ALU.mult),
             reads=[("gtmp", 0), ("ps", bp_), ("ps", bp_ + 1)], writes=[("gtmp", 0)])
        P.op("dve", lambda e, gt=gt, tt=tt, x1=x1: e.scalar_tensor_tensor(out=x1[:, tt, :], in0=x1[:, tt, :], scalar=2.0 * ALPHA, in1=gt[:],
                                                                       op0=ALU.mult, op1=ALU.add),
             reads=[("x1", par, tt), ("gtmp", 0)], writes=[("x1", par, tt)])

    def gate_b(Tt, tt):
        par = Tt % 2
        x1 = x1s[par]
        i = 4 * Tt + tt
        ln_norm(x1, par, tt, eps4b)
        ln_affine(x1, par, tt, "ln3_g", "ln3_b")
        o = P.op("pool", lambda e, i=i, tt=tt, x1=x1: e.dma_start(out=out_d[128 * i:128 * (i + 1), :], in_=x1[:, tt, :]),
                 reads=[("x1", par, tt)], writes=[("out", i)], dma=True, lane=("o", par, tt))
        P.final_waits.append(o.idx)

    bset = [0]

    def ln_batch(x1, par, tts, epst):
        k = bset[0] % 2
        bset[0] += 1
        n = len(tts)
        st, mv = lnstB[k], lnmvB[k]

        def b0():
            for j, tt in enumerate(tts):
                for c in range(2):
                    P.op("dve", lambda e, j=j, tt=tt, c=c: e.bn_stats(out=st[:, j, c, :], in_=x1[:, tt, 512 * c:512 * (c + 1)]),
                         reads=[("x1", par, tt)], writes=[("lnBst", k, j, c)])
                P.op("dve", lambda e, j=j: e.bn_aggr(out=mv[:, j, 0:2], in_=st[:, j, :, :]),
                     reads=[("lnBst", k, j, 0), ("lnBst", k, j, 1)], writes=[("lnBmv", k, j)])

        def b1():
            P.op("act", lambda e: e.activation(out=mv[:, 0:n, 2:3], in_=mv[:, 0:n, 1:2], func=AF.Sqrt, bias=epst[:, 0:1]),
                 reads=[("lnBmv", k, j) for j in range(n)] + ["epsb", "eps4b"], writes=[("lnBsd", k)])

        def b2():
            P.op("dve", lambda e: e.reciprocal(out=mv[:, 0:n, 2:3], in_=mv[:, 0:n, 2:3]), reads=[("lnBsd", k)], writes=[("lnBsd", k)])
            P.op("dve", lambda e: e.scalar_tensor_tensor(out=mv[:, 0:n, 3:4], in0=mv[:, 0:n, 0:1], scalar=-1.0, in1=mv[:, 0:n, 2:3],
                                                         op0=ALU.mult, op1=ALU.mult),
                 reads=[("lnBmv", k, j) for j in range(n)] + [("lnBsd", k)], writes=[("lnBnm", k)])

        def b3():
            for j, tt in enumerate(tts):
                P.op("act", lambda e, j=j, tt=tt: e.activation(out=x1[:, tt, :], in_=x1[:, tt, :], func=AF.Identity,
                                                               scale=mv[:, j, 2:3], bias=mv[:, j, 3:4]),
                     reads=[("x1", par, tt), ("lnBsd", k), ("lnBnm", k)], writes=[("x1", par, tt)])
        return [b0, b1, b2, b3]

    def stB_wo(Tt, tt, b0m, load_x):
        i = 4 * Tt + tt
        if load_x:
            stB_xload(Tt, tt)
        for hf in range(2):
            for k in range(8):
                P.op("pe", lambda e, hf=hf, k=k, i=i: e.matmul(ps[:, b0m + hf, :], lhsT=CC[:, k, 128 * i:128 * (i + 1)],
                                                              rhs=grp[k][:, 512 * hf:512 * (hf + 1)], start=(k == 0), stop=(k == 7)),
                     reads=[("CT", i, k), ("grp", k)], writes=[("ps", b0m + hf)])

    def stB_xload(Tt, tt):
        i = 4 * Tt + tt
        P.op("sp", lambda e, i=i: e.dma_start(out=xr[0][:], in_=x_d[128 * i:128 * (i + 1), :]),
             writes=[("xr", 0)], dma=True, lane=("xr", 0))

    def stB_stt(Tt, tt, b0m):
        par = Tt % 2
        x1 = x1s[par]
        P.op("dve", lambda e: e.scalar_tensor_tensor(
            out=x1[:, tt, :].rearrange("p (a b) -> p a b", a=2), in0=xr[0][:].rearrange("p (a b) -> p a b", a=2), scalar=ALPHA,
            in1=ps[:, b0m:b0m + 2, :], op0=ALU.mult, op1=ALU.add),
            reads=[("xr", 0), ("ps", b0m), ("ps", b0m + 1)], writes=[("x1", par, tt)])

    def stB_tr(Tt, tt, b0p):
        par = Tt % 2
        cc_transposes(x1s[par], par, tt, b0p)

    def stB_ev(Tt, tt, b0p):
        par = Tt % 2
        cc_evac(4 * Tt + tt, b0p, 0)
        ln_affine(x1s[par], par, tt, "ln1_g", "ln1_b")

    def gate_b_out(Tt, tt):
        par = Tt % 2
        x1 = x1s[par]
        i = 4 * Tt + tt
        ln_affine(x1, par, tt, "ln3_g", "ln3_b")
        o = P.op("pool", lambda e: e.dma_start(out=out_d[128 * i:128 * (i + 1), :], in_=x1[:, tt, :]),
                 reads=[("x1", par, tt)], writes=[("out", i)], dma=True, lane=("o", par, tt))
        P.final_waits.append(o.idx)

    grp_load(0, 8, 0)
    p_loads(0)
    stB_main(0, 0, 4)
    stB_main(0, 1, 6)
    stB_post(0, 0, 0)
    stB_main(0, 2, 4)
    stB_post(0, 1, 2)
    stB_main(0, 3, 6)
    stB_post(0, 2, 0)
    stB_post(0, 3, 2)

    for Tt in range(8):
        nxt = Tt + 1 < 8
        extra = {}
        tail_extra = []

        def at(f, fn):
            extra.setdefault(f, []).append(fn)
        if Tt > 0:
            pp = (Tt - 1) % 2
            gb = ln_batch(x1s[pp], pp, [0, 1, 2, 3], eps4b)
            for q in range(4):
                at(q, gb[q])
            for tt in range(4):
                at(4 + tt, lambda tt=tt, Tt=Tt: gate_b_out(Tt - 1, tt))
        if nxt:
            extra.setdefault(0, []).insert(0, lambda Tt=Tt: grp_load(0, 8, 0))
            np_ = (Tt + 1) % 2
            N1 = Tt + 1
            at(6, lambda: stB_wo(N1, 0, 4, True))
            at(7, lambda: stB_wo(N1, 1, 6, False))
            lbA = ln_batch(x1s[np_], np_, [0, 1], epsb)
            at(8, lambda: stB_stt(N1, 0, 4))
            at(8, lambda: stB_xload(N1, 1))
            at(10, lambda: stB_stt(N1, 1, 6))
            at(10, lbA[0])
            at(11, lbA[1])
            at(12, lbA[2])
            at(13, lbA[3])
            at(14, lambda: stB_tr(N1, 0, 4))
            at(14, lambda: stB_tr(N1, 1, 6))
            at(15, lambda: stB_ev(N1, 0, 4))
            at(16, lambda: stB_ev(N1, 1, 6))
            at(16, lambda: stB_wo(N1, 2, 4, True))
            at(17, lambda: stB_wo(N1, 3, 6, False))
            lbB = ln_batch(x1s[np_], np_, [2, 3], epsb)
            at(18, lambda: stB_stt(N1, 2, 4))
            at(18, lambda: stB_xload(N1, 3))
            at(19, lambda: stB_stt(N1, 3, 6))
            at(19, lbB[0])
            at(20, lbB[1])
            at(21, lbB[2])
            at(21, lbB[3])
            tail_extra = [lambda: stB_tr(N1, 2, 4), lambda: stB_tr(N1, 3, 6), lambda: stB_ev(N1, 2, 4), lambda: stB_ev(N1, 3, 6)]
        if Tt > 0:
            p_loads(Tt)
        for f in range(NF):
            ffn_up_chunk(Tt, f)
            for fn in extra.get(f, []):
                fn()
        for fn in tail_extra:
            fn()
        grp_load(0, 10, 74)
        p_transposes()
        ffn_down(Tt)
        ln2_main(Tt, 0)
        ln2_main(Tt, 1)
        ln2_post(Tt, 0)
        ln2_main(Tt, 2)
        ln2_post(Tt, 1)
        ln2_main(Tt, 3)
        ln2_post(Tt, 2)
        ln2_post(Tt, 3)
        for tt in range(4):
            gate_a(Tt, tt)
    for tt in range(4):
        gate_b(7, tt)
    P.emit()
    return nc


def _consts():
    j = np.arange(128)[:, None]
    i = np.arange(128)[None, :]
    maskL = (j <= i).astype(np.float32)
    maskU = (j >= i).astype(np.float32)
    ident = np.eye(128, dtype=np.float32)
    inv8 = np.float32(500000.0) ** (-(np.arange(0, 16, 2, dtype=np.float32)) / np.float32(16.0))
    ropec = np.zeros((128, 2), np.float32)
    for p_ in range(128):
        d = p_ % 64
        if d < 16:
            ropec[p_, 0] = inv8[d % 8]
            ropec[p_, 1] = -1.0 if d < 8 else 1.0
    return ident, maskL, maskU, ropec


def _prep_shared(inp):
    f = lambda a: np.ascontiguousarray(np.asarray(a, dtype=np.float32))
    w_in = f(inp["w_in"][0])
    perm = np.arange(1024)
    for h in range(16):
        base = h * 64
        perm[base:base + 8] = np.arange(base + 8, base + 16)
        perm[base + 8:base + 16] = np.arange(base, base + 8)
    w_sw = np.ascontiguousarray(w_in[:, :1024][:, perm])
    conv_w = f(inp["conv_w"][0])
    conv_b = f(inp["conv_b"][0])
    cw = np.ascontiguousarray(conv_w.reshape(3, NF, 128).transpose(2, 0, 1).reshape(128, 3 * NF))
    cb = np.ascontiguousarray(conv_b.reshape(NF, 128).T)
    ident, maskL, maskU, ropec = _consts()
    sh = {
        "w_in": w_in, "w_sw": w_sw,
        "ln_z_g": f(inp["ln_z_g"]), "ln_z_b": f(inp["ln_z_b"]),
        "w_sT": np.ascontiguousarray(f(inp["w_s"][0]).transpose(0, 2, 1)),
        "b_s": f(inp["b_s"][0]),
        "w_o": f(inp["w_o"][0]),
        "w_ff_a": f(inp["w_ff_a"][0]), "w_ff_b": f(inp["w_ff_b"][0]),
        "cw": cw, "cb": cb,
        "w_ff_down": f(inp["w_ff_down"][0]),
        "w_ple_gate": f(inp["w_ple_gate"][0]),
        "w_ple_in": f(inp["w_ple_in"][0]),
        "ident": ident, "maskL": maskL, "maskU": maskU, "ropec": ropec,
    }
    for nm in ("ln1_g", "ln1_b", "ln2_g", "ln2_b", "ln3_g", "ln3_b", "b_ple_gate"):
        sh[nm] = f(inp[nm])
    lnT = np.zeros((128, 32), np.float32)
    for j, (gn, bn) in enumerate((("ln1_g", "ln1_b"), ("ln2_g", "ln2_b"))):
        lnT[:, 16 * j:16 * j + 8] = sh[gn].reshape(8, 128).T
        lnT[:, 16 * j + 8:16 * j + 16] = sh[bn].reshape(8, 128).T
    sh["lnT"] = np.ascontiguousarray(lnT)
    return sh


_NC_CACHE = {}


def kernel(**inputs):
    sh = _prep_shared(inputs)
    x = np.asarray(inputs["x"], dtype=np.float32)
    p = np.asarray(inputs["p"], dtype=np.float32)
    pos = np.asarray(inputs["positions"], dtype=np.int32)
    in_maps = []
    for b in range(8):
        m = dict(sh)
        m["x"] = np.ascontiguousarray(x[b])
        m["p"] = np.ascontiguousarray(p[0, b])
        m["pos"] = np.ascontiguousarray(pos[b:b + 1])
        in_maps.append(m)
    nc = build()
    res = run_bass_kernel_spmd(nc, in_maps, core_ids=list(range(8)))
    out = np.stack([np.asarray(r["out"], dtype=np.float32) for r in res.results], axis=0)
    return out
```

```python
import numpy as np
import concourse.bass as bass
import concourse.mybir as mybir
from concourse.bass_utils import run_bass_kernel_spmd

F32 = mybir.dt.float32
BF16 = mybir.dt.bfloat16
I32 = mybir.dt.int32
AF = mybir.ActivationFunctionType
ALU = mybir.AluOpType

S = 4096
D = 1024
DFF = 2816
NF = 22
DPLE = 256
ALPHA = float(2.0 ** 0.25)
EPS = 1e-5
PI = float(np.pi)
TWO_PI = float(2 * np.pi)
KB = 1024


class Op:
    __slots__ = ("idx", "eng", "fn", "deps", "is_dma", "lane", "sig", "signals")

    def __init__(self, idx, eng, fn, is_dma, lane):
        self.idx = idx
        self.eng = eng
        self.fn = fn
        self.deps = set()
        self.is_dma = is_dma
        self.lane = lane
        self.sig = None
        self.signals = False


class Prog:
    ENGS = ("pe", "act", "dve", "pool", "sp")

    def __init__(self, nc):
        self.nc = nc
        self.ops = []
        self.state = {}
        self.final_waits = []
        self.last_eng = {}
        self.last_lane = {}
        self.pending_fence = {}

    def op(self, eng, fn, reads=(), writes=(), dma=False, lane=None):
        o = Op(len(self.ops), eng, fn, dma, lane)
        self.ops.append(o)
        pf = self.pending_fence.pop(eng, None)
        if pf is not None:
            o.deps.update(pf)
        for k in reads:
            st = self.state.get(k)
            if st is None:
                st = [None, []]
                self.state[k] = st
            if st[0] is not None:
                self._dep(o, st[0], "raw")
            if isinstance(k, tuple) and k[0] == "ps":
                for r in st[1]:
                    if self.ops[r].eng != eng:
                        o.deps.add(r)
            if not dma:
                st[1] = [r for r in st[1] if self.ops[r].is_dma or self.ops[r].eng != eng]
            st[1].append(o.idx)
        for k in writes:
            st = self.state.get(k)
            if st is None:
                st = [None, []]
                self.state[k] = st
            if st[0] is not None:
                self._dep(o, st[0], "waw")
            for r in st[1]:
                if r != o.idx:
                    self._dep(o, r, "war")
            st[0] = o.idx
            st[1] = []
        if dma:
            self.last_lane[tuple(lane)] = o.idx
        else:
            self.last_eng[eng] = o.idx
        return o

    def _dep(self, o, j, kind):
        t = self.ops[j]
        if t.eng == o.eng and not t.is_dma and not o.is_dma:
            if o.eng == "pe":
                return
        o.deps.add(j)

    def fence(self):
        deps = set(self.last_eng.values()) | set(self.last_lane.values())
        for e in self.ENGS:
            self.pending_fence[e] = set(deps)
        self.state = {}

    def emit(self):
        nc = self.nc
        ops = self.ops
        for o in ops:
            o.deps.discard(o.idx)
            if o.is_dma:
                o.signals = True
            for j in o.deps:
                ops[j].signals = True
        for j in self.final_waits:
            ops[j].signals = True
        sems = {}
        counters = {}
        for o in ops:
            if not o.signals:
                continue
            key = ("dma",) + tuple(o.lane) if o.is_dma else ("eng", o.eng)
            if key not in sems:
                sems[key] = nc.alloc_semaphore("s_" + "_".join(str(x) for x in key))
                counters[key] = 0
            counters[key] += 16 if o.is_dma else 1
            o.sig = (key, counters[key])
        streams = {e: [o for o in ops if o.eng == e] for e in self.ENGS}
        final = [ops[j].sig for j in self.final_waits]

        def run_stream(e, engh):
            waited = {}
            for o in streams[e]:
                need = {}
                for j in o.deps:
                    t = ops[j]
                    if t.eng == e and not t.is_dma and not o.is_dma and e == "pe":
                        continue
                    k, v = t.sig
                    if need.get(k, 0) < v:
                        need[k] = v
                for k, v in need.items():
                    if waited.get(k, 0) >= v:
                        continue
                    engh.wait_ge(sems[k], v)
                    waited[k] = v
                ins = o.fn(engh)
                if o.signals:
                    ins.then_inc(sems[o.sig[0]], 16 if o.is_dma else 1)
            if e == "sp":
                need = {}
                for k, v in final:
                    if need.get(k, 0) < v:
                        need[k] = v
                for k, v in need.items():
                    engh.wait_ge(sems[k], v)

        with nc.Block() as block:
            @block.tensor
            def _(pe):
                run_stream("pe", pe)

            @block.scalar
            def _(act):
                run_stream("act", act)

            @block.vector
            def _(dve):
                run_stream("dve", dve)

            @block.gpsimd
            def _(pool):
                run_stream("pool", pool)

            @block.sync
            def _(sp):
                run_stream("sp", sp)


def build(stop_after=None):
    nc = bass.Bass("TRN2", target_bir_lowering=False)
    P = Prog(nc)

    def din(name, shape, dt=F32):
        return nc.dram_tensor(name, list(shape), dt, kind="ExternalInput").ap()

    x_d = din("x", [S, D])
    p_d = din("p", [S, DPLE])
    pos_d = din("pos", [1, S], I32)
    w_in_d = din("w_in", [D, 2560])
    w_sw_d = din("w_sw", [D, 1024])
    lzg_d = din("ln_z_g", [1, 512])
    lzb_d = din("ln_z_b", [1, 512])
    wsT_d = din("w_sT", [8, 128, 128])
    bs_d = din("b_s", [8, 128])
    wo_d = din("w_o", [D, D])
    ln_d = {}
    for nm in ("ln1_g", "ln1_b", "ln2_g", "ln2_b", "ln3_g", "ln3_b", "b_ple_gate"):
        ln_d[nm] = din(nm, [1, D])
    wa_d = din("w_ff_a", [D, DFF])
    wb_d = din("w_ff_b", [D, DFF])
    cw_d = din("cw", [128, 3 * NF])
    cb_d = din("cb", [128, NF])
    wd_d = din("w_ff_down", [DFF, D])
    wg_d = din("w_ple_gate", [D, D])
    wp_d = din("w_ple_in", [DPLE, D])
    ident_d = din("ident", [128, 128])
    mL_d = din("maskL", [128, 128])
    mU_d = din("maskU", [128, 128])
    ropec_d = din("ropec", [128, 2])
    lnT_d = din("lnT", [128, 32])
    if stop_after == "A":
        out_d = nc.dram_tensor("out", [128, 8 * S], BF16, kind="ExternalOutput").ap()
    elif stop_after in ("B", "D", "G"):
        out_d = nc.dram_tensor("out", [128, 4 * D], F32, kind="ExternalOutput").ap()
    elif stop_after == "F":
        out_d = nc.dram_tensor("out", [128, NF * 512], BF16, kind="ExternalOutput").ap()
    else:
        out_d = nc.dram_tensor("out", [S, D], F32, kind="ExternalOutput").ap()

    tape = nc.dram_tensor("tape", [84, 128, D], BF16).ap()
    chunks84 = []
    for k in range(8):
        chunks84.append((wo_d[128 * k:128 * (k + 1), :], False))
    for f in range(NF):
        chunks84.append((wa_d[:, 128 * f:128 * (f + 1)], True))
        chunks84.append((wb_d[:, 128 * f:128 * (f + 1)], True))
    for f in range(NF):
        chunks84.append((wd_d[128 * f:128 * (f + 1), :], False))
    for k in range(8):
        chunks84.append((wg_d[128 * k:128 * (k + 1), :], False))
    for k in range(2):
        chunks84.append((wp_d[128 * k:128 * (k + 1), :], False))

    def sb(name, shape, dt, off):
        assert off % 32 == 0, (name, off)
        return nc.alloc_sbuf_tensor_at(name, list(shape), dt, offset=int(off))

    ps = nc.alloc_psum_tensor("ps", [128, 8, 512], F32)
    psb = ps.bitcast(BF16)

    CC = sb("CC", [128, 8, S], BF16, 16 * KB)
    TC = sb("TC", [128, S], F32, 48 * KB)
    TS = sb("TS", [128, S], F32, 64 * KB)
    c0 = 80 * KB
    identf = sb("identf", [128, 128], F32, c0)
    identb = sb("identb", [128, 128], BF16, c0 + 512)
    mL = sb("mL", [128, 128], BF16, c0 + 768)
    mU = sb("mU", [128, 128], BF16, c0 + 1024)
    ropec = sb("ropec", [128, 2], F32, c0 + 1280)
    lnst = sb("lnst", [128, 4, 2, 6], F32, c0 + 1312)
    lnmv = sb("lnmv", [128, 4, 4], F32, c0 + 1312 + 192)

    P.op("sp", lambda e: e.dma_start(out=identf[:], in_=ident_d), writes=["identf"], dma=True, lane=("c", 0))
    P.op("pool", lambda e: e.dma_start(out=identb[:], in_=ident_d), writes=["identb"], dma=True, lane=("c", 1))
    P.op("pool", lambda e: e.dma_start(out=mL[:], in_=mL_d), writes=["mL"], dma=True, lane=("c", 2))
    P.op("pool", lambda e: e.dma_start(out=mU[:], in_=mU_d), writes=["mU"], dma=True, lane=("c", 3))
    P.op("sp", lambda e: e.dma_start(out=ropec[:], in_=ropec_d), writes=["ropec"], dma=True, lane=("c", 4))

    e0 = 146 * KB
    HT = 1024
    pos_i = sb("pos_i", [128, HT], I32, e0)
    pos_f = sb("pos_f", [128, HT], F32, e0 + 4 * KB)
    ang = sb("ang", [128, HT], F32, e0 + 8 * KB)
    a2 = sb("a2", [128, HT], F32, e0 + 12 * KB)
    ki = sb("ki", [128, HT], I32, e0 + 16 * KB)
    rr = sb("rr", [128, HT], F32, e0 + 20 * KB)
    tt_ = sb("tt_", [128, HT], F32, e0 + 24 * KB)
    C1 = 6.28125
    C2 = TWO_PI - C1
    for h in range(4):
        cs = slice(h * HT, (h + 1) * HT)
        P.op("sp", lambda e, cs=cs: e.dma_start(out=pos_i[:], in_=pos_d[0, cs].partition_broadcast(128)),
             writes=["pos_i"], dma=True, lane=("c", 5))
        P.op("dve", lambda e: e.tensor_copy(out=pos_f[:], in_=pos_i[:]), reads=["pos_i"], writes=["pos_f"])
        P.op("dve", lambda e: e.tensor_scalar(out=ang[:], in0=pos_f[:], scalar1=ropec[:, 0:1], scalar2=None, op0=ALU.mult),
             reads=["pos_f", "ropec"], writes=["ang"])
        for tab, shift in ((TS, 0.0), (TC, PI / 2)):
            if shift != 0.0:
                P.op("dve", lambda e, shift=shift: e.tensor_scalar(out=a2[:], in0=ang[:], scalar1=shift, scalar2=None, op0=ALU.add),
                     reads=["ang"], writes=["a2"])
                src = a2
                srck = "a2"
            else:
                src = ang
                srck = "ang"
            P.op("dve", lambda e, src=src: e.tensor_scalar(out=ki[:], in0=src[:], scalar1=1.0 / TWO_PI, scalar2=None, op0=ALU.mult),
                 reads=[srck], writes=["ki"])
            P.op("dve", lambda e, src=src: e.scalar_tensor_tensor(out=rr[:], in0=ki[:], scalar=-C1, in1=src[:], op0=ALU.mult, op1=ALU.add),
                 reads=["ki", srck], writes=["rr"])
            P.op("dve", lambda e: e.scalar_tensor_tensor(out=rr[:], in0=ki[:], scalar=-C2, in1=rr[:], op0=ALU.mult, op1=ALU.add),
                 reads=["ki", "rr"], writes=["rr"])
            P.op("dve", lambda e: e.tensor_scalar(out=tt_[:], in0=rr[:], scalar1=PI, scalar2=TWO_PI, op0=ALU.is_gt, op1=ALU.mult),
                 reads=["rr"], writes=["tt_"])
            P.op("dve", lambda e: e.tensor_tensor(out=rr[:], in0=rr[:], in1=tt_[:], op=ALU.subtract), reads=["rr", "tt_"], writes=["rr"])
            P.op("dve", lambda e: e.tensor_scalar(out=tt_[:], in0=rr[:], scalar1=-PI, scalar2=TWO_PI, op0=ALU.is_lt, op1=ALU.mult),
                 reads=["rr"], writes=["tt_"])
            P.op("dve", lambda e: e.tensor_tensor(out=rr[:], in0=rr[:], in1=tt_[:], op=ALU.add), reads=["rr", "tt_"], writes=["rr"])
            P.op("dve", lambda e: e.tensor_scalar(out=rr[:], in0=rr[:], scalar1=PI, scalar2=-PI, op0=ALU.min, op1=ALU.max),
                 reads=["rr"], writes=["rr"])
            if shift == 0.0:
                P.op("act", lambda e, tab=tab, cs=cs: e.activation(out=tab[:, cs], in_=rr[:], func=AF.Sin, scale=ropec[:, 1:2]),
                     reads=["rr", "ropec"], writes=[("tab", "S", h)])
            else:
                P.op("act", lambda e, tab=tab, cs=cs: e.activation(out=tab[:, cs], in_=rr[:], func=AF.Sin),
                     reads=["rr"], writes=[("tab", "C", h)])

    P.fence()

    XT = sb("XT", [128, 8, S], BF16, 82 * KB)
    Qt = sb("Qt", [128, S], BF16, 146 * KB)
    Kt = sb("Kt", [128, S], BF16, 154 * KB)
    Vt = sb("Vt", [128, S], BF16, 162 * KB)
    wbuf = [sb("wbuf%d" % i, [128, 8, 128], BF16, 170 * KB + 2 * KB * i) for i in range(4)]
    t1 = [sb("t1_%d" % i, [128, 512], F32, 178 * KB + 4 * KB * i) for i in range(2)]
    t2 = [sb("t2_%d" % i, [128, 512], F32, 180 * KB + 4 * KB * i) for i in range(2)]
    xs = [sb("xs%d" % i, [128, D], BF16, 186 * KB + 2 * KB * i) for i in range(2)]
    NPT = 6
    PT = [sb("PT%d" % i, [128, 512], BF16, 190 * KB + KB * i) for i in range(NPT)]
    VB1 = sb("VB1", [128, 8, 3, 64], BF16, 196 * KB)
    VB4 = sb("VB4", [128, 8, 3, 64], BF16, 199 * KB)
    VB16 = sb("VB16", [128, 32, 3, 64], BF16, 202 * KB)
    rc = [sb("rc%d" % i, [128, 512], F32, 214 * KB + 2 * KB * i) for i in range(2)]

    for nm_, t in (("VB1", VB1), ("VB4", VB4), ("VB16", VB16)):
        P.op("pool", lambda e, t=t: e.memset(t[:, :, 1, :], 1.0), writes=[("vbones", nm_)])

    for i in range(32):
        j = i % 2
        P.op("pool", lambda e, i=i, j=j: e.dma_start(out=xs[j][:], in_=x_d[128 * i:128 * (i + 1), :]),
             writes=[("xs", j)], dma=True, lane=("xs", j))
        bk = 4 + j
        for k in range(8):
            P.op("pe", lambda e, j=j, k=k, bk=bk: e.transpose(out=psb[:, bk, k * 128:(k + 1) * 128],
                                                              in_=xs[j][:, k * 128:(k + 1) * 128], identity=identb[:]),
                 reads=[("xs", j), "identb"], writes=[("ps", bk)])
        eng = "act" if i % 2 == 0 else "dve"
        src = psb[:, bk, :].rearrange("p (k c) -> p k c", c=128)
        dst = XT[:, :, 128 * i:128 * (i + 1)]
        if eng == "act":
            P.op("act", lambda e, src=src, dst=dst: e.copy(out=dst, in_=src), reads=[("ps", bk)], writes=[("XT", i)])
        else:
            P.op("dve", lambda e, src=src, dst=dst: e.tensor_copy(out=dst, in_=src), reads=[("ps", bk)], writes=[("XT", i)])

    wcount = [0]

    def load_w(src_ap):
        s_ = wcount[0] % 4
        wcount[0] += 1
        P.op("pool", lambda e, s_=s_, src_ap=src_ap: e.dma_start(out=wbuf[s_][:], in_=src_ap.rearrange("(kc p) c -> p kc c", p=128)),
             writes=[("wb", s_)], dma=True, lane=("wb", s_))
        return s_

    def proj_mm(bank, slot, T8):
        for kc in range(8):
            P.op("pe", lambda e, bank=bank, slot=slot, kc=kc, T8=T8: e.matmul(
                ps[:, bank, :], lhsT=wbuf[slot][:, kc, :], rhs=XT[:, kc, 512 * T8:512 * (T8 + 1)],
                start=(kc == 0), stop=(kc == 7)),
                reads=[("wb", slot)] + [("XT", 4 * T8 + q) for q in range(4)], writes=[("ps", bank)])

    pt_ctr = [0]
    grp_ctr = [0]
    acc_ctr = [0]
    tr_ctr = [0]
    msk_ctr = [0]

    def build_vb(vbt, nm_, idx0, tok_ap_fns):
        cnt = len(tok_ap_fns)
        bk = tr_ctr[0] % 2
        tr_ctr[0] += 1
        for q, fn in enumerate(tok_ap_fns):
            P.op("pe", lambda e, q=q, fn=fn: e.transpose(out=psb[:, bk, q * 128:(q + 1) * 128], in_=fn(), identity=identb[:]),
                 reads=["Vall", "identb"], writes=[("ps", bk)])
        src = psb[:, bk, 0:cnt * 128].rearrange("p (q a b) -> p q a b", a=2, b=64)
        P.op("dve", lambda e: e.tensor_copy(out=vbt[:, idx0:idx0 + cnt, 0:3:2, :], in_=src), reads=[("ps", bk)],
             writes=[("vb", nm_, idx0 + q) for q in range(cnt)])

    for hp in range(4):
        for (dest, dkey, c_main, c_sw) in ((Qt, "Q", hp * 128, hp * 128), (Kt, "K", 512 + hp * 128, 512 + hp * 128)):
            sa = load_w(w_in_d[:, c_main:c_main + 128])
            sbw = load_w(w_sw_d[:, c_sw:c_sw + 128])
            for T8 in range(8):
                st_ = T8 % 2
                ba, bb = 2 * st_, 2 * st_ + 1
                proj_mm(ba, sa, T8)
                proj_mm(bb, sbw, T8)
                cs = slice(512 * T8, 512 * (T8 + 1))
                P.op("dve", lambda e, st_=st_, ba=ba, cs=cs: e.tensor_tensor(out=t1[st_][:], in0=ps[:, ba, :], in1=TC[:, cs], op=ALU.mult),
                     reads=[("ps", ba)] + [("tab", "C", h_) for h_ in range(4)], writes=[("t1", st_)])
                P.op("dve", lambda e, st_=st_, bb=bb, cs=cs: e.tensor_tensor(out=t2[st_][:], in0=ps[:, bb, :], in1=TS[:, cs], op=ALU.mult),
                     reads=[("ps", bb)] + [("tab", "S", h_) for h_ in range(4)], writes=[("t2", st_)])
                P.op("pool", lambda e, st_=st_, dest=dest, cs=cs: e.tensor_tensor(out=dest[:, cs], in0=t1[st_][:], in1=t2[st_][:], op=ALU.add),
                     reads=[("t1", st_), ("t2", st_)], writes=[(dkey, T8), dkey + "all"])
        sv = load_w(w_in_d[:, 1024 + hp * 128:1024 + (hp + 1) * 128])
        for T8 in range(8):
            bk = 2 * (T8 % 2)
            proj_mm(bk, sv, T8)
            cs = slice(512 * T8, 512 * (T8 + 1))
            P.op("act", lambda e, bk=bk, cs=cs: e.copy(out=Vt[:, cs], in_=ps[:, bk, :]), reads=[("ps", bk)], writes=["Vall"])

        for m in range(21 * hp, 21 * (hp + 1)):
            src_ap, view3 = chunks84[m]
            if view3:
                P.op("pool", lambda e, m=m, src_ap=src_ap: e.dma_start(out=tape[m].rearrange("p (kc c) -> p kc c", c=128),
                                                                   in_=src_ap.rearrange("(kc p) c -> p kc c", p=128)),
                     writes=[("tape", m)], dma=True, lane=("tape", m % 4))
            else:
                P.op("pool", lambda e, m=m, src_ap=src_ap: e.dma_start(out=tape[m], in_=src_ap),
                     writes=[("tape", m)], dma=True, lane=("tape", m % 4))

        for n16 in range(2):
            for r8 in range(2):
                build_vb(VB16, "VB16", n16 * 16 + r8 * 8,
                         [(lambda n16=n16, r16=r8 * 8 + q: Vt[:, 2048 * n16 + r16:2048 * (n16 + 1):16]) for q in range(8)])

        G = []
        for W in range(8):
            n16 = W // 4
            Wl = W % 4
            for hh in range(2):
                po = 64 * hh
                accb = 6 + acc_ctr[0] % 2
                acc_ctr[0] += 1
                ri = acc_ctr[0] % 2
                groups = []
                blk = []
                for b in range(4):
                    n = 4 * W + b
                    sl = slice(128 * n, 128 * (n + 1))
                    blk.append((sl, sl, slice(128 * b, 128 * (b + 1)), (VB1, 'VB1', n % 8), b))
                groups.append((blk, 128, "L"))
                blk = []
                for b in range(4):
                    n = 4 * W + b
                    if n == 0:
                        continue
                    blk.append((slice(128 * (n - 1), 128 * n), slice(128 * n, 128 * (n + 1)), slice(128 * b, 128 * (b + 1)),
                                (VB1, 'VB1', (n - 1) % 8), b))
                groups.append((blk, 128, "U"))
                blk = []
                for r4 in range(4):
                    sl = slice(512 * W + r4, 512 * (W + 1), 4)
                    blk.append((sl, sl, slice(r4, 512, 4), (VB4, 'VB4', (W % 2) * 4 + r4), r4))
                groups.append((blk, 128, "L"))
                if W > 0:
                    blk = []
                    for r4 in range(4):
                        blk.append((slice(512 * (W - 1) + r4, 512 * W, 4), slice(512 * W + r4, 512 * (W + 1), 4),
                                    slice(r4, 512, 4), (VB4, 'VB4', ((W - 1) % 2) * 4 + r4), r4))
                    groups.append((blk, 128, "U"))
                blk = []
                for r16 in range(16):
                    ksl = slice(2048 * n16 + r16, 2048 * (n16 + 1), 16)
                    qsl = slice(512 * W + r16, 512 * (W + 1), 16)
                    blk.append((ksl, qsl, slice(r16, 512, 16), (VB16, 'VB16', n16 * 16 + r16), r16))
                groups.append((blk, 32, "L16"))
                if n16 == 1:
                    blk = []
                    for r16 in range(16):
                        ksl = slice(r16, 2048, 16)
                        qsl = slice(512 * W + r16, 512 * (W + 1), 16)
                        blk.append((ksl, qsl, slice(r16, 512, 16), (VB16, 'VB16', r16), r16))
                    groups.append((blk, 32, "U16"))
                ngr = len(groups)
                for gi, (blk, N, mtype) in enumerate(groups):
                    G.append(dict(blk=blk, N=N, mtype=mtype, W=W, hh=hh, po=po, Wl=Wl, accb=accb, ri=ri,
                                  first=(gi == 0), last=(gi == ngr - 1), newwin=(gi == 0 and hh == 0)))

        def emit_qk(g):
            W = g["W"]
            if g["newwin"]:
                build_vb(VB1, "VB1", (4 * W) % 8, [(lambda n=4 * W + b: Vt[:, 128 * n:128 * (n + 1)]) for b in range(4)])
                build_vb(VB4, "VB4", (W % 2) * 4, [(lambda W=W, r4=r4: Vt[:, 512 * W + r4:512 * (W + 1):4]) for r4 in range(4)])
            blk, N, mtype, po, Wl = g["blk"], g["N"], g["mtype"], g["po"], g["Wl"]
            sbk = 4 + grp_ctr[0] % 2
            grp_ctr[0] += 1
            pt = pt_ctr[0] % NPT
            pt_ctr[0] += 1
            g["pt"] = pt
            cols_lo = blk[0][4] * N
            cols_hi = (blk[-1][4] + 1) * N
            for (ksl, qsl, acols, vbt, pos) in blk:
                P.op("pe", lambda e, sbk=sbk, pos=pos, N=N, ksl=ksl, qsl=qsl, po=po: e.matmul(
                    ps[:, sbk, pos * N:(pos + 1) * N], lhsT=Kt[po:po + 64, ksl], rhs=Qt[po:po + 64, qsl],
                    start=True, stop=True),
                    reads=["Kall", "Qall"], writes=[("ps", sbk)])
            P.op("act", lambda e, sbk=sbk, pt=pt, lo=cols_lo, hi=cols_hi: e.activation(
                out=PT[pt][:, lo:hi], in_=ps[:, sbk, lo:hi], func=AF.Exp, scale=0.125),
                reads=[("ps", sbk)], writes=[("PT", pt)])
            nb = (cols_hi - cols_lo) // N
            if mtype == "L":
                mfn = lambda nb=nb: mL[:, :].unsqueeze(1).broadcast_to([128, nb, 128])
            elif mtype == "U":
                mfn = lambda nb=nb: mU[:, :].unsqueeze(1).broadcast_to([128, nb, 128])
            elif mtype == "L16":
                mfn = lambda Wl=Wl: mL[:, 32 * Wl:32 * (Wl + 1)].unsqueeze(1).broadcast_to([128, 16, 32])
            else:
                mfn = lambda Wl=Wl: mU[:, 32 * Wl:32 * (Wl + 1)].unsqueeze(1).broadcast_to([128, 16, 32])
            meng = "pool" if msk_ctr[0] % 2 == 0 else "dve"
            msk_ctr[0] += 1
            P.op(meng, lambda e, pt=pt, lo=cols_lo, hi=cols_hi, N=N, mfn=mfn: e.tensor_tensor(
                out=PT[pt][:, lo:hi].rearrange("p (a b) -> p a b", b=N),
                in0=PT[pt][:, lo:hi].rearrange("p (a b) -> p a b", b=N), in1=mfn(), op=ALU.mult),
                reads=[("PT", pt), "mL", "mU"], writes=[("PT", pt)])

        def emit_pv(g, hp=hp):
            blk, N, po, accb, ri, hh, W, pt = g["blk"], g["N"], g["po"], g["accb"], g["ri"], g["hh"], g["W"], g["pt"]
            for bi, (ksl, qsl, acols, vbt, pos) in enumerate(blk):
                first = g["first"] and bi == 0
                last = g["last"] and (bi == len(blk) - 1)
                P.op("pe", lambda e, accb=accb, acols=acols, vbt=vbt, hh=hh, pt=pt, pos=pos, N=N, first=first, last=last: e.matmul(
                    ps[:, accb, acols], lhsT=vbt[0][:, vbt[2], hh:hh + 2, :].rearrange("p a b -> p (a b)"),
                    rhs=PT[pt][:, pos * N:(pos + 1) * N], start=first, stop=last, skip_group_check=True),
                    reads=[("PT", pt), ("vb", vbt[1], vbt[2]), ("vbones", vbt[1])], writes=[("ps", accb)])
            if g["last"]:
                dlo = 64 - po
                P.op("act", lambda e, ri=ri, accb=accb, dlo=dlo: e.activation(out=rc[ri][dlo:dlo + 64, :], in_=ps[dlo:dlo + 64, accb, :], func=AF.Ln),
                     reads=[("ps", accb)], writes=[("rc", ri)])
                P.op("act", lambda e, ri=ri, dlo=dlo: e.activation(out=rc[ri][dlo:dlo + 64, :], in_=rc[ri][dlo:dlo + 64, :], func=AF.Exp, scale=-1.0),
                     reads=[("rc", ri)], writes=[("rc", ri)])
                P.op("dve", lambda e, ri=ri, accb=accb, dlo=dlo, po=po, hp=hp, W=W: e.tensor_tensor(
                    out=CC[po:po + 64, hp, 512 * W:512 * (W + 1)], in0=ps[po:po + 64, accb, :], in1=rc[ri][dlo:dlo + 64, :], op=ALU.mult),
                    reads=[("ps", accb), ("rc", ri)], writes=[("CCa", hp, W, hh)])

        LOOK = 2
        for q in range(min(LOOK, len(G))):
            emit_qk(G[q])
        for q in range(len(G)):
            emit_pv(G[q])
            if q + LOOK < len(G):
                emit_qk(G[q + LOOK])
    P.fence()

    g0 = 146 * KB
    wz = sb("wz", [128, 8, 512], BF16, g0)
    wu = [sb("wu%d" % c, [128, 8, 128], BF16, g0 + 8 * KB + 2 * KB * c) for c in range(4)]
    wsf = sb("wsf", [128, 8, 128], F32, g0 + 16 * KB)
    wsm = sb("wsm", [128, 8, 128], BF16, g0 + 20 * KB)
    bsb = sb("bsb", [128, 4, 128], F32, g0 + 22 * KB)
    lzg = sb("lzg", [128, 512], F32, g0 + 24 * KB)
    lzb = sb("lzb", [128, 512], F32, g0 + 26 * KB)
    ug = [sb("ug%d" % i, [128, 4, 512], F32, g0 + 28 * KB + 8 * KB * i) for i in range(2)]
    zg = [sb("zg%d" % i, [128, 512], F32, g0 + 54 * KB + 2 * KB * i) for i in range(4)]
    zn = [sb("zn%d" % i, [128, 512], BF16, g0 + 62 * KB + KB * i) for i in range(4)]
    mx = [sb("mx%d" % i, [128, 4, 128], F32, g0 + 50 * KB + 2 * KB * i) for i in range(2)]

    P.op("pool", lambda e: e.dma_start(out=wz[:], in_=w_in_d[:, 2048:2560].rearrange("(kc p) c -> p kc c", p=128)),
         writes=["wz"], dma=True, lane=("g", 0))
    for c in range(4):
        P.op("pool", lambda e, c=c: e.dma_start(out=wu[c][:], in_=w_in_d[:, 1536 + 128 * c:1536 + 128 * (c + 1)].rearrange("(kc p) c -> p kc c", p=128)),
             writes=[("wu", c)], dma=True, lane=("g", 1 + c))
    P.op("sp", lambda e: e.dma_start(out=wsf[:], in_=wsT_d.rearrange("g j i -> j g i")), writes=["wsf"], dma=True, lane=("g", 5))
    for gp in range(4):
        for hf in range(2):
            P.op("sp", lambda e, gp=gp, hf=hf: e.dma_start(out=bsb[64 * hf:64 * (hf + 1), gp, :], in_=bs_d[2 * gp + hf, :].partition_broadcast(64)),
                 writes=[("bsb", gp, hf)], dma=True, lane=("g", 6 + gp * 2 + hf))
    P.op("sp", lambda e: e.dma_start(out=lzg[:], in_=lzg_d[0, :].partition_broadcast(128)), writes=["lzg"], dma=True, lane=("g", 14))
    P.op("sp", lambda e: e.dma_start(out=lzb[:], in_=lzb_d[0, :].partition_broadcast(128)), writes=["lzb"], dma=True, lane=("g", 15))
    P.op("dve", lambda e: e.tensor_tensor(out=wsm[:], in0=wsf[:], in1=mL[:, :].unsqueeze(1).broadcast_to([128, 8, 128]), op=ALU.mult),
         reads=["wsf", "mL"], writes=["wsm"])

    def layernorm_stats(src_fn, nchunks, slot, key_in):
        for c in range(nchunks):
            P.op("dve", lambda e, c=c: e.bn_stats(out=lnst[:, slot, c, :], in_=src_fn(c)), reads=[key_in], writes=[("lnst", slot, c)])
        P.op("dve", lambda e: e.bn_aggr(out=lnmv[:, slot, 0:2], in_=lnst[:, slot, 0:nchunks, :]),
             reads=[("lnst", slot, c) for c in range(nchunks)], writes=[("mv", slot)])
        P.op("act", lambda e: e.activation(out=lnmv[:, slot, 2:3], in_=lnmv[:, slot, 1:2], func=AF.Sqrt, bias=epsb[:, 0:1]),
             reads=[("mv", slot), "epsb"], writes=[("sd", slot)])
        P.op("dve", lambda e: e.reciprocal(out=lnmv[:, slot, 2:3], in_=lnmv[:, slot, 2:3]), reads=[("sd", slot)], writes=[("sd", slot)])
        P.op("dve", lambda e: e.tensor_scalar(out=lnmv[:, slot, 3:4], in0=lnmv[:, slot, 0:1], scalar1=lnmv[:, slot, 2:3], scalar2=-1.0,
                                              op0=ALU.mult, op1=ALU.mult),
             reads=[("mv", slot), ("sd", slot)], writes=[("nmr", slot)])

    epsb = sb("epsb", [128, 1], F32, c0 + 1600)
    P.op("dve", lambda e: e.memset(epsb[:], EPS), writes=["epsb"])

    ln_ctr = [0]

    def g_uproj(T8):
        ub = T8 % 2
        for c in range(4):
            bk = c % 2
            for kc in range(8):
                P.op("pe", lambda e, bk=bk, c=c, kc=kc, T8=T8: e.matmul(ps[:, bk, :], lhsT=wu[c][:, kc, :], rhs=XT[:, kc, 512 * T8:512 * (T8 + 1)],
                                                                    start=(kc == 0), stop=(kc == 7)),
                     reads=[("wu", c)], writes=[("ps", bk)])
            P.op("act", lambda e, bk=bk, c=c, ub=ub: e.activation(out=ug[ub][:, c, :], in_=ps[:, bk, :], func=AF.Gelu),
                 reads=[("ps", bk)], writes=[("ug", ub, c)])

    def g_zproj(i):
        zb = i % 4
        bk = 2 + i % 2
        for kc in range(8):
            P.op("pe", lambda e, bk=bk, kc=kc, i=i: e.matmul(ps[:, bk, :], lhsT=XT[:, kc, 128 * i:128 * (i + 1)], rhs=wz[:, kc, :],
                                                             start=(kc == 0), stop=(kc == 7)),
                 reads=["wz"], writes=[("ps", bk)])
        P.op("act", lambda e, bk=bk, zb=zb: e.activation(out=zg[zb][:], in_=ps[:, bk, :], func=AF.Gelu),
             reads=[("ps", bk)], writes=[("zg", zb)])
        slot = ln_ctr[0] % 4
        ln_ctr[0] += 1
        layernorm_stats(lambda c, zb=zb: zg[zb][:], 1, slot, ("zg", zb))
        P.op("act", lambda e, zb=zb, slot=slot: e.activation(out=zg[zb][:], in_=zg[zb][:], func=AF.Identity,
                                                             scale=lnmv[:, slot, 2:3], bias=lnmv[:, slot, 3:4]),
             reads=[("zg", zb), ("sd", slot), ("nmr", slot)], writes=[("zg", zb)])
        P.op("pool", lambda e, zb=zb: e.tensor_tensor(out=zg[zb][:], in0=zg[zb][:], in1=lzg[:], op=ALU.mult),
             reads=[("zg", zb), "lzg"], writes=[("zg", zb)])
        P.op("pool", lambda e, zb=zb: e.tensor_tensor(out=zn[zb][:], in0=zg[zb][:], in1=lzb[:], op=ALU.add),
             reads=[("zg", zb), "lzb"], writes=[("zn", zb)])

    def g_spatial(i):
        zb = i % 4
        mb = i % 2
        T8, tt = i // 4, i % 4
        ub = T8 % 2
        sbk = 4 + i % 2
        for gp in range(4):
            for hf in range(2):
                g = 2 * gp + hf
                P.op("pe", lambda e, sbk=sbk, gp=gp, hf=hf, g=g, zb=zb: e.matmul(
                    ps[64 * hf:64 * (hf + 1), sbk, 128 * gp:128 * (gp + 1)], lhsT=zn[zb][:, 64 * g:64 * (g + 1)], rhs=wsm[:, g, :],
                    start=True, stop=True),
                    reads=[("zn", zb), "wsm"], writes=[("ps", sbk)])
        P.op("dve", lambda e, sbk=sbk, mb=mb: e.tensor_tensor(out=mx[mb][:], in0=ps[:, sbk, :].rearrange("p (a b) -> p a b", b=128),
                                                           in1=bsb[:], op=ALU.add),
             reads=[("ps", sbk)] + [("bsb", gp, hf) for gp in range(4) for hf in range(2)], writes=[("mx", mb)])
        P.op("pool", lambda e, mb=mb, ub=ub, tt=tt, i=i: e.tensor_tensor(out=CC[:, 4:8, 128 * i:128 * (i + 1)], in0=mx[mb][:],
                                                                      in1=ug[ub][:, :, 128 * tt:128 * (tt + 1)], op=ALU.mult),
             reads=[("mx", mb)] + [("ug", ub, c) for c in range(4)], writes=[("CCg", i)])

    GL = 3
    for i in range(32 + GL):
        if i < 32:
            if i % 4 == 0:
                g_uproj(i // 4)
            g_zproj(i)
        if i >= GL:
            g_spatial(i - GL)

    if stop_after == "A":
        P.fence()
        o = P.op("sp", lambda e: e.dma_start(out=out_d, in_=CC[:].rearrange("p a b -> p (a b)")), writes=["out"], dma=True, lane=("o", 0))
        P.final_waits = [o.idx]
        P.emit()
        return nc
    P.fence()

    t0 = 82 * KB
    lnp = {}
    for qi, nm in enumerate(("ln1_g", "ln1_b", "ln2_g", "ln2_b", "ln3_g", "ln3_b", "b_ple_gate")):
        lnp[nm] = sb("lnp_" + nm, [128, D], F32, t0 + 4 * KB * qi)
        P.op("sp", lambda e, nm=nm: e.dma_start(out=lnp[nm][:], in_=ln_d[nm][0, :].partition_broadcast(128)),
             writes=[("lnp", nm)], dma=True, lane=("lnp", qi))
    x1s = [sb("x1_%d" % i, [128, 4, D], F32, 110 * KB + 16 * KB * i) for i in range(2)]
    gT = sb("gT", [128, NF, 512], BF16, 142 * KB)
    pT = sb("pT", [128, 2, 512], BF16, 164 * KB)
    NR = 16
    ring = [sb("ring%d" % i, [128, D], BF16, 166 * KB + 2 * KB * i) for i in range(NR)]
    xr = [sb("xr0", [128, D], F32, 198 * KB), sb("xr1", [128, D], F32, 217 * KB)]
    pbt = [sb("pbt%d" % i, [128, DPLE], BF16, 202 * KB + 512 * i) for i in range(4)]
    abuf = [sb("abuf%d" % i, [128, 520], F32, 204 * KB + 2080 * i) for i in range(2)]
    cacc = [sb("cacc%d" % i, [128, 512], F32, 209 * KB + 2 * KB * i) for i in range(2)]
    gtmp = [sb("gtmp0", [128, D], F32, 213 * KB)]
    cw = sb("cw", [128, 3 * NF], F32, 221 * KB)
    cb = sb("cb", [128, NF], F32, 221 * KB + 288)
    halo = sb("halo", [128, NF, 2], F32, 221 * KB + 384)
    lnT = sb("lnT", [128, 32], F32, 221 * KB + 576)

    P.op("sp", lambda e: e.dma_start(out=cw[:], in_=cw_d), writes=["cw"], dma=True, lane=("t", 0))
    P.op("sp", lambda e: e.dma_start(out=cb[:], in_=cb_d), writes=["cb"], dma=True, lane=("t", 1))
    P.op("sp", lambda e: e.dma_start(out=lnT[:], in_=lnT_d), writes=["lnT"], dma=True, lane=("t", 2))
    P.op("dve", lambda e: e.memset(halo[:], 0.0), writes=["halo"])

    NCH = 8 * 84
    import os as _os
    PREF = int(_os.environ.get("MK_PREF", "6"))
    emitted = [0]
    usectr = [0]

    def ring_next():
        n = usectr[0]
        usectr[0] += 1
        while emitted[0] <= min(n + PREF, NCH - 1):
            m = emitted[0]
            emitted[0] += 1
            s_ = m % NR
            P.op("sp", lambda e, s_=s_, m=m: e.dma_start(out=ring[s_][:], in_=tape[m % 84]),
                 writes=[("ring", s_)], dma=True, lane=("ring", s_))
        return n % NR

    def ln_norm(x1, par, tt):
        slot = ln_ctr[0] % 4
        ln_ctr[0] += 1
        layernorm_stats(lambda c: x1[:, tt, 512 * c:512 * (c + 1)], 2, slot, ("x1", par, tt))
        P.op("act", lambda e: e.activation(out=x1[:, tt, :], in_=x1[:, tt, :], func=AF.Identity,
                                           scale=lnmv[:, slot, 2:3], bias=lnmv[:, slot, 3:4]),
             reads=[("x1", par, tt), ("sd", slot), ("nmr", slot)], writes=[("x1", par, tt)])

    def ln_affine(x1, par, tt, gname, bname):
        P.op("pool", lambda e: e.tensor_tensor(out=x1[:, tt, :], in0=x1[:, tt, :], in1=lnp[gname][:], op=ALU.mult),
             reads=[("x1", par, tt), ("lnp", gname)], writes=[("x1", par, tt)])
        P.op("pool", lambda e: e.tensor_tensor(out=x1[:, tt, :], in0=x1[:, tt, :], in1=lnp[bname][:], op=ALU.add),
             reads=[("x1", par, tt), ("lnp", bname)], writes=[("x1", par, tt)])

    evc = [0]

    def transpose_to_cc(x1, par, tt, i, b0, lnbase):
        for k in range(8):
            P.op("pe", lambda e, k=k: e.transpose(out=ps[:, b0 + k // 4, (k % 4) * 128:(k % 4 + 1) * 128],
                                                  in_=x1[:, tt, 128 * k:128 * (k + 1)], identity=identf[:]),
                 reads=[("x1", par, tt), "identf"], writes=[("ps", b0 + k // 4)])
        for k in range(8):
            src = ps[:, b0 + k // 4, (k % 4) * 128:(k % 4 + 1) * 128]
            dst = CC[:, k, 128 * i:128 * (i + 1)]
            gcol = lnT[:, lnbase + k:lnbase + k + 1]
            bcol = lnT[:, lnbase + 8 + k:lnbase + 8 + k + 1]
            if (evc[0] + k // 4) % 2 == 0:
                P.op("act", lambda e, src=src, dst=dst, gcol=gcol, bcol=bcol: e.activation(out=dst, in_=src, func=AF.Identity, scale=gcol, bias=bcol),
                     reads=[("ps", b0 + k // 4), "lnT"], writes=[("CT", i, k)])
            else:
                P.op("dve", lambda e, src=src, dst=dst, gcol=gcol, bcol=bcol: e.tensor_scalar(out=dst, in0=src, scalar1=gcol, scalar2=bcol,
                                                                                          op0=ALU.mult, op1=ALU.add),
                     reads=[("ps", b0 + k // 4), "lnT"], writes=[("CT", i, k)])
        evc[0] += 1

    pending_b = [None]
    for Tt in range(8):
        tok0 = 512 * Tt
        par = Tt % 2
        x1 = x1s[par]
        for tt in range(4):
            i = 4 * Tt + tt
            P.op("pool", lambda e, i=i, tt=tt: e.dma_start(out=pbt[tt][:], in_=p_d[128 * i:128 * (i + 1), :]),
                 writes=[("pbt", tt)], dma=True, lane=("pbt", tt))
        so = [ring_next() for k in range(8)]

        def stB_main(tt, Tt=Tt, so=so, x1=x1, par=par):
            i = 4 * Tt + tt
            xj = i % 2
            P.op("sp", lambda e, i=i, xj=xj: e.dma_start(out=xr[xj][:], in_=x_d[128 * i:128 * (i + 1), :]),
                 writes=[("xr", xj)], dma=True, lane=("xr", xj))
            b0 = 2 * (tt % 2)
            for hf in range(2):
                for k in range(8):
                    P.op("pe", lambda e, hf=hf, k=k, i=i, b0=b0, sl_=so[k]: e.matmul(ps[:, b0 + hf, :], lhsT=CC[:, k, 128 * i:128 * (i + 1)],
                                                                      rhs=ring[sl_][:, 512 * hf:512 * (hf + 1)],
                                                                      start=(k == 0), stop=(k == 7)),
                         reads=[("CT", i, k), ("ring", so[k])], writes=[("ps", b0 + hf)])
            P.op("dve", lambda e, tt=tt, b0=b0, xj=xj: e.scalar_tensor_tensor(
                out=x1[:, tt, :].rearrange("p (a b) -> p a b", a=2), in0=xr[xj][:].rearrange("p (a b) -> p a b", a=2), scalar=ALPHA,
                in1=ps[:, b0:b0 + 2, :], op0=ALU.mult, op1=ALU.add),
                reads=[("xr", xj), ("ps", b0), ("ps", b0 + 1)], writes=[("x1", par, tt)])
            ln_norm(x1, par, tt)

        def stB_post(tt, Tt=Tt, x1=x1, par=par):
            transpose_to_cc(x1, par, tt, 4 * Tt + tt, 4 + 2 * (tt % 2), 0)
            ln_affine(x1, par, tt, "ln1_g", "ln1_b")

        pend = pending_b[0] if pending_b[0] else [lambda: None] * 4
        stB_main(0)
        pend[0]()
        stB_main(1)
        pend[1]()
        stB_post(0)
        stB_main(2)
        pend[2]()
        stB_post(1)
        stB_main(3)
        pend[3]()
        stB_post(2)
        stB_post(3)
        for f in range(NF):
            sa = ring_next()
            sbb = ring_next()
            ba = f % 2
            bb = 2 + f % 2
            aj = f % 2
            for (bank, slot) in ((ba, sa), (bb, sbb)):
                for kc in range(8):
                    P.op("pe", lambda e, bank=bank, slot=slot, kc=kc, tok0=tok0: e.matmul(
                        ps[:, bank, :], lhsT=ring[slot][:, 128 * kc:128 * (kc + 1)], rhs=CC[:, kc, tok0:tok0 + 512],
                        start=(kc == 0), stop=(kc == 7)),
                        reads=[("ring", slot)] + [("CT", 4 * Tt + q, kc) for q in range(4)], writes=[("ps", bank)])
            P.op("dve", lambda e, aj=aj, f=f: e.tensor_copy(out=abuf[aj][:, 0:2], in_=halo[:, f, :]),
                 reads=[("halo", f), "halo"], writes=[("abh", aj)])
            P.op("act", lambda e, aj=aj, ba=ba: e.copy(out=abuf[aj][:, 2:514], in_=ps[:, ba, :]),
                 reads=[("ps", ba)], writes=[("ab", aj)])
            P.op("dve", lambda e, aj=aj, f=f: e.tensor_copy(out=halo[:, f, :], in_=abuf[aj][:, 512:514]),
                 reads=[("ab", aj)], writes=[("halo", f)])
            P.op("act", lambda e, aj=aj, ba=ba, f=f: e.activation(out=cacc[aj][:], in_=ps[:, ba, :], func=AF.Identity,
                                                                  scale=cw[:, 2 * NF + f:2 * NF + f + 1], bias=cb[:, f:f + 1]),
                 reads=[("ps", ba), "cw", "cb"], writes=[("cacc", aj)])
            P.op("dve", lambda e, aj=aj, f=f: e.scalar_tensor_tensor(out=cacc[aj][:], in0=abuf[aj][:, 1:513], scalar=cw[:, NF + f:NF + f + 1],
                                                                    in1=cacc[aj][:], op0=ALU.mult, op1=ALU.add),
                 reads=[("ab", aj), ("abh", aj), ("cacc", aj), "cw"], writes=[("cacc", aj)])
            P.op("dve", lambda e, aj=aj, f=f: e.scalar_tensor_tensor(out=cacc[aj][:], in0=abuf[aj][:, 0:512], scalar=cw[:, f:f + 1],
                                                                    in1=cacc[aj][:], op0=ALU.mult, op1=ALU.add),
                 reads=[("ab", aj), ("abh", aj), ("cacc", aj), "cw"], writes=[("cacc", aj)])
            P.op("act", lambda e, aj=aj: e.activation(out=cacc[aj][:], in_=cacc[aj][:], func=AF.Gelu),
                 reads=[("cacc", aj)], writes=[("cacc", aj)])
            P.op("dve", lambda e, aj=aj, bb=bb, f=f: e.tensor_tensor(out=gT[:, f, :], in0=cacc[aj][:], in1=ps[:, bb, :], op=ALU.mult),
                 reads=[("cacc", aj), ("ps", bb)], writes=[("gT", f)])

        for tt in range(4):
            for kc in range(2):
                P.op("pe", lambda e, tt=tt, kc=kc: e.transpose(out=psb[:, 4, (tt * 2 + kc) * 128:(tt * 2 + kc + 1) * 128],
                                                               in_=pbt[tt][:, 128 * kc:128 * (kc + 1)], identity=identb[:]),
                     reads=[("pbt", tt), "identb"], writes=[("ps", 4)])
        P.op("dve", lambda e: e.tensor_copy(out=pT[:].rearrange("p k (t c) -> p k t c", c=128),
                                            in_=psb[:, 4, :].rearrange("p (t k c) -> p k t c", t=4, k=2)),
             reads=[("ps", 4)], writes=["pT"])

        for f in range(NF):
            sd = ring_next()
            for tt in range(4):
                for hf in range(2):
                    P.op("pe", lambda e, f=f, tt=tt, hf=hf, sd=sd: e.matmul(ps[:, 2 * tt + hf, :], lhsT=gT[:, f, 128 * tt:128 * (tt + 1)],
                                                                        rhs=ring[sd][:, 512 * hf:512 * (hf + 1)],
                                                                        start=(f == 0), stop=(f == NF - 1)),
                         reads=[("gT", f), ("ring", sd)], writes=[("ps", 2 * tt + hf)])
        sg = [ring_next() for k in range(8)]
        sp_ = [ring_next() for k in range(2)]

        def ln2_main(tt, Tt=Tt, x1=x1, par=par):
            b0 = 2 * tt
            P.op("dve", lambda e, tt=tt, b0=b0: e.scalar_tensor_tensor(
                out=x1[:, tt, :].rearrange("p (a b) -> p a b", a=2), in0=x1[:, tt, :].rearrange("p (a b) -> p a b", a=2), scalar=ALPHA,
                in1=ps[:, b0:b0 + 2, :], op0=ALU.mult, op1=ALU.add),
                reads=[("x1", par, tt), ("ps", b0), ("ps", b0 + 1)], writes=[("x1", par, tt)])
            ln_norm(x1, par, tt)

        def ln2_post(tt, Tt=Tt, x1=x1, par=par):
            transpose_to_cc(x1, par, tt, 4 * Tt + tt, 2 * tt, 16)
            ln_affine(x1, par, tt, "ln2_g", "ln2_b")

        def gate_a(tt, Tt=Tt, sg=sg, sp_=sp_, x1=x1, par=par):
            i = 4 * Tt + tt
            st_ = 0
            bg_, bp_ = 4 * (tt % 2), 4 * (tt % 2) + 2
            for hf in range(2):
                for k in range(8):
                    P.op("pe", lambda e, hf=hf, k=k, i=i, bg_=bg_, sl_=sg[k]: e.matmul(ps[:, bg_ + hf, :], lhsT=CC[:, k, 128 * i:128 * (i + 1)],
                                                                       rhs=ring[sl_][:, 512 * hf:512 * (hf + 1)],
                                                                       start=(k == 0), stop=(k == 7)),
                         reads=[("CT", i, k), ("ring", sg[k])], writes=[("ps", bg_ + hf)])
            for hf in range(2):
                for k in range(2):
                    P.op("pe", lambda e, hf=hf, k=k, tt=tt, bp_=bp_, sl_=sp_[k]: e.matmul(ps[:, bp_ + hf, :], lhsT=pT[:, k, 128 * tt:128 * (tt + 1)],
                                                                        rhs=ring[sl_][:, 512 * hf:512 * (hf + 1)],
                                                                        start=(k == 0), stop=(k == 1)),
                         reads=["pT", ("ring", sp_[k])], writes=[("ps", bp_ + hf)])
            gt = gtmp[st_]
            P.op("dve", lambda e, gt=gt, bg_=bg_: e.tensor_tensor(out=gt[:].rearrange("p (a b) -> p a b", a=2), in0=ps[:, bg_:bg_ + 2, :],
                                                              in1=lnp["b_ple_gate"][:].rearrange("p (a b) -> p a b", a=2), op=ALU.add),
                 reads=[("ps", bg_), ("ps", bg_ + 1), ("lnp", "b_ple_gate")], writes=[("gtmp", st_)])
            P.op("act", lambda e, gt=gt: e.activation(out=gt[:], in_=gt[:], func=AF.Sigmoid), reads=[("gtmp", st_)], writes=[("gtmp", st_)])
            P.op("dve", lambda e, gt=gt, bp_=bp_: e.tensor_tensor(out=gt[:].rearrange("p (a b) -> p a b", a=2),
                                                              in0=gt[:].rearrange("p (a b) -> p a b", a=2), in1=ps[:, bp_:bp_ + 2, :], op=ALU.mult),
                 reads=[("gtmp", st_), ("ps", bp_), ("ps", bp_ + 1)], writes=[("gtmp", st_)])
            P.op("dve", lambda e, gt=gt, tt=tt: e.scalar_tensor_tensor(out=x1[:, tt, :], in0=x1[:, tt, :], scalar=ALPHA, in1=gt[:],
                                                                    op0=ALU.mult, op1=ALU.add),
                 reads=[("x1", par, tt), ("gtmp", st_)], writes=[("x1", par, tt)])

        def gate_b(tt, Tt=Tt, x1=x1, par=par):
            i = 4 * Tt + tt
            ln_norm(x1, par, tt)
            ln_affine(x1, par, tt, "ln3_g", "ln3_b")
            o = P.op("pool", lambda e, i=i, tt=tt: e.dma_start(out=out_d[128 * i:128 * (i + 1), :], in_=x1[:, tt, :]),
                     reads=[("x1", par, tt)], writes=[("out", i)], dma=True, lane=("o", par, tt))
            P.final_waits.append(o.idx)

        ln2_main(0)
        ln2_main(1)
        ln2_post(0)
        ln2_main(2)
        ln2_post(1)
        ln2_main(3)
        ln2_post(2)
        ln2_post(3)
        for tt in range(4):
            gate_a(tt)
        pending_b[0] = [(lambda tt=tt, gb=gate_b: gb(tt)) for tt in range(4)]
    for fn in pending_b[0]:
        fn()
    P.emit()
    return nc


def _consts():
    j = np.arange(128)[:, None]
    i = np.arange(128)[None, :]
    maskL = (j <= i).astype(np.float32)
    maskU = (j >= i).astype(np.float32)
    ident = np.eye(128, dtype=np.float32)
    inv8 = np.float32(500000.0) ** (-(np.arange(0, 16, 2, dtype=np.float32)) / np.float32(16.0))
    ropec = np.zeros((128, 2), np.float32)
    for p_ in range(128):
        d = p_ % 64
        if d < 16:
            ropec[p_, 0] = inv8[d % 8]
            ropec[p_, 1] = -1.0 if d < 8 else 1.0
    return ident, maskL, maskU, ropec


def _prep_shared(inp):
    f = lambda a: np.ascontiguousarray(np.asarray(a, dtype=np.float32))
    w_in = f(inp["w_in"][0])
    perm = np.arange(1024)
    for h in range(16):
        base = h * 64
        perm[base:base + 8] = np.arange(base + 8, base + 16)
        perm[base + 8:base + 16] = np.arange(base, base + 8)
    w_sw = np.ascontiguousarray(w_in[:, :1024][:, perm])
    conv_w = f(inp["conv_w"][0])
    conv_b = f(inp["conv_b"][0])
    cw = np.ascontiguousarray(conv_w.reshape(3, NF, 128).transpose(2, 0, 1).reshape(128, 3 * NF))
    cb = np.ascontiguousarray(conv_b.reshape(NF, 128).T)
    ident, maskL, maskU, ropec = _consts()
    sh = {
        "w_in": w_in, "w_sw": w_sw,
        "ln_z_g": f(inp["ln_z_g"]), "ln_z_b": f(inp["ln_z_b"]),
        "w_sT": np.ascontiguousarray(f(inp["w_s"][0]).transpose(0, 2, 1)),
        "b_s": f(inp["b_s"][0]),
        "w_o": f(inp["w_o"][0]),
        "w_ff_a": f(inp["w_ff_a"][0]), "w_ff_b": f(inp["w_ff_b"][0]),
        "cw": cw, "cb": cb,
        "w_ff_down": f(inp["w_ff_down"][0]),
        "w_ple_gate": f(inp["w_ple_gate"][0]),
        "w_ple_in": f(inp["w_ple_in"][0]),
        "ident": ident, "maskL": maskL, "maskU": maskU, "ropec": ropec,
    }
    for nm in ("ln1_g", "ln1_b", "ln2_g", "ln2_b", "ln3_g", "ln3_b", "b_ple_gate"):
        sh[nm] = f(inp[nm])
    lnT = np.zeros((128, 32), np.float32)
    for j, (gn, bn) in enumerate((("ln1_g", "ln1_b"), ("ln2_g", "ln2_b"))):
        lnT[:, 16 * j:16 * j + 8] = sh[gn].reshape(8, 128).T
        lnT[:, 16 * j + 8:16 * j + 16] = sh[bn].reshape(8, 128).T
    sh["lnT"] = np.ascontiguousarray(lnT)
    return sh


_NC_CACHE = {}


def kernel(**inputs):
    sh = _prep_shared(inputs)
    x = np.asarray(inputs["x"], dtype=np.float32)
    p = np.asarray(inputs["p"], dtype=np.float32)
    pos = np.asarray(inputs["positions"], dtype=np.int32)
    in_maps = []
    for b in range(8):
        m = dict(sh)
        m["x"] = np.ascontiguousarray(x[b])
        m["p"] = np.ascontiguousarray(p[0, b])
        m["pos"] = np.ascontiguousarray(pos[b:b + 1])
        in_maps.append(m)
    nc = build()
    res = run_bass_kernel_spmd(nc, in_maps, core_ids=list(range(8)))
    out = np.stack([np.asarray(r["out"], dtype=np.float32) for r in res.results], axis=0)
    return out
```

```python
import numpy as np
import concourse.bass as bass
import concourse.mybir as mybir
from concourse.bass_utils import run_bass_kernel_spmd

F32 = mybir.dt.float32
BF16 = mybir.dt.bfloat16
I32 = mybir.dt.int32
AF = mybir.ActivationFunctionType
ALU = mybir.AluOpType

S = 4096
D = 1024
DFF = 2816
NF = 22
DPLE = 256
ALPHA = float(2.0 ** 0.25)
EPS = 1e-5
PI = float(np.pi)
TWO_PI = float(2 * np.pi)
KB = 1024


class Op:
    __slots__ = ("idx", "eng", "fn", "deps", "is_dma", "lane", "sig", "signals")

    def __init__(self, idx, eng, fn, is_dma, lane):
        self.idx = idx
        self.eng = eng
        self.fn = fn
        self.deps = set()
        self.is_dma = is_dma
        self.lane = lane
        self.sig = None
        self.signals = False


class Prog:
    ENGS = ("pe", "act", "dve", "pool", "sp")

    def __init__(self, nc):
        self.nc = nc
        self.ops = []
        self.state = {}
        self.final_waits = []
        self.last_eng = {}
        self.last_lane = {}
        self.pending_fence = {}

    def op(self, eng, fn, reads=(), writes=(), dma=False, lane=None):
        o = Op(len(self.ops), eng, fn, dma, lane)
        self.ops.append(o)
        pf = self.pending_fence.pop(eng, None)
        if pf is not None:
            o.deps.update(pf)
        for k in reads:
            st = self.state.get(k)
            if st is None:
                st = [None, []]
                self.state[k] = st
            if st[0] is not None:
                self._dep(o, st[0], "raw")
            if isinstance(k, tuple) and k[0] == "ps":
                for r in st[1]:
                    if self.ops[r].eng != eng:
                        o.deps.add(r)
            if not dma:
                st[1] = [r for r in st[1] if self.ops[r].is_dma or self.ops[r].eng != eng]
            st[1].append(o.idx)
        for k in writes:
            st = self.state.get(k)
            if st is None:
                st = [None, []]
                self.state[k] = st
            if st[0] is not None:
                self._dep(o, st[0], "waw")
            for r in st[1]:
                if r != o.idx:
                    self._dep(o, r, "war")
            st[0] = o.idx
            st[1] = []
        if dma:
            self.last_lane[tuple(lane)] = o.idx
        else:
            self.last_eng[eng] = o.idx
        return o

    def _dep(self, o, j, kind):
        t = self.ops[j]
        if t.eng == o.eng and not t.is_dma and not o.is_dma:
            if o.eng == "pe":
                return
        o.deps.add(j)

    def fence(self):
        deps = set(self.last_eng.values()) | set(self.last_lane.values())
        for e in self.ENGS:
            self.pending_fence[e] = set(deps)
        self.state = {}

    def emit(self):
        nc = self.nc
        ops = self.ops
        for o in ops:
            o.deps.discard(o.idx)
            if o.is_dma:
                o.signals = True
            for j in o.deps:
                ops[j].signals = True
        for j in self.final_waits:
            ops[j].signals = True
        sems = {}
        counters = {}
        for o in ops:
            if not o.signals:
                continue
            key = ("dma",) + tuple(o.lane) if o.is_dma else ("eng", o.eng)
            if key not in sems:
                sems[key] = nc.alloc_semaphore("s_" + "_".join(str(x) for x in key))
                counters[key] = 0
            counters[key] += 16 if o.is_dma else 1
            o.sig = (key, counters[key])
        streams = {e: [o for o in ops if o.eng == e] for e in self.ENGS}
        final = [ops[j].sig for j in self.final_waits]

        def run_stream(e, engh):
            waited = {}
            for o in streams[e]:
                need = {}
                for j in o.deps:
                    t = ops[j]
                    if t.eng == e and not t.is_dma and not o.is_dma and e == "pe":
                        continue
                    k, v = t.sig
                    if need.get(k, 0) < v:
                        need[k] = v
                for k, v in need.items():
                    if waited.get(k, 0) >= v:
                        continue
                    engh.wait_ge(sems[k], v)
                    waited[k] = v
                ins = o.fn(engh)
                if o.signals:
                    ins.then_inc(sems[o.sig[0]], 16 if o.is_dma else 1)
            if e == "sp":
                need = {}
                for k, v in final:
                    if need.get(k, 0) < v:
                        need[k] = v
                for k, v in need.items():
                    engh.wait_ge(sems[k], v)

        with nc.Block() as block:
            @block.tensor
            def _(pe):
                run_stream("pe", pe)

            @block.scalar
            def _(act):
                run_stream("act", act)

            @block.vector
            def _(dve):
                run_stream("dve", dve)

            @block.gpsimd
            def _(pool):
                run_stream("pool", pool)

            @block.sync
            def _(sp):
                run_stream("sp", sp)


def build(stop_after=None):
    nc = bass.Bass("TRN2", target_bir_lowering=False)
    P = Prog(nc)

    def din(name, shape, dt=F32):
        return nc.dram_tensor(name, list(shape), dt, kind="ExternalInput").ap()

    x_d = din("x", [S, D])
    p_d = din("p", [S, DPLE])
    pos_d = din("pos", [1, S], I32)
    w_in_d = din("w_in", [D, 2560])
    w_sw_d = din("w_sw", [D, 1024])
    lzg_d = din("ln_z_g", [1, 512])
    lzb_d = din("ln_z_b", [1, 512])
    wsT_d = din("w_sT", [8, 128, 128])
    bs_d = din("b_s", [8, 128])
    wo_d = din("w_o", [D, D])
    ln_d = {}
    for nm in ("ln1_g", "ln1_b", "ln2_g", "ln2_b", "ln3_g", "ln3_b", "b_ple_gate"):
        ln_d[nm] = din(nm, [1, D])
    wa_d = din("w_ff_a", [D, DFF])
    wb_d = din("w_ff_b", [D, DFF])
    cw_d = din("cw", [128, 3 * NF])
    cb_d = din("cb", [128, NF])
    wd_d = din("w_ff_down", [DFF, D])
    wg_d = din("w_ple_gate", [D, D])
    wp_d = din("w_ple_in", [DPLE, D])
    ident_d = din("ident", [128, 128])
    mL_d = din("maskL", [128, 128])
    mU_d = din("maskU", [128, 128])
    ropec_d = din("ropec", [128, 2])
    lnT_d = din("lnT", [128, 32])
    if stop_after == "A":
        out_d = nc.dram_tensor("out", [128, 8 * S], BF16, kind="ExternalOutput").ap()
    elif stop_after in ("B", "D", "G"):
        out_d = nc.dram_tensor("out", [128, 4 * D], F32, kind="ExternalOutput").ap()
    elif stop_after == "F":
        out_d = nc.dram_tensor("out", [128, NF * 512], BF16, kind="ExternalOutput").ap()
    else:
        out_d = nc.dram_tensor("out", [S, D], F32, kind="ExternalOutput").ap()

    tape = nc.dram_tensor("tape", [84, 128, D], BF16).ap()
    chunks84 = []
    for k in range(8):
        chunks84.append((wo_d[128 * k:128 * (k + 1), :], False))
    for f in range(NF):
        chunks84.append((wa_d[:, 128 * f:128 * (f + 1)], True))
        chunks84.append((wb_d[:, 128 * f:128 * (f + 1)], True))
    for f in range(NF):
        chunks84.append((wd_d[128 * f:128 * (f + 1), :], False))
    for k in range(8):
        chunks84.append((wg_d[128 * k:128 * (k + 1), :], False))
    for k in range(2):
        chunks84.append((wp_d[128 * k:128 * (k + 1), :], False))

    def sb(name, shape, dt, off):
        assert off % 32 == 0, (name, off)
        return nc.alloc_sbuf_tensor_at(name, list(shape), dt, offset=int(off))

    ps = nc.alloc_psum_tensor("ps", [128, 8, 512], F32)
    psb = ps.bitcast(BF16)

    CC = sb("CC", [128, 8, S], BF16, 16 * KB)
    TC = sb("TC", [128, S], F32, 48 * KB)
    TS = sb("TS", [128, S], F32, 64 * KB)
    c0 = 80 * KB
    identf = sb("identf", [128, 128], F32, c0)
    identb = sb("identb", [128, 128], BF16, c0 + 512)
    mL = sb("mL", [128, 128], BF16, c0 + 768)
    mU = sb("mU", [128, 128], BF16, c0 + 1024)
    ropec = sb("ropec", [128, 2], F32, c0 + 1280)
    lnst = sb("lnst", [128, 4, 2, 6], F32, c0 + 1312)
    lnmv = sb("lnmv", [128, 4, 4], F32, c0 + 1312 + 192)

    P.op("sp", lambda e: e.dma_start(out=identf[:], in_=ident_d), writes=["identf"], dma=True, lane=("c", 0))
    P.op("pool", lambda e: e.dma_start(out=identb[:], in_=ident_d), writes=["identb"], dma=True, lane=("c", 1))
    P.op("pool", lambda e: e.dma_start(out=mL[:], in_=mL_d), writes=["mL"], dma=True, lane=("c", 2))
    P.op("pool", lambda e: e.dma_start(out=mU[:], in_=mU_d), writes=["mU"], dma=True, lane=("c", 3))
    P.op("sp", lambda e: e.dma_start(out=ropec[:], in_=ropec_d), writes=["ropec"], dma=True, lane=("c", 4))

    e0 = 146 * KB
    HT = 1024
    pos_i = sb("pos_i", [128, HT], I32, e0)
    pos_f = sb("pos_f", [128, HT], F32, e0 + 4 * KB)
    ang = sb("ang", [128, HT], F32, e0 + 8 * KB)
    a2 = sb("a2", [128, HT], F32, e0 + 12 * KB)
    ki = sb("ki", [128, HT], I32, e0 + 16 * KB)
    rr = sb("rr", [128, HT], F32, e0 + 20 * KB)
    tt_ = sb("tt_", [128, HT], F32, e0 + 24 * KB)
    C1 = 6.28125
    C2 = TWO_PI - C1
    for h in range(4):
        cs = slice(h * HT, (h + 1) * HT)
        P.op("sp", lambda e, cs=cs: e.dma_start(out=pos_i[:], in_=pos_d[0, cs].partition_broadcast(128)),
             writes=["pos_i"], dma=True, lane=("c", 5))
        P.op("dve", lambda e: e.tensor_copy(out=pos_f[:], in_=pos_i[:]), reads=["pos_i"], writes=["pos_f"])
        P.op("dve", lambda e: e.tensor_scalar(out=ang[:], in0=pos_f[:], scalar1=ropec[:, 0:1], scalar2=None, op0=ALU.mult),
             reads=["pos_f", "ropec"], writes=["ang"])
        for tab, shift in ((TS, 0.0), (TC, PI / 2)):
            if shift != 0.0:
                P.op("dve", lambda e, shift=shift: e.tensor_scalar(out=a2[:], in0=ang[:], scalar1=shift, scalar2=None, op0=ALU.add),
                     reads=["ang"], writes=["a2"])
                src = a2
                srck = "a2"
            else:
                src = ang
                srck = "ang"
            P.op("dve", lambda e, src=src: e.tensor_scalar(out=ki[:], in0=src[:], scalar1=1.0 / TWO_PI, scalar2=None, op0=ALU.mult),
                 reads=[srck], writes=["ki"])
            P.op("dve", lambda e, src=src: e.scalar_tensor_tensor(out=rr[:], in0=ki[:], scalar=-C1, in1=src[:], op0=ALU.mult, op1=ALU.add),
                 reads=["ki", srck], writes=["rr"])
            P.op("dve", lambda e: e.scalar_tensor_tensor(out=rr[:], in0=ki[:], scalar=-C2, in1=rr[:], op0=ALU.mult, op1=ALU.add),
                 reads=["ki", "rr"], writes=["rr"])
            P.op("dve", lambda e: e.tensor_scalar(out=tt_[:], in0=rr[:], scalar1=PI, scalar2=TWO_PI, op0=ALU.is_gt, op1=ALU.mult),
                 reads=["rr"], writes=["tt_"])
            P.op("dve", lambda e: e.tensor_tensor(out=rr[:], in0=rr[:], in1=tt_[:], op=ALU.subtract), reads=["rr", "tt_"], writes=["rr"])
            P.op("dve", lambda e: e.tensor_scalar(out=rr[:], in0=rr[:], scalar1=PI, scalar2=-PI, op0=ALU.min, op1=ALU.max),
                 reads=["rr"], writes=["rr"])
            if shift == 0.0:
                P.op("act", lambda e, tab=tab, cs=cs: e.activation(out=tab[:, cs], in_=rr[:], func=AF.Sin, scale=ropec[:, 1:2]),
                     reads=["rr", "ropec"], writes=[("tab", "S", h)])
            else:
                P.op("act", lambda e, tab=tab, cs=cs: e.activation(out=tab[:, cs], in_=rr[:], func=AF.Sin),
                     reads=["rr"], writes=[("tab", "C", h)])

    P.fence()

    XT = sb("XT", [128, 8, S], BF16, 82 * KB)
    Qt = sb("Qt", [128, S], BF16, 146 * KB)
    Kt = sb("Kt", [128, S], BF16, 154 * KB)
    Vt = sb("Vt", [128, S], BF16, 162 * KB)
    wbuf = [sb("wbuf%d" % i, [128, 8, 128], BF16, 170 * KB + 2 * KB * i) for i in range(4)]
    t1 = [sb("t1_%d" % i, [128, 512], F32, 178 * KB + 4 * KB * i) for i in range(2)]
    t2 = [sb("t2_%d" % i, [128, 512], F32, 180 * KB + 4 * KB * i) for i in range(2)]
    xs = [sb("xs%d" % i, [128, D], BF16, 186 * KB + 2 * KB * i) for i in range(2)]
    NPT = 6
    PT = [sb("PT%d" % i, [128, 512], BF16, 190 * KB + KB * i) for i in range(NPT)]
    VB1 = sb("VB1", [128, 8, 3, 64], BF16, 196 * KB)
    VB4 = sb("VB4", [128, 8, 3, 64], BF16, 199 * KB)
    VB16 = sb("VB16", [128, 32, 3, 64], BF16, 202 * KB)
    rc = [sb("rc%d" % i, [128, 512], F32, 214 * KB + 2 * KB * i) for i in range(2)]

    for nm_, t in (("VB1", VB1), ("VB4", VB4), ("VB16", VB16)):
        P.op("pool", lambda e, t=t: e.memset(t[:, :, 1, :], 1.0), writes=[("vbones", nm_)])

    for i in range(32):
        j = i % 2
        P.op("pool", lambda e, i=i, j=j: e.dma_start(out=xs[j][:], in_=x_d[128 * i:128 * (i + 1), :]),
             writes=[("xs", j)], dma=True, lane=("xs", j))
        bk = 4 + j
        for k in range(8):
            P.op("pe", lambda e, j=j, k=k, bk=bk: e.transpose(out=psb[:, bk, k * 128:(k + 1) * 128],
                                                              in_=xs[j][:, k * 128:(k + 1) * 128], identity=identb[:]),
                 reads=[("xs", j), "identb"], writes=[("ps", bk)])
        eng = "act" if i % 2 == 0 else "dve"
        src = psb[:, bk, :].rearrange("p (k c) -> p k c", c=128)
        dst = XT[:, :, 128 * i:128 * (i + 1)]
        if eng == "act":
            P.op("act", lambda e, src=src, dst=dst: e.copy(out=dst, in_=src), reads=[("ps", bk)], writes=[("XT", i)])
        else:
            P.op("dve", lambda e, src=src, dst=dst: e.tensor_copy(out=dst, in_=src), reads=[("ps", bk)], writes=[("XT", i)])

    wcount = [0]

    def load_w(src_ap):
        s_ = wcount[0] % 4
        wcount[0] += 1
        P.op("pool", lambda e, s_=s_, src_ap=src_ap: e.dma_start(out=wbuf[s_][:], in_=src_ap.rearrange("(kc p) c -> p kc c", p=128)),
             writes=[("wb", s_)], dma=True, lane=("wb", s_))
        return s_

    def proj_mm(bank, slot, T8):
        for kc in range(8):
            P.op("pe", lambda e, bank=bank, slot=slot, kc=kc, T8=T8: e.matmul(
                ps[:, bank, :], lhsT=wbuf[slot][:, kc, :], rhs=XT[:, kc, 512 * T8:512 * (T8 + 1)],
                start=(kc == 0), stop=(kc == 7)),
                reads=[("wb", slot)] + [("XT", 4 * T8 + q) for q in range(4)], writes=[("ps", bank)])

    pt_ctr = [0]
    grp_ctr = [0]
    acc_ctr = [0]
    tr_ctr = [0]
    msk_ctr = [0]

    def build_vb(vbt, nm_, idx0, tok_ap_fns):
        cnt = len(tok_ap_fns)
        bk = tr_ctr[0] % 2
        tr_ctr[0] += 1
        for q, fn in enumerate(tok_ap_fns):
            P.op("pe", lambda e, q=q, fn=fn: e.transpose(out=psb[:, bk, q * 128:(q + 1) * 128], in_=fn(), identity=identb[:]),
                 reads=["Vall", "identb"], writes=[("ps", bk)])
        src = psb[:, bk, 0:cnt * 128].rearrange("p (q a b) -> p q a b", a=2, b=64)
        P.op("dve", lambda e: e.tensor_copy(out=vbt[:, idx0:idx0 + cnt, 0:3:2, :], in_=src), reads=[("ps", bk)],
             writes=[("vb", nm_, idx0 + q) for q in range(cnt)])

    for hp in range(4):
        for (dest, dkey, c_main, c_sw) in ((Qt, "Q", hp * 128, hp * 128), (Kt, "K", 512 + hp * 128, 512 + hp * 128)):
            sa = load_w(w_in_d[:, c_main:c_main + 128])
            sbw = load_w(w_sw_d[:, c_sw:c_sw + 128])
            for T8 in range(8):
                st_ = T8 % 2
                ba, bb = 2 * st_, 2 * st_ + 1
                proj_mm(ba, sa, T8)
                proj_mm(bb, sbw, T8)
                cs = slice(512 * T8, 512 * (T8 + 1))
                P.op("dve", lambda e, st_=st_, ba=ba, cs=cs: e.tensor_tensor(out=t1[st_][:], in0=ps[:, ba, :], in1=TC[:, cs], op=ALU.mult),
                     reads=[("ps", ba)] + [("tab", "C", h_) for h_ in range(4)], writes=[("t1", st_)])
                P.op("dve", lambda e, st_=st_, bb=bb, cs=cs: e.tensor_tensor(out=t2[st_][:], in0=ps[:, bb, :], in1=TS[:, cs], op=ALU.mult),
                     reads=[("ps", bb)] + [("tab", "S", h_) for h_ in range(4)], writes=[("t2", st_)])
                P.op("pool", lambda e, st_=st_, dest=dest, cs=cs: e.tensor_tensor(out=dest[:, cs], in0=t1[st_][:], in1=t2[st_][:], op=ALU.add),
                     reads=[("t1", st_), ("t2", st_)], writes=[(dkey, T8), dkey + "all"])
        sv = load_w(w_in_d[:, 1024 + hp * 128:1024 + (hp + 1) * 128])
        for T8 in range(8):
            bk = 2 * (T8 % 2)
            proj_mm(bk, sv, T8)
            cs = slice(512 * T8, 512 * (T8 + 1))
            P.op("act", lambda e, bk=bk, cs=cs: e.copy(out=Vt[:, cs], in_=ps[:, bk, :]), reads=[("ps", bk)], writes=["Vall"])

        for m in range(21 * hp, 21 * (hp + 1)):
            src_ap, view3 = chunks84[m]
            if view3:
                P.op("pool", lambda e, m=m, src_ap=src_ap: e.dma_start(out=tape[m].rearrange("p (kc c) -> p kc c", c=128),
                                                                   in_=src_ap.rearrange("(kc p) c -> p kc c", p=128)),
                     writes=[("tape", m)], dma=True, lane=("tape", m % 4))
            else:
                P.op("pool", lambda e, m=m, src_ap=src_ap: e.dma_start(out=tape[m], in_=src_ap),
                     writes=[("tape", m)], dma=True, lane=("tape", m % 4))

        for n16 in range(2):
            for r8 in range(2):
                build_vb(VB16, "VB16", n16 * 16 + r8 * 8,
                         [(lambda n16=n16, r16=r8 * 8 + q: Vt[:, 2048 * n16 + r16:2048 * (n16 + 1):16]) for q in range(8)])

        G = []
        for W in range(8):
            n16 = W // 4
            Wl = W % 4
            for hh in range(2):
                po = 64 * hh
                accb = 6 + acc_ctr[0] % 2
                acc_ctr[0] += 1
                ri = acc_ctr[0] % 2
                groups = []
                blk = []
                for b in range(4):
                    n = 4 * W + b
                    sl = slice(128 * n, 128 * (n + 1))
                    blk.append((sl, sl, slice(128 * b, 128 * (b + 1)), (VB1, 'VB1', n % 8), b))
                groups.append((blk, 128, "L"))
                blk = []
                for b in range(4):
                    n = 4 * W + b
                    if n == 0:
                        continue
                    blk.append((slice(128 * (n - 1), 128 * n), slice(128 * n, 128 * (n + 1)), slice(128 * b, 128 * (b + 1)),
                                (VB1, 'VB1', (n - 1) % 8), b))
                groups.append((blk, 128, "U"))
                blk = []
                for r4 in range(4):
                    sl = slice(512 * W + r4, 512 * (W + 1), 4)
                    blk.append((sl, sl, slice(r4, 512, 4), (VB4, 'VB4', (W % 2) * 4 + r4), r4))
                groups.append((blk, 128, "L"))
                if W > 0:
                    blk = []
                    for r4 in range(4):
                        blk.append((slice(512 * (W - 1) + r4, 512 * W, 4), slice(512 * W + r4, 512 * (W + 1), 4),
                                    slice(r4, 512, 4), (VB4, 'VB4', ((W - 1) % 2) * 4 + r4), r4))
                    groups.append((blk, 128, "U"))
                blk = []
                for r16 in range(16):
                    ksl = slice(2048 * n16 + r16, 2048 * (n16 + 1), 16)
                    qsl = slice(512 * W + r16, 512 * (W + 1), 16)
                    blk.append((ksl, qsl, slice(r16, 512, 16), (VB16, 'VB16', n16 * 16 + r16), r16))
                groups.append((blk, 32, "L16"))
                if n16 == 1:
                    blk = []
                    for r16 in range(16):
                        ksl = slice(r16, 2048, 16)
                        qsl = slice(512 * W + r16, 512 * (W + 1), 16)
                        blk.append((ksl, qsl, slice(r16, 512, 16), (VB16, 'VB16', r16), r16))
                    groups.append((blk, 32, "U16"))
                ngr = len(groups)
                for gi, (blk, N, mtype) in enumerate(groups):
                    G.append(dict(blk=blk, N=N, mtype=mtype, W=W, hh=hh, po=po, Wl=Wl, accb=accb, ri=ri,
                                  first=(gi == 0), last=(gi == ngr - 1), newwin=(gi == 0 and hh == 0)))

        def emit_qk(g):
            W = g["W"]
            if g["newwin"]:
                build_vb(VB1, "VB1", (4 * W) % 8, [(lambda n=4 * W + b: Vt[:, 128 * n:128 * (n + 1)]) for b in range(4)])
                build_vb(VB4, "VB4", (W % 2) * 4, [(lambda W=W, r4=r4: Vt[:, 512 * W + r4:512 * (W + 1):4]) for r4 in range(4)])
            blk, N, mtype, po, Wl = g["blk"], g["N"], g["mtype"], g["po"], g["Wl"]
            sbk = 4 + grp_ctr[0] % 2
            grp_ctr[0] += 1
            pt = pt_ctr[0] % NPT
            pt_ctr[0] += 1
            g["pt"] = pt
            cols_lo = blk[0][4] * N
            cols_hi = (blk[-1][4] + 1) * N
            for (ksl, qsl, acols, vbt, pos) in blk:
                P.op("pe", lambda e, sbk=sbk, pos=pos, N=N, ksl=ksl, qsl=qsl, po=po: e.matmul(
                    ps[:, sbk, pos * N:(pos + 1) * N], lhsT=Kt[po:po + 64, ksl], rhs=Qt[po:po + 64, qsl],
                    start=True, stop=True),
                    reads=["Kall", "Qall"], writes=[("ps", sbk)])
            P.op("act", lambda e, sbk=sbk, pt=pt, lo=cols_lo, hi=cols_hi: e.activation(
                out=PT[pt][:, lo:hi], in_=ps[:, sbk, lo:hi], func=AF.Exp, scale=0.125),
                reads=[("ps", sbk)], writes=[("PT", pt)])
            nb = (cols_hi - cols_lo) // N
            if mtype == "L":
                mfn = lambda nb=nb: mL[:, :].unsqueeze(1).broadcast_to([128, nb, 128])
            elif mtype == "U":
                mfn = lambda nb=nb: mU[:, :].unsqueeze(1).broadcast_to([128, nb, 128])
            elif mtype == "L16":
                mfn = lambda Wl=Wl: mL[:, 32 * Wl:32 * (Wl + 1)].unsqueeze(1).broadcast_to([128, 16, 32])
            else:
                mfn = lambda Wl=Wl: mU[:, 32 * Wl:32 * (Wl + 1)].unsqueeze(1).broadcast_to([128, 16, 32])
            meng = "pool" if msk_ctr[0] % 2 == 0 else "dve"
            msk_ctr[0] += 1
            P.op(meng, lambda e, pt=pt, lo=cols_lo, hi=cols_hi, N=N, mfn=mfn: e.tensor_tensor(
                out=PT[pt][:, lo:hi].rearrange("p (a b) -> p a b", b=N),
                in0=PT[pt][:, lo:hi].rearrange("p (a b) -> p a b", b=N), in1=mfn(), op=ALU.mult),
                reads=[("PT", pt), "mL", "mU"], writes=[("PT", pt)])

        def emit_pv(g, hp=hp):
            blk, N, po, accb, ri, hh, W, pt = g["blk"], g["N"], g["po"], g["accb"], g["ri"], g["hh"], g["W"], g["pt"]
            for bi, (ksl, qsl, acols, vbt, pos) in enumerate(blk):
                first = g["first"] and bi == 0
                last = g["last"] and (bi == len(blk) - 1)
                P.op("pe", lambda e, accb=accb, acols=acols, vbt=vbt, hh=hh, pt=pt, pos=pos, N=N, first=first, last=last: e.matmul(
                    ps[:, accb, acols], lhsT=vbt[0][:, vbt[2], hh:hh + 2, :].rearrange("p a b -> p (a b)"),
                    rhs=PT[pt][:, pos * N:(pos + 1) * N], start=first, stop=last, skip_group_check=True),
                    reads=[("PT", pt), ("vb", vbt[1], vbt[2]), ("vbones", vbt[1])], writes=[("ps", accb)])
            if g["last"]:
                dlo = 64 - po
                P.op("act", lambda e, ri=ri, accb=accb, dlo=dlo: e.activation(out=rc[ri][dlo:dlo + 64, :], in_=ps[dlo:dlo + 64, accb, :], func=AF.Ln),
                     reads=[("ps", accb)], writes=[("rc", ri)])
                P.op("act", lambda e, ri=ri, dlo=dlo: e.activation(out=rc[ri][dlo:dlo + 64, :], in_=rc[ri][dlo:dlo + 64, :], func=AF.Exp, scale=-1.0),
                     reads=[("rc", ri)], writes=[("rc", ri)])
                P.op("dve", lambda e, ri=ri, accb=accb, dlo=dlo, po=po, hp=hp, W=W: e.tensor_tensor(
                    out=CC[po:po + 64, hp, 512 * W:512 * (W + 1)], in0=ps[po:po + 64, accb, :], in1=rc[ri][dlo:dlo + 64, :], op=ALU.mult),
                    reads=[("ps", accb), ("rc", ri)], writes=[("CCa", hp, W, hh)])

        LOOK = 3
        for q in range(min(LOOK, len(G))):
            emit_qk(G[q])
        for q in range(len(G)):
            emit_pv(G[q])
            if q + LOOK < len(G):
                emit_qk(G[q + LOOK])
    P.fence()

    g0 = 146 * KB
    wz = sb("wz", [128, 8, 512], BF16, g0)
    wu = [sb("wu%d" % c, [128, 8, 128], BF16, g0 + 8 * KB + 2 * KB * c) for c in range(4)]
    wsf = sb("wsf", [128, 8, 128], F32, g0 + 16 * KB)
    wsm = sb("wsm", [128, 8, 128], BF16, g0 + 20 * KB)
    bsb = sb("bsb", [128, 4, 128], F32, g0 + 22 * KB)
    lzg = sb("lzg", [128, 512], F32, g0 + 24 * KB)
    lzb = sb("lzb", [128, 512], F32, g0 + 26 * KB)
    ug = [sb("ug%d" % i, [128, 4, 512], F32, g0 + 28 * KB + 8 * KB * i) for i in range(2)]
    zg = [sb("zg%d" % i, [128, 512], F32, g0 + 54 * KB + 2 * KB * i) for i in range(4)]
    zn = [sb("zn%d" % i, [128, 512], BF16, g0 + 62 * KB + KB * i) for i in range(4)]
    mx = [sb("mx%d" % i, [128, 4, 128], F32, g0 + 50 * KB + 2 * KB * i) for i in range(2)]

    P.op("pool", lambda e: e.dma_start(out=wz[:], in_=w_in_d[:, 2048:2560].rearrange("(kc p) c -> p kc c", p=128)),
         writes=["wz"], dma=True, lane=("g", 0))
    for c in range(4):
        P.op("pool", lambda e, c=c: e.dma_start(out=wu[c][:], in_=w_in_d[:, 1536 + 128 * c:1536 + 128 * (c + 1)].rearrange("(kc p) c -> p kc c", p=128)),
             writes=[("wu", c)], dma=True, lane=("g", 1 + c))
    P.op("sp", lambda e: e.dma_start(out=wsf[:], in_=wsT_d.rearrange("g j i -> j g i")), writes=["wsf"], dma=True, lane=("g", 5))
    for gp in range(4):
        for hf in range(2):
            P.op("sp", lambda e, gp=gp, hf=hf: e.dma_start(out=bsb[64 * hf:64 * (hf + 1), gp, :], in_=bs_d[2 * gp + hf, :].partition_broadcast(64)),
                 writes=[("bsb", gp, hf)], dma=True, lane=("g", 6 + gp * 2 + hf))
    P.op("sp", lambda e: e.dma_start(out=lzg[:], in_=lzg_d[0, :].partition_broadcast(128)), writes=["lzg"], dma=True, lane=("g", 14))
    P.op("sp", lambda e: e.dma_start(out=lzb[:], in_=lzb_d[0, :].partition_broadcast(128)), writes=["lzb"], dma=True, lane=("g", 15))
    P.op("dve", lambda e: e.tensor_tensor(out=wsm[:], in0=wsf[:], in1=mL[:, :].unsqueeze(1).broadcast_to([128, 8, 128]), op=ALU.mult),
         reads=["wsf", "mL"], writes=["wsm"])

    def layernorm_stats(src_fn, nchunks, slot, key_in):
        for c in range(nchunks):
            P.op("dve", lambda e, c=c: e.bn_stats(out=lnst[:, slot, c, :], in_=src_fn(c)), reads=[key_in], writes=[("lnst", slot, c)])
        P.op("dve", lambda e: e.bn_aggr(out=lnmv[:, slot, 0:2], in_=lnst[:, slot, 0:nchunks, :]),
             reads=[("lnst", slot, c) for c in range(nchunks)], writes=[("mv", slot)])
        P.op("act", lambda e: e.activation(out=lnmv[:, slot, 2:3], in_=lnmv[:, slot, 1:2], func=AF.Sqrt, bias=epsb[:, 0:1]),
             reads=[("mv", slot), "epsb"], writes=[("sd", slot)])
        P.op("dve", lambda e: e.reciprocal(out=lnmv[:, slot, 2:3], in_=lnmv[:, slot, 2:3]), reads=[("sd", slot)], writes=[("sd", slot)])
        P.op("dve", lambda e: e.tensor_scalar(out=lnmv[:, slot, 3:4], in0=lnmv[:, slot, 0:1], scalar1=lnmv[:, slot, 2:3], scalar2=-1.0,
                                              op0=ALU.mult, op1=ALU.mult),
             reads=[("mv", slot), ("sd", slot)], writes=[("nmr", slot)])

    epsb = sb("epsb", [128, 1], F32, c0 + 1600)
    P.op("dve", lambda e: e.memset(epsb[:], EPS), writes=["epsb"])

    ln_ctr = [0]

    def g_uproj(T8):
        ub = T8 % 2
        for c in range(4):
            bk = c % 2
            for kc in range(8):
                P.op("pe", lambda e, bk=bk, c=c, kc=kc, T8=T8: e.matmul(ps[:, bk, :], lhsT=wu[c][:, kc, :], rhs=XT[:, kc, 512 * T8:512 * (T8 + 1)],
                                                                    start=(kc == 0), stop=(kc == 7)),
                     reads=[("wu", c)], writes=[("ps", bk)])
            P.op("act", lambda e, bk=bk, c=c, ub=ub: e.activation(out=ug[ub][:, c, :], in_=ps[:, bk, :], func=AF.Gelu),
                 reads=[("ps", bk)], writes=[("ug", ub, c)])

    def g_zproj(i):
        zb = i % 4
        bk = 2 + i % 2
        for kc in range(8):
            P.op("pe", lambda e, bk=bk, kc=kc, i=i: e.matmul(ps[:, bk, :], lhsT=XT[:, kc, 128 * i:128 * (i + 1)], rhs=wz[:, kc, :],
                                                             start=(kc == 0), stop=(kc == 7)),
                 reads=["wz"], writes=[("ps", bk)])
        P.op("act", lambda e, bk=bk, zb=zb: e.activation(out=zg[zb][:], in_=ps[:, bk, :], func=AF.Gelu),
             reads=[("ps", bk)], writes=[("zg", zb)])
        slot = ln_ctr[0] % 4
        ln_ctr[0] += 1
        layernorm_stats(lambda c, zb=zb: zg[zb][:], 1, slot, ("zg", zb))
        P.op("act", lambda e, zb=zb, slot=slot: e.activation(out=zg[zb][:], in_=zg[zb][:], func=AF.Identity,
                                                             scale=lnmv[:, slot, 2:3], bias=lnmv[:, slot, 3:4]),
             reads=[("zg", zb), ("sd", slot), ("nmr", slot)], writes=[("zg", zb)])
        P.op("pool", lambda e, zb=zb: e.tensor_tensor(out=zg[zb][:], in0=zg[zb][:], in1=lzg[:], op=ALU.mult),
             reads=[("zg", zb), "lzg"], writes=[("zg", zb)])
        P.op("pool", lambda e, zb=zb: e.tensor_tensor(out=zn[zb][:], in0=zg[zb][:], in1=lzb[:], op=ALU.add),
             reads=[("zg", zb), "lzb"], writes=[("zn", zb)])

    def g_spatial(i):
        zb = i % 4
        mb = i % 2
        T8, tt = i // 4, i % 4
        ub = T8 % 2
        sbk = 4 + i % 2
        for gp in range(4):
            for hf in range(2):
                g = 2 * gp + hf
                P.op("pe", lambda e, sbk=sbk, gp=gp, hf=hf, g=g, zb=zb: e.matmul(
                    ps[64 * hf:64 * (hf + 1), sbk, 128 * gp:128 * (gp + 1)], lhsT=zn[zb][:, 64 * g:64 * (g + 1)], rhs=wsm[:, g, :],
                    start=True, stop=True),
                    reads=[("zn", zb), "wsm"], writes=[("ps", sbk)])
        P.op("dve", lambda e, sbk=sbk, mb=mb: e.tensor_tensor(out=mx[mb][:], in0=ps[:, sbk, :].rearrange("p (a b) -> p a b", b=128),
                                                           in1=bsb[:], op=ALU.add),
             reads=[("ps", sbk)] + [("bsb", gp, hf) for gp in range(4) for hf in range(2)], writes=[("mx", mb)])
        P.op("pool", lambda e, mb=mb, ub=ub, tt=tt, i=i: e.tensor_tensor(out=CC[:, 4:8, 128 * i:128 * (i + 1)], in0=mx[mb][:],
                                                                      in1=ug[ub][:, :, 128 * tt:128 * (tt + 1)], op=ALU.mult),
             reads=[("mx", mb)] + [("ug", ub, c) for c in range(4)], writes=[("CCg", i)])

    GL = 2
    for i in range(32 + GL):
        if i < 32:
            if i % 4 == 0:
                g_uproj(i // 4)
            g_zproj(i)
        if i >= GL:
            g_spatial(i - GL)

    if stop_after == "A":
        P.fence()
        o = P.op("sp", lambda e: e.dma_start(out=out_d, in_=CC[:].rearrange("p a b -> p (a b)")), writes=["out"], dma=True, lane=("o", 0))
        P.final_waits = [o.idx]
        P.emit()
        return nc
    P.fence()

    t0 = 82 * KB
    lnp = {}
    for qi, nm in enumerate(("ln1_g", "ln1_b", "ln2_g", "ln2_b", "ln3_g", "ln3_b", "b_ple_gate")):
        lnp[nm] = sb("lnp_" + nm, [128, D], F32, t0 + 4 * KB * qi)
        P.op("sp", lambda e, nm=nm: e.dma_start(out=lnp[nm][:], in_=ln_d[nm][0, :].partition_broadcast(128)),
             writes=[("lnp", nm)], dma=True, lane=("lnp", qi))
    x1s = [sb("x1_%d" % i, [128, 4, D], F32, 110 * KB + 16 * KB * i) for i in range(2)]
    gT = sb("gT", [128, NF, 512], BF16, 142 * KB)
    pT = sb("pT", [128, 2, 512], BF16, 164 * KB)
    NR = 16
    ring = [sb("ring%d" % i, [128, D], BF16, 166 * KB + 2 * KB * i) for i in range(NR)]
    xr = [sb("xr0", [128, D], F32, 198 * KB), sb("xr1", [128, D], F32, 217 * KB)]
    pbt = [sb("pbt%d" % i, [128, DPLE], BF16, 202 * KB + 512 * i) for i in range(4)]
    abuf = [sb("abuf%d" % i, [128, 520], F32, 204 * KB + 2080 * i) for i in range(2)]
    cacc = [sb("cacc%d" % i, [128, 512], F32, 209 * KB + 2 * KB * i) for i in range(2)]
    gtmp = [sb("gtmp0", [128, D], F32, 213 * KB)]
    cw = sb("cw", [128, 3 * NF], F32, 221 * KB)
    cb = sb("cb", [128, NF], F32, 221 * KB + 288)
    halo = sb("halo", [128, NF, 2], F32, 221 * KB + 384)
    lnT = sb("lnT", [128, 32], F32, 221 * KB + 576)

    P.op("sp", lambda e: e.dma_start(out=cw[:], in_=cw_d), writes=["cw"], dma=True, lane=("t", 0))
    P.op("sp", lambda e: e.dma_start(out=cb[:], in_=cb_d), writes=["cb"], dma=True, lane=("t", 1))
    P.op("sp", lambda e: e.dma_start(out=lnT[:], in_=lnT_d), writes=["lnT"], dma=True, lane=("t", 2))
    P.op("dve", lambda e: e.memset(halo[:], 0.0), writes=["halo"])

    NCH = 8 * 84
    import os as _os
    PREF = int(_os.environ.get("MK_PREF", "6"))
    emitted = [0]
    usectr = [0]

    def ring_next():
        n = usectr[0]
        usectr[0] += 1
        while emitted[0] <= min(n + PREF, NCH - 1):
            m = emitted[0]
            emitted[0] += 1
            s_ = m % NR
            P.op("sp", lambda e, s_=s_, m=m: e.dma_start(out=ring[s_][:], in_=tape[m % 84]),
                 writes=[("ring", s_)], dma=True, lane=("ring", s_))
        return n % NR

    def ln_norm(x1, par, tt):
        slot = ln_ctr[0] % 4
        ln_ctr[0] += 1
        layernorm_stats(lambda c: x1[:, tt, 512 * c:512 * (c + 1)], 2, slot, ("x1", par, tt))
        P.op("act", lambda e: e.activation(out=x1[:, tt, :], in_=x1[:, tt, :], func=AF.Identity,
                                           scale=lnmv[:, slot, 2:3], bias=lnmv[:, slot, 3:4]),
             reads=[("x1", par, tt), ("sd", slot), ("nmr", slot)], writes=[("x1", par, tt)])

    def ln_affine(x1, par, tt, gname, bname):
        P.op("pool", lambda e: e.tensor_tensor(out=x1[:, tt, :], in0=x1[:, tt, :], in1=lnp[gname][:], op=ALU.mult),
             reads=[("x1", par, tt), ("lnp", gname)], writes=[("x1", par, tt)])
        P.op("pool", lambda e: e.tensor_tensor(out=x1[:, tt, :], in0=x1[:, tt, :], in1=lnp[bname][:], op=ALU.add),
             reads=[("x1", par, tt), ("lnp", bname)], writes=[("x1", par, tt)])

    evc = [0]

    def transpose_to_cc(x1, par, tt, i, b0, lnbase):
        for k in range(8):
            P.op("pe", lambda e, k=k: e.transpose(out=ps[:, b0 + k // 4, (k % 4) * 128:(k % 4 + 1) * 128],
                                                  in_=x1[:, tt, 128 * k:128 * (k + 1)], identity=identf[:]),
                 reads=[("x1", par, tt), "identf"], writes=[("ps", b0 + k // 4)])
        for k in range(8):
            src = ps[:, b0 + k // 4, (k % 4) * 128:(k % 4 + 1) * 128]
            dst = CC[:, k, 128 * i:128 * (i + 1)]
            gcol = lnT[:, lnbase + k:lnbase + k + 1]
            bcol = lnT[:, lnbase + 8 + k:lnbase + 8 + k + 1]
            if (evc[0] + k // 4) % 2 == 0:
                P.op("act", lambda e, src=src, dst=dst, gcol=gcol, bcol=bcol: e.activation(out=dst, in_=src, func=AF.Identity, scale=gcol, bias=bcol),
                     reads=[("ps", b0 + k // 4), "lnT"], writes=[("CT", i, k)])
            else:
                P.op("dve", lambda e, src=src, dst=dst, gcol=gcol, bcol=bcol: e.tensor_scalar(out=dst, in0=src, scalar1=gcol, scalar2=bcol,
                                                                                          op0=ALU.mult, op1=ALU.add),
                     reads=[("ps", b0 + k // 4), "lnT"], writes=[("CT", i, k)])
        evc[0] += 1

    pending_b = [None]
    for Tt in range(8):
        tok0 = 512 * Tt
        par = Tt % 2
        x1 = x1s[par]
        for tt in range(4):
            i = 4 * Tt + tt
            P.op("pool", lambda e, i=i, tt=tt: e.dma_start(out=pbt[tt][:], in_=p_d[128 * i:128 * (i + 1), :]),
                 writes=[("pbt", tt)], dma=True, lane=("pbt", tt))
        so = [ring_next() for k in range(8)]

        def stB_main(tt, Tt=Tt, so=so, x1=x1, par=par):
            i = 4 * Tt + tt
            xj = i % 2
            P.op("sp", lambda e, i=i, xj=xj: e.dma_start(out=xr[xj][:], in_=x_d[128 * i:128 * (i + 1), :]),
                 writes=[("xr", xj)], dma=True, lane=("xr", xj))
            b0 = 2 * (tt % 2)
            for hf in range(2):
                for k in range(8):
                    P.op("pe", lambda e, hf=hf, k=k, i=i, b0=b0, sl_=so[k]: e.matmul(ps[:, b0 + hf, :], lhsT=CC[:, k, 128 * i:128 * (i + 1)],
                                                                      rhs=ring[sl_][:, 512 * hf:512 * (hf + 1)],
                                                                      start=(k == 0), stop=(k == 7)),
                         reads=[("CT", i, k), ("ring", so[k])], writes=[("ps", b0 + hf)])
            P.op("dve", lambda e, tt=tt, b0=b0, xj=xj: e.scalar_tensor_tensor(
                out=x1[:, tt, :].rearrange("p (a b) -> p a b", a=2), in0=xr[xj][:].rearrange("p (a b) -> p a b", a=2), scalar=ALPHA,
                in1=ps[:, b0:b0 + 2, :], op0=ALU.mult, op1=ALU.add),
                reads=[("xr", xj), ("ps", b0), ("ps", b0 + 1)], writes=[("x1", par, tt)])
            ln_norm(x1, par, tt)

        def stB_post(tt, Tt=Tt, x1=x1, par=par):
            transpose_to_cc(x1, par, tt, 4 * Tt + tt, 4 + 2 * (tt % 2), 0)
            ln_affine(x1, par, tt, "ln1_g", "ln1_b")

        pend = pending_b[0] if pending_b[0] else [lambda: None] * 4
        stB_main(0)
        pend[0]()
        stB_main(1)
        pend[1]()
        stB_post(0)
        stB_main(2)
        pend[2]()
        stB_post(1)
        stB_main(3)
        pend[3]()
        stB_post(2)
        stB_post(3)
        for f in range(NF):
            sa = ring_next()
            sbb = ring_next()
            ba = f % 2
            bb = 2 + f % 2
            aj = f % 2
            for (bank, slot) in ((ba, sa), (bb, sbb)):
                for kc in range(8):
                    P.op("pe", lambda e, bank=bank, slot=slot, kc=kc, tok0=tok0: e.matmul(
                        ps[:, bank, :], lhsT=ring[slot][:, 128 * kc:128 * (kc + 1)], rhs=CC[:, kc, tok0:tok0 + 512],
                        start=(kc == 0), stop=(kc == 7)),
                        reads=[("ring", slot)] + [("CT", 4 * Tt + q, kc) for q in range(4)], writes=[("ps", bank)])
            P.op("dve", lambda e, aj=aj, f=f: e.tensor_copy(out=abuf[aj][:, 0:2], in_=halo[:, f, :]),
                 reads=[("halo", f), "halo"], writes=[("abh", aj)])
            P.op("act", lambda e, aj=aj, ba=ba: e.copy(out=abuf[aj][:, 2:514], in_=ps[:, ba, :]),
                 reads=[("ps", ba)], writes=[("ab", aj)])
            P.op("dve", lambda e, aj=aj, f=f: e.tensor_copy(out=halo[:, f, :], in_=abuf[aj][:, 512:514]),
                 reads=[("ab", aj)], writes=[("halo", f)])
            P.op("act", lambda e, aj=aj, ba=ba, f=f: e.activation(out=cacc[aj][:], in_=ps[:, ba, :], func=AF.Identity,
                                                                  scale=cw[:, 2 * NF + f:2 * NF + f + 1], bias=cb[:, f:f + 1]),
                 reads=[("ps", ba), "cw", "cb"], writes=[("cacc", aj)])
            P.op("dve", lambda e, aj=aj, f=f: e.scalar_tensor_tensor(out=cacc[aj][:], in0=abuf[aj][:, 1:513], scalar=cw[:, NF + f:NF + f + 1],
                                                                    in1=cacc[aj][:], op0=ALU.mult, op1=ALU.add),
                 reads=[("ab", aj), ("abh", aj), ("cacc", aj), "cw"], writes=[("cacc", aj)])
            P.op("dve", lambda e, aj=aj, f=f: e.scalar_tensor_tensor(out=cacc[aj][:], in0=abuf[aj][:, 0:512], scalar=cw[:, f:f + 1],
                                                                    in1=cacc[aj][:], op0=ALU.mult, op1=ALU.add),
                 reads=[("ab", aj), ("abh", aj), ("cacc", aj), "cw"], writes=[("cacc", aj)])
            P.op("act", lambda e, aj=aj: e.activation(out=cacc[aj][:], in_=cacc[aj][:], func=AF.Gelu),
                 reads=[("cacc", aj)], writes=[("cacc", aj)])
            P.op("dve", lambda e, aj=aj, bb=bb, f=f: e.tensor_tensor(out=gT[:, f, :], in0=cacc[aj][:], in1=ps[:, bb, :], op=ALU.mult),
                 reads=[("cacc", aj), ("ps", bb)], writes=[("gT", f)])

        for tt in range(4):
            for kc in range(2):
                P.op("pe", lambda e, tt=tt, kc=kc: e.transpose(out=psb[:, 4, (tt * 2 + kc) * 128:(tt * 2 + kc + 1) * 128],
                                                               in_=pbt[tt][:, 128 * kc:128 * (kc + 1)], identity=identb[:]),
                     reads=[("pbt", tt), "identb"], writes=[("ps", 4)])
        P.op("dve", lambda e: e.tensor_copy(out=pT[:].rearrange("p k (t c) -> p k t c", c=128),
                                            in_=psb[:, 4, :].rearrange("p (t k c) -> p k t c", t=4, k=2)),
             reads=[("ps", 4)], writes=["pT"])

        for f in range(NF):
            sd = ring_next()
            for tt in range(4):
                for hf in range(2):
                    P.op("pe", lambda e, f=f, tt=tt, hf=hf, sd=sd: e.matmul(ps[:, 2 * tt + hf, :], lhsT=gT[:, f, 128 * tt:128 * (tt + 1)],
                                                                        rhs=ring[sd][:, 512 * hf:512 * (hf + 1)],
                                                                        start=(f == 0), stop=(f == NF - 1)),
                         reads=[("gT", f), ("ring", sd)], writes=[("ps", 2 * tt + hf)])
        sg = [ring_next() for k in range(8)]
        sp_ = [ring_next() for k in range(2)]

        def ln2_main(tt, Tt=Tt, x1=x1, par=par):
            b0 = 2 * tt
            P.op("dve", lambda e, tt=tt, b0=b0: e.scalar_tensor_tensor(
                out=x1[:, tt, :].rearrange("p (a b) -> p a b", a=2), in0=x1[:, tt, :].rearrange("p (a b) -> p a b", a=2), scalar=ALPHA,
                in1=ps[:, b0:b0 + 2, :], op0=ALU.mult, op1=ALU.add),
                reads=[("x1", par, tt), ("ps", b0), ("ps", b0 + 1)], writes=[("x1", par, tt)])
            ln_norm(x1, par, tt)

        def ln2_post(tt, Tt=Tt, x1=x1, par=par):
            transpose_to_cc(x1, par, tt, 4 * Tt + tt, 2 * tt, 16)
            ln_affine(x1, par, tt, "ln2_g", "ln2_b")

        def gate_a(tt, Tt=Tt, sg=sg, sp_=sp_, x1=x1, par=par):
            i = 4 * Tt + tt
            st_ = 0
            bg_, bp_ = 4 * (tt % 2), 4 * (tt % 2) + 2
            for hf in range(2):
                for k in range(8):
                    P.op("pe", lambda e, hf=hf, k=k, i=i, bg_=bg_, sl_=sg[k]: e.matmul(ps[:, bg_ + hf, :], lhsT=CC[:, k, 128 * i:128 * (i + 1)],
                                                                       rhs=ring[sl_][:, 512 * hf:512 * (hf + 1)],
                                                                       start=(k == 0), stop=(k == 7)),
                         reads=[("CT", i, k), ("ring", sg[k])], writes=[("ps", bg_ + hf)])
            for hf in range(2):
                for k in range(2):
                    P.op("pe", lambda e, hf=hf, k=k, tt=tt, bp_=bp_, sl_=sp_[k]: e.matmul(ps[:, bp_ + hf, :], lhsT=pT[:, k, 128 * tt:128 * (tt + 1)],
                                                                        rhs=ring[sl_][:, 512 * hf:512 * (hf + 1)],
                                                                        start=(k == 0), stop=(k == 1)),
                         reads=["pT", ("ring", sp_[k])], writes=[("ps", bp_ + hf)])
            gt = gtmp[st_]
            P.op("dve", lambda e, gt=gt, bg_=bg_: e.tensor_tensor(out=gt[:].rearrange("p (a b) -> p a b", a=2), in0=ps[:, bg_:bg_ + 2, :],
                                                              in1=lnp["b_ple_gate"][:].rearrange("p (a b) -> p a b", a=2), op=ALU.add),
                 reads=[("ps", bg_), ("ps", bg_ + 1), ("lnp", "b_ple_gate")], writes=[("gtmp", st_)])
            P.op("act", lambda e, gt=gt: e.activation(out=gt[:], in_=gt[:], func=AF.Sigmoid), reads=[("gtmp", st_)], writes=[("gtmp", st_)])
            P.op("dve", lambda e, gt=gt, bp_=bp_: e.tensor_tensor(out=gt[:].rearrange("p (a b) -> p a b", a=2),
                                                              in0=gt[:].rearrange("p (a b) -> p a b", a=2), in1=ps[:, bp_:bp_ + 2, :], op=ALU.mult),
                 reads=[("gtmp", st_), ("ps", bp_), ("ps", bp_ + 1)], writes=[("gtmp", st_)])
            P.op("dve", lambda e, gt=gt, tt=tt: e.scalar_tensor_tensor(out=x1[:, tt, :], in0=x1[:, tt, :], scalar=ALPHA, in1=gt[:],
                                                                    op0=ALU.mult, op1=ALU.add),
                 reads=[("x1", par, tt), ("gtmp", st_)], writes=[("x1", par, tt)])

        def gate_b(tt, Tt=Tt, x1=x1, par=par):
            i = 4 * Tt + tt
            ln_norm(x1, par, tt)
            ln_affine(x1, par, tt, "ln3_g", "ln3_b")
            o = P.op("pool", lambda e, i=i, tt=tt: e.dma_start(out=out_d[128 * i:128 * (i + 1), :], in_=x1[:, tt, :]),
                     reads=[("x1", par, tt)], writes=[("out", i)], dma=True, lane=("o", par, tt))
            P.final_waits.append(o.idx)

        ln2_main(0)
        ln2_main(1)
        ln2_post(0)
        ln2_main(2)
        ln2_post(1)
        ln2_main(3)
        ln2_post(2)
        ln2_post(3)
        for tt in range(4):
            gate_a(tt)
        pending_b[0] = [(lambda tt=tt, gb=gate_b: gb(tt)) for tt in range(4)]
    for fn in pending_b[0]:
        fn()
    P.emit()
    return nc


def _consts():
    j = np.arange(128)[:, None]
    i = np.arange(128)[None, :]
    maskL = (j <= i).astype(np.float32)
    maskU = (j >= i).astype(np.float32)
    ident = np.eye(128, dtype=np.float32)
    inv8 = np.float32(500000.0) ** (-(np.arange(0, 16, 2, dtype=np.float32)) / np.float32(16.0))
    ropec = np.zeros((128, 2), np.float32)
    for p_ in range(128):
        d = p_ % 64
        if d < 16:
            ropec[p_, 0] = inv8[d % 8]
            ropec[p_, 1] = -1.0 if d < 8 else 1.0
    return ident, maskL, maskU, ropec


def _prep_shared(inp):
    f = lambda a: np.ascontiguousarray(np.asarray(a, dtype=np.float32))
    w_in = f(inp["w_in"][0])
    perm = np.arange(1024)
    for h in range(16):
        base = h * 64
        perm[base:base + 8] = np.arange(base + 8, base + 16)
        perm[base + 8:base + 16] = np.arange(base, base + 8)
    w_sw = np.ascontiguousarray(w_in[:, :1024][:, perm])
    conv_w = f(inp["conv_w"][0])
    conv_b = f(inp["conv_b"][0])
    cw = np.ascontiguousarray(conv_w.reshape(3, NF, 128).transpose(2, 0, 1).reshape(128, 3 * NF))
    cb = np.ascontiguousarray(conv_b.reshape(NF, 128).T)
    ident, maskL, maskU, ropec = _consts()
    sh = {
        "w_in": w_in, "w_sw": w_sw,
        "ln_z_g": f(inp["ln_z_g"]), "ln_z_b": f(inp["ln_z_b"]),
        "w_sT": np.ascontiguousarray(f(inp["w_s"][0]).transpose(0, 2, 1)),
        "b_s": f(inp["b_s"][0]),
        "w_o": f(inp["w_o"][0]),
        "w_ff_a": f(inp["w_ff_a"][0]), "w_ff_b": f(inp["w_ff_b"][0]),
        "cw": cw, "cb": cb,
        "w_ff_down": f(inp["w_ff_down"][0]),
        "w_ple_gate": f(inp["w_ple_gate"][0]),
        "w_ple_in": f(inp["w_ple_in"][0]),
        "ident": ident, "maskL": maskL, "maskU": maskU, "ropec": ropec,
    }
    for nm in ("ln1_g", "ln1_b", "ln2_g", "ln2_b", "ln3_g", "ln3_b", "b_ple_gate"):
        sh[nm] = f(inp[nm])
    lnT = np.zeros((128, 32), np.float32)
    for j, (gn, bn) in enumerate((("ln1_g", "ln1_b"), ("ln2_g", "ln2_b"))):
        lnT[:, 16 * j:16 * j + 8] = sh[gn].reshape(8, 128).T
        lnT[:, 16 * j + 8:16 * j + 16] = sh[bn].reshape(8, 128).T
    sh["lnT"] = np.ascontiguousarray(lnT)
    return sh


_NC_CACHE = {}


def kernel(**inputs):
    sh = _prep_shared(inputs)
    x = np.asarray(inputs["x"], dtype=np.float32)
    p = np.asarray(inputs["p"], dtype=np.float32)
    pos = np.asarray(inputs["positions"], dtype=np.int32)
    in_maps = []
    for b in range(8):
        m = dict(sh)
        m["x"] = np.ascontiguousarray(x[b])
        m["p"] = np.ascontiguousarray(p[0, b])
        m["pos"] = np.ascontiguousarray(pos[b:b + 1])
        in_maps.append(m)
    nc = build()
    res = run_bass_kernel_spmd(nc, in_maps, core_ids=list(range(8)))
    out = np.stack([np.asarray(r["out"], dtype=np.float32) for r in res.results], axis=0)
    return out
```
